# Optimizing a Trainium2 kernel written in Bass

```python
import math
import jax, jax.numpy as jnp
from jax import lax
import numpy as np

D_MODEL = 1024
BATCH = 8
SEQ = 2048
DEPTH = 1
DEC_BATCH = 8
DEC_SEQ = 8192
PAST_LEN = 128

N_MEM = 256
EPS = 1e-6
H_M = 4
DK_M = 128
DV_M = 256
CHUNK = 64
W_M = H_M * DV_M
H_D = 8
DK_D = 64
DV_D = 128
W_D = H_D * DV_D
ROT_DIM = DK_D // 4
ROPE_THETA = 500000.0
Q_BLOCK = 128
H_X = 4
DH_X = D_MODEL // H_X
D_FF = 2816

SPLIT_SIZES = (H_M * DK_M, H_M * DK_M, W_M, W_M, 2 * H_M, 2 * H_M,
               H_D * 2 * DK_D, H_D * 2 * DK_D, W_D, D_MODEL, D_MODEL)
N_MIX_COLS = sum(SPLIT_SIZES)
SPLIT_OFFSETS = tuple(int(o) for o in np.cumsum(SPLIT_SIZES)[:-1])

kernel_name = "hybrid_mlstm_diffattn_encoder"


def rms_norm(x, g):
    xf = x.astype(jnp.float32)
    y = xf * lax.rsqrt(jnp.mean(xf * xf, axis=-1, keepdims=True) + EPS)
    return (y * g.astype(jnp.float32)).astype(x.dtype)


def swiglu_ffn(x, w_in, w_out):
    g, u = jnp.split(x @ w_in, 2, axis=-1)
    return (jax.nn.silu(g) * u) @ w_out


def partial_rope(x, pos):
    inv_freq = ROPE_THETA ** (-jnp.arange(0, ROT_DIM, 2, dtype=jnp.float32) / ROT_DIM)
    ang = pos[:, None] * inv_freq[None, :]
    cos = jnp.concatenate([jnp.cos(ang)] * 2, axis=-1)[None, :, None, None, :]
    sin = jnp.concatenate([jnp.sin(ang)] * 2, axis=-1)[None, :, None, None, :]
    xr = x[..., :ROT_DIM].astype(jnp.float32)
    x1, x2 = jnp.split(xr, 2, axis=-1)
    xr = xr * cos + jnp.concatenate([-x2, x1], axis=-1) * sin
    return jnp.concatenate([xr.astype(x.dtype), x[..., ROT_DIM:]], axis=-1)


def mlstm_chunkwise(q, k, v, ig, lf):
    B, H, S, DK = q.shape
    DV = v.shape[-1]
    nc = S // CHUNK

    def to_chunks(t):
        return jnp.moveaxis(t.reshape(B, H, nc, CHUNK, *t.shape[3:]), 2, 0)

    qc, kc, vc, igc, lfc = (to_chunks(t) for t in (q, k, v, ig, lf))
    lower = jnp.tril(jnp.ones((CHUNK, CHUNK), dtype=bool))

    def step(carry, inp):
        C, n, m = carry
        qj, kj, vj, igj, lfj = inp
        b = jnp.cumsum(lfj, axis=-1)
        dmat = jnp.where(lower, b[..., :, None] - b[..., None, :] + igj[..., None, :], -jnp.inf)
        inter = b + m[..., None]
        m_row = jnp.maximum(inter, jnp.max(dmat, axis=-1))
        w = jnp.exp(dmat - m_row[..., None]) * jnp.einsum('bhid,bhjd->bhij', qj, kj)
        s_inter = jnp.exp(inter - m_row)
        num = s_inter[..., None] * jnp.einsum('bhvd,bhid->bhiv', C, qj) + jnp.einsum('bhij,bhjv->bhiv', w, vj)
        den = s_inter * jnp.einsum('bhd,bhid->bhi', n, qj) + jnp.sum(w, axis=-1)
        h = num / jnp.maximum(jnp.abs(den), jnp.exp(-m_row))[..., None]
        b_last = b[..., -1]
        a = b_last[..., None] - b + igj
        m_new = jnp.maximum(b_last + m, jnp.max(a, axis=-1))
        decay = jnp.exp(b_last + m - m_new)
        wa = jnp.exp(a - m_new[..., None])
        C_new = decay[..., None, None] * C + jnp.einsum('bhj,bhjv,bhjd->bhvd', wa, vj, kj)
        n_new = decay[..., None] * n + jnp.einsum('bhj,bhjd->bhd', wa, kj)
        return (C_new, n_new, m_new), h

    init = (jnp.zeros((B, H, DV, DK), jnp.float32), jnp.zeros((B, H, DK), jnp.float32),
            jnp.zeros((B, H), jnp.float32))
    _, hs = lax.scan(step, init, (qc, kc, vc, igc, lfc))
    return jnp.moveaxis(hs, 0, 2).reshape(B, H, S, DV)


def mlstm_bidirectional(q, k, v, ig, lf):
    h_f = mlstm_chunkwise(q, k, v, ig[..., 0], lf[..., 0])
    flip = lambda t: jnp.flip(t, axis=2)
    h_b = flip(mlstm_chunkwise(flip(q), flip(k), flip(v), flip(ig[..., 1]), flip(lf[..., 1])))
    return h_f + h_b


def diff_attention(q, k, v, lam):
    B, H, _, S, DK = q.shape
    nqb = S // Q_BLOCK
    qb = jnp.moveaxis(q.reshape(B, H, 2, nqb, Q_BLOCK, DK), 3, 0)
    scale = DK ** -0.5

    def block(qi):
        s = jnp.einsum('bhcqd,bhckd->bhcqk', qi, k).astype(jnp.float32) * scale
        p = jax.nn.softmax(s, axis=-1)
        a = p[:, :, 0] - lam * p[:, :, 1]
        return jnp.einsum('bhqk,bhkv->bhqv', a.astype(v.dtype), v)

    o = lax.map(block, qb)
    return jnp.moveaxis(o, 0, 2).reshape(B, H, S, v.shape[-1])


def token_mixing(u, w_mix_in, b_igate, b_fgate, mlstm_norm, w_branch_a, lambda_q1, lambda_k1,
                 lambda_q2, lambda_k2, diff_norm, w_branch_b, w_mix_out, lambda_init):
    B, S, _ = u.shape
    z = u @ w_mix_in
    (q_m, k_m, v_m, o_m, ig, fg, q_d, k_d, v_d, g_a, g_b) = jnp.split(z, list(SPLIT_OFFSETS), axis=-1)

    def heads(t, h):
        return t.reshape(B, S, h, -1).transpose(0, 2, 1, 3)

    qm = heads(q_m, H_M).astype(jnp.float32)
    km = heads(k_m, H_M).astype(jnp.float32) * (DK_M ** -0.5)
    vm = heads(v_m, H_M).astype(jnp.float32)
    igp = (ig.reshape(B, S, 2, H_M).astype(jnp.float32) + b_igate.astype(jnp.float32)).transpose(0, 3, 1, 2)
    lfp = jax.nn.log_sigmoid(fg.reshape(B, S, 2, H_M).astype(jnp.float32)
                             + b_fgate.astype(jnp.float32)).transpose(0, 3, 1, 2)
    h_m = mlstm_bidirectional(qm, km, vm, igp, lfp).transpose(0, 2, 1, 3).astype(u.dtype)
    h_m = rms_norm(h_m, mlstm_norm.reshape(H_M, DV_M)).reshape(B, S, W_M) * jax.nn.sigmoid(o_m)
    y_a = h_m @ w_branch_a

    pos = jnp.arange(S, dtype=jnp.float32)
    qd = partial_rope(q_d.reshape(B, S, H_D, 2, DK_D), pos).transpose(0, 2, 3, 1, 4)
    kd = partial_rope(k_d.reshape(B, S, H_D, 2, DK_D), pos).transpose(0, 2, 3, 1, 4)
    vd = heads(v_d, H_D)
    lam = (jnp.exp(jnp.sum(lambda_q1.astype(jnp.float32) * lambda_k1.astype(jnp.float32)))
           - jnp.exp(jnp.sum(lambda_q2.astype(jnp.float32) * lambda_k2.astype(jnp.float32))) + lambda_init)
    o_d = diff_attention(qd, kd, vd, lam).transpose(0, 2, 1, 3)
    o_d = rms_norm(o_d, diff_norm.reshape(H_D, DV_D)) * (1.0 - lambda_init)
    y_b = o_d.reshape(B, S, W_D) @ w_branch_b

    merged = jax.nn.sigmoid(g_a) * y_a + jax.nn.sigmoid(g_b) * y_b
    return merged @ w_mix_out


def memory_cross_attention(u, mem_n, w_xq, w_xkv, w_xo):
    B, S, _ = u.shape
    M = mem_n.shape[1]
    q = (u @ w_xq).reshape(B, S, H_X, DH_X)
    k, v = jnp.split(mem_n @ w_xkv, 2, axis=-1)
    k = k.reshape(B, M, H_X, DH_X)
    v = v.reshape(B, M, H_X, DH_X)
    s = jnp.einsum('bshd,bmhd->bhsm', q, k).astype(jnp.float32) * (DH_X ** -0.5)
    p = jax.nn.softmax(s, axis=-1)
    o = jnp.einsum('bhsm,bmhd->bshd', p.astype(v.dtype), v).reshape(B, S, D_MODEL)
    return o @ w_xo


def encoder_layer(x, mem, layer_idx, ffn1_norm, ffn1_w_in, ffn1_w_out, mix_norm, w_mix_in, b_igate, b_fgate,
                  mlstm_norm, w_branch_a, lambda_q1, lambda_k1, lambda_q2, lambda_k2, diff_norm, w_branch_b,
                  w_mix_out, xattn_norm, mem_norm, w_xq, w_xkv, w_xo, ffn2_norm, ffn2_w_in, ffn2_w_out):
    x = x + 0.5 * swiglu_ffn(rms_norm(x, ffn1_norm), ffn1_w_in, ffn1_w_out)
    lambda_init = 0.8 - 0.6 * math.exp(-0.3 * layer_idx)
    x = x + token_mixing(rms_norm(x, mix_norm), w_mix_in, b_igate, b_fgate, mlstm_norm, w_branch_a,
                         lambda_q1, lambda_k1, lambda_q2, lambda_k2, diff_norm, w_branch_b, w_mix_out, lambda_init)
    x = x + memory_cross_attention(rms_norm(x, xattn_norm), rms_norm(mem, mem_norm), w_xq, w_xkv, w_xo)
    x = x + 0.5 * swiglu_ffn(rms_norm(x, ffn2_norm), ffn2_w_in, ffn2_w_out)
    return x


def trunk(x, mem, layer_weights, final_norm):
    for l in range(DEPTH):
        x = encoder_layer(x, mem, l, *[w[l] for w in layer_weights])
    return rms_norm(x, final_norm)


def setup_inputs(seed: int = 0) -> dict:
    key = jax.random.key(seed)
    ks = jax.random.split(key, 32)
    f32 = jnp.float32
    L = DEPTH

    def normal(k, shape, scale):
        return jax.random.normal(k, shape, f32) * scale

    def gain(k, shape):
        return 1.0 + 0.05 * jax.random.normal(k, shape, f32)

    return {
        "x_prompt": normal(ks[0], (BATCH, SEQ, D_MODEL), 1.0),
        "x_sample": normal(ks[1], (DEC_BATCH, DEC_SEQ, D_MODEL), 1.0),
        "mem_prompt": normal(ks[2], (BATCH, N_MEM, D_MODEL), 1.0),
        "mem_sample": normal(ks[3], (DEC_BATCH, N_MEM, D_MODEL), 1.0),
        "ffn1_norm": gain(ks[4], (L, D_MODEL)),
        "ffn1_w_in": normal(ks[5], (L, D_MODEL, 2 * D_FF), D_MODEL ** -0.5),
        "ffn1_w_out": normal(ks[6], (L, D_FF, D_MODEL), D_FF ** -0.5),
        "mix_norm": gain(ks[7], (L, D_MODEL)),
        "w_mix_in": normal(ks[8], (L, D_MODEL, N_MIX_COLS), D_MODEL ** -0.5),
        "b_igate": normal(ks[9], (L, 2, H_M), 0.1),
        "b_fgate": jnp.linspace(3.0, 6.0, H_M, dtype=f32) + normal(ks[10], (L, 2, H_M), 0.1),
        "mlstm_norm": gain(ks[11], (L, W_M)),
        "w_branch_a": normal(ks[12], (L, W_M, D_MODEL), W_M ** -0.5),
        "lambda_q1": normal(ks[13], (L, DK_D), 0.1),
        "lambda_k1": normal(ks[14], (L, DK_D), 0.1),
        "lambda_q2": normal(ks[15], (L, DK_D), 0.1),
        "lambda_k2": normal(ks[16], (L, DK_D), 0.1),
        "diff_norm": gain(ks[17], (L, W_D)),
        "w_branch_b": normal(ks[18], (L, W_D, D_MODEL), W_D ** -0.5),
        "w_mix_out": normal(ks[19], (L, D_MODEL, D_MODEL), D_MODEL ** -0.5),
        "xattn_norm": gain(ks[20], (L, D_MODEL)),
        "mem_norm": gain(ks[21], (L, D_MODEL)),
        "w_xq": normal(ks[22], (L, D_MODEL, D_MODEL), D_MODEL ** -0.5),
        "w_xkv": normal(ks[23], (L, D_MODEL, 2 * D_MODEL), D_MODEL ** -0.5),
        "w_xo": normal(ks[24], (L, D_MODEL, D_MODEL), D_MODEL ** -0.5),
        "ffn2_norm": gain(ks[25], (L, D_MODEL)),
        "ffn2_w_in": normal(ks[26], (L, D_MODEL, 2 * D_FF), D_MODEL ** -0.5),
        "ffn2_w_out": normal(ks[27], (L, D_FF, D_MODEL), D_FF ** -0.5),
        "final_norm": gain(ks[28], (D_MODEL,)),
    }


def reference(x_prompt, x_sample, mem_prompt, mem_sample, ffn1_norm, ffn1_w_in, ffn1_w_out, mix_norm,
              w_mix_in, b_igate, b_fgate, mlstm_norm, w_branch_a, lambda_q1, lambda_k1, lambda_q2, lambda_k2,
              diff_norm, w_branch_b, w_mix_out, xattn_norm, mem_norm, w_xq, w_xkv, w_xo, ffn2_norm,
              ffn2_w_in, ffn2_w_out, final_norm):
    layer_weights = (ffn1_norm, ffn1_w_in, ffn1_w_out, mix_norm, w_mix_in, b_igate, b_fgate, mlstm_norm,
                     w_branch_a, lambda_q1, lambda_k1, lambda_q2, lambda_k2, diff_norm, w_branch_b, w_mix_out,
                     xattn_norm, mem_norm, w_xq, w_xkv, w_xo, ffn2_norm, ffn2_w_in, ffn2_w_out)
    y_prompt = trunk(x_prompt, mem_prompt, layer_weights, final_norm)
    y_sample = trunk(x_sample, mem_sample, layer_weights, final_norm)
    return (y_prompt, y_sample)
```

```python
import contextlib
import math
import numpy as np
import concourse.bass as bass
import concourse.mybir as mybir
from concourse.bass_utils import run_bass_kernel_spmd

F32 = mybir.dt.float32
BF16 = mybir.dt.bfloat16
I32 = mybir.dt.int32
ALU = mybir.AluOpType
AF = mybir.ActivationFunctionType
AX = mybir.AxisListType

D = 1024
DFF = 2816
NMEM = 256
EPS = 1e-6
LAMBDA_INIT = 0.8 - 0.6 * math.exp(-0.3 * 0)
T = 512
NWS = 6
O_QM, O_KM, O_VM, O_OM, O_IG, O_FG, O_QD, O_KD, O_VD, O_GA, O_GB = (
    0, 512, 1024, 2048, 3072, 3080, 3088, 4112, 5136, 6160, 7184)
NMIX = 8208
C_QM, C_KM, C_QD, C_KD, C_GA, C_GB, C_OM = 0, 4, 8, 16, 24, 32, 40
NFC = 48
NF = NFC * 128 + 16
NT_ = 2560
TWO_PI = 2.0 * math.pi
CW1 = 6.28125
CW2 = TWO_PI - CW1


class Buf:
    __slots__ = ("name", "w", "r")

    def __init__(self, name):
        self.name = name
        self.w = {}
        self.r = {}


class Ctx:
    def __init__(self, nc, es):
        self.nc = nc
        self.E = {"pe": nc.tensor, "act": nc.scalar, "dve": nc.vector, "pool": nc.gpsimd, "sp": nc.sync}
        self.semh = {}
        self.cnt = {}
        self.known = {}
        for e in self.E:
            self.semh[e] = es.enter_context(nc.semaphore("s_" + e))
            self.cnt[e] = 0
            self.known[e] = {}
        self.dq = {}
        for q, n in (("sp", 16), ("pool", 16), ("act", 6)):
            keys = []
            for i in range(n):
                k = (q, i)
                self.semh[k] = es.enter_context(nc.semaphore("d_%s%d" % (q, i)))
                self.cnt[k] = 0
                keys.append(k)
            self.dq[q] = [keys, 0]

    def _waits(self, e, need):
        eng = self.E[e]
        kn = self.known[e]
        for k, v in need.items():
            if k == e:
                if e == "pe":
                    continue
                v = min(v, self.cnt[e])
            if kn.get(k, 0) < v:
                eng.wait_ge(self.semh[k], v)
                kn[k] = v

    @staticmethod
    def _need(r, w):
        need = {}
        for b in r:
            for k, v in b.w.items():
                if need.get(k, 0) < v:
                    need[k] = v
        for b in w:
            for k, v in b.w.items():
                if need.get(k, 0) < v:
                    need[k] = v
            for k, v in b.r.items():
                if need.get(k, 0) < v:
                    need[k] = v
        return need

    def op(self, e, fn, r=(), w=(), inc=True):
        self._waits(e, self._need(r, w))
        ins = fn(self.E[e])
        if inc:
            self.cnt[e] += 1
            ins.then_inc(self.semh[e], 1)
            stamp = self.cnt[e]
        else:
            stamp = self.cnt[e] + 1
        for b in w:
            b.w = {e: stamp}
            b.r = {}
        for b in r:
            if b.r.get(e, 0) < stamp:
                b.r[e] = stamp
        return ins

    def dma(self, q, out, in_, r=(), w=(), merge=False, **kw):
        need = self._need(r, w)
        keys, i = self.dq[q]
        k = keys[i % len(keys)]
        self.dq[q][1] = i + 1
        if need.get(k, 0) < self.cnt[k]:
            need[k] = self.cnt[k]
        self._waits(q, need)
        ins = self.E[q].dma_start(out=out, in_=in_, **kw)
        ins.then_inc(self.semh[k], 16)
        self.cnt[k] += 16
        v = self.cnt[k]
        for b in w:
            if merge:
                b.w[k] = v
            else:
                b.w = {k: v}
                b.r = {}
        for b in r:
            if b.r.get(k, 0) < v:
                b.r[k] = v
        return ins

    def barrier(self, engines=None):
        for e in (engines or self.E):
            need = {k: v for k, v in self.cnt.items() if k != e and v > 0}
            self._waits(e, need)


def build(S_LIST=(2048, 8192), debug=False, stop_after=99):
    nc = bass.Bass("TRN2", target_bir_lowering=False)
    NSEQ = len(S_LIST)
    dbg_kind = "ExternalOutput" if debug else "Internal"

    def din(name, shape, dt=F32):
        return nc.dram_tensor(name, list(shape), dt, kind="ExternalInput").ap()

    def dscr(name, shape, dt=BF16, dbg=False):
        return nc.dram_tensor(name, list(shape), dt, kind=(dbg_kind if dbg else "Internal")).ap()

    x_in = [din("x%d" % s, [S_LIST[s], D]) for s in range(NSEQ)]
    mem_in = [din("mem%d" % s, [NMEM, D]) for s in range(NSEQ)]
    y_out = [nc.dram_tensor("y%d" % s, [S_LIST[s], D], F32, kind="ExternalOutput").ap() for s in range(NSEQ)]
    w_ffn1_in = din("ffn1_w_in", [D, 2 * DFF])
    w_ffn1_out = din("ffn1_w_out", [DFF, D])
    w_mix_in = din("w_mix_in", [D, NMIX])
    w_br_a = din("w_branch_a", [D, D])
    w_br_b = din("w_branch_b", [D, D])
    w_mix_out = din("w_mix_out", [D, D])
    w_xq = din("w_xq", [D, D])
    w_xkv = din("w_xkv", [D, 2 * D])
    w_xo = din("w_xo", [D, D])
    w_ffn2_in = din("ffn2_w_in", [D, 2 * DFF])
    w_ffn2_out = din("ffn2_w_out", [DFF, D])
    gcols_in = din("gcols", [128, 7 * 8])
    final_gain = din("final_norm", [D])
    lam_in = din("lam_vecs", [4 * 64])
    gate_bias_in = din("gate_bias", [8, 2])
    c_ident = din("c_ident", [128, 128])
    c_masks = din("c_masks", [128, 256])
    c_rope = din("c_rope", [128, 2])
    c_pos = din("c_pos", [128, T])
    c_ab = din("c_ab", [8, 2])
    c_rot = din("c_rot", [128, 128])

    W1 = dscr("W1", [D, 2 * DFF])
    W1o = dscr("W1o", [DFF, D])
    W2 = dscr("W2", [D, 2 * DFF])
    W2o = dscr("W2o", [DFF, D])
    WF = dscr("WF", [D, NF], dbg=True)
    WT = dscr("WT", [D, NT_], dbg=True)
    WA = dscr("WA", [D, D])
    WB = dscr("WB", [D, D])
    WMO = dscr("WMO", [D, D])
    WXQ = dscr("WXQ", [D, D])
    WXKV = dscr("WXKV", [D, 2 * D])
    WXO = dscr("WXO", [D, D])
    sc = []
    for s in range(NSEQ):
        S = S_LIST[s]
        nch = S // 128
        sc.append(dict(
            x1=dscr("x1_%d" % s, [S, D], F32, dbg=True),
            qmT=dscr("qmT_%d" % s, [4, 128, S], dbg=True),
            kmT=dscr("kmT_%d" % s, [4, 128, S], dbg=True),
            qdT=dscr("qdT_%d" % s, [8, 128, S], dbg=True),
            kdT=dscr("kdT_%d" % s, [8, 128, S], dbg=True),
            gaT=dscr("gaT_%d" % s, [8, 128, S], dbg=True),
            gbT=dscr("gbT_%d" % s, [8, 128, S], dbg=True),
            omT=dscr("omT_%d" % s, [8, 128, S], dbg=True),
            gig=dscr("gig_%d" % s, [8, S], F32, dbg=True),
            gfg=dscr("gfg_%d" % s, [8, S], F32, dbg=True),
            km=dscr("km_%d" % s, [S, 512], dbg=True),
            vm=dscr("vm_%d" % s, [S, 1024], dbg=True),
            vd=dscr("vd_%d" % s, [S, 1024], dbg=True),
            dec=dscr("dec_%d" % s, [8 * nch], F32, dbg=True),
            hmT=dscr("hmT_%d" % s, [8, 128, S], dbg=True),
            odT=dscr("odT_%d" % s, [8, 128, S], dbg=True),
        ))

    top = contextlib.ExitStack()
    with top:
        cx = Ctx(nc, top)

        uid = {"n": 0}

        def sb(es, name, shape, dt):
            uid["n"] += 1
            return es.enter_context(nc.sbuf_tensor("sb%d_%s" % (uid["n"], name), list(shape), dt))

        def pst(es, name, shape, dt):
            uid["n"] += 1
            return es.enter_context(nc.psum_tensor("ps%d_%s" % (uid["n"], name), list(shape), dt))

        ident = sb(top, "ident", [128, 128], F32)
        ident16 = sb(top, "ident16", [128, 128], BF16)
        rot32 = sb(top, "rot32", [128, 128], F32)
        rot16 = sb(top, "rot16", [128, 128], BF16)
        masks = sb(top, "masks", [128, 256], F32)
        gcols = sb(top, "gcols", [128, 56], F32)
        ropec = sb(top, "ropec", [128, 2], F32)
        posrow = sb(top, "posrow", [128, T], F32)
        abc = sb(top, "abc", [8, 2], F32)
        gbias = sb(top, "gbias", [8, 2], F32)
        lamv = sb(top, "lamv", [128, 256], F32)
        lamc = sb(top, "lamc", [128, 4], F32)
        fgain = sb(top, "fgain", [128, D], F32)
        b_const = Buf("const")
        for dst, src in ((ident, c_ident), (masks, c_masks), (gcols, gcols_in), (ropec, c_rope),
                         (posrow, c_pos), (abc, c_ab), (gbias, gate_bias_in), (rot32, c_rot)):
            cx.dma("sp", dst[:], src[:, :], w=[b_const], merge=True)
        cx.dma("sp", lamv[:], lam_in.partition_broadcast(128), w=[b_const], merge=True)
        cx.dma("sp", fgain[:], final_gain.partition_broadcast(128), w=[b_const], merge=True)
        b_c2 = Buf("const2")
        cx.op("dve", lambda e: e.tensor_copy(ident16[:], ident[:]), r=[b_const], w=[b_c2])
        cx.op("dve", lambda e: e.tensor_copy(rot16[:], rot32[:]), r=[b_const, b_c2], w=[b_c2])
        b_l = Buf("lam")
        cx.op("dve", lambda e: e.tensor_tensor(lamv[:, 0:64], lamv[:, 0:64], lamv[:, 64:128], ALU.mult),
              r=[b_const], w=[b_l])
        cx.op("dve", lambda e: e.tensor_tensor(lamv[:, 128:192], lamv[:, 128:192], lamv[:, 192:256], ALU.mult),
              r=[b_l], w=[b_l])
        cx.op("dve", lambda e: e.tensor_reduce(lamc[:, 0:1], lamv[:, 0:64], AX.X, ALU.add), r=[b_l], w=[b_l])
        cx.op("dve", lambda e: e.tensor_reduce(lamc[:, 1:2], lamv[:, 128:192], AX.X, ALU.add), r=[b_l], w=[b_l])
        cx.op("act", lambda e: e.activation(out=lamc[:, 0:2], in_=lamc[:, 0:2], func=AF.Exp), r=[b_l], w=[b_l])
        cx.op("dve", lambda e: e.tensor_tensor(lamc[:, 2:3], lamc[:, 0:1], lamc[:, 1:2], ALU.subtract),
              r=[b_l], w=[b_l])
        cx.op("dve", lambda e: e.tensor_scalar(lamc[:, 3:4], lamc[:, 2:3], LAMBDA_INIT, -1.0, ALU.add, ALU.mult),
              r=[b_l], w=[b_l])
        NEG_LAM = lamc[:, 3:4]

        with contextlib.ExitStack() as ph:
            CW = 2048
            wl = [sb(ph, "wl%d" % i, [128, 4112], F32) for i in range(3)]
            wo = [sb(ph, "wo%d" % i, [128, 5136], BF16) for i in range(3)]
            wl_b = [Buf("wl%d" % i) for i in range(3)]
            wo_b = [Buf("wo%d" % i) for i in range(3)]
            state = {"i": 0, "e": 0}

            def conv_engine():
                state["e"] += 1
                return "act" if state["e"] % 2 else "dve"

            def scale_op(dst_ap, src_ap, gcol, const, rb, wb, eng=None):
                e = eng or conv_engine()
                if gcol is None:
                    if e == "act":
                        cx.op("act", lambda en: en.activation(out=dst_ap, in_=src_ap, func=AF.Copy, scale=float(const)),
                              r=rb, w=wb)
                    else:
                        cx.op("dve", lambda en: en.tensor_scalar(dst_ap, src_ap, float(const), None, ALU.mult),
                              r=rb, w=wb)
                elif const == 1.0 and e == "act":
                    cx.op("act", lambda en: en.activation(out=dst_ap, in_=src_ap, func=AF.Copy, scale=gcol),
                          r=rb, w=wb)
                else:
                    cx.op("dve", lambda en: en.tensor_scalar(dst_ap, src_ap, gcol, float(const), ALU.mult, ALU.mult),
                          r=rb, w=wb)

            def convert_plain(src, dst, K, N, gidx=None, const=1.0):
                for kc in range(K // 128):
                    for c0 in range(0, N, CW):
                        cw = min(CW, N - c0)
                        i = state["i"] % 3
                        state["i"] += 1
                        cx.dma("sp", wl[i][:, 0:cw], src[kc * 128:(kc + 1) * 128, c0:c0 + cw], w=[wl_b[i]])
                        g = None if gidx is None else gcols[:, gidx * 8 + kc:gidx * 8 + kc + 1]
                        scale_op(wo[i][:, 0:cw], wl[i][:, 0:cw], g, const, [wl_b[i], b_const], [wo_b[i]])
                        cx.dma("pool", dst[kc * 128:(kc + 1) * 128, c0:c0 + cw], wo[i][:, 0:cw], r=[wo_b[i]])

            convert_plain(w_ffn1_in, W1, D, 2 * DFF, gidx=0)
            convert_plain(w_ffn1_out, W1o, DFF, D)
            for kc in range(8):
                g = gcols[:, 8 + kc:8 + kc + 1]
                halves = []
                for hf in range(2):
                    i = state["i"] % 3
                    state["i"] += 1
                    c0 = hf * 4112
                    cwid = 4112 if hf == 0 else NMIX - 4112
                    cx.dma("sp", wl[i][:, 0:cwid], w_mix_in[kc * 128:(kc + 1) * 128, c0:c0 + cwid], w=[wl_b[i]])
                    halves.append(i)
                ia, ib = halves
                io = state["i"] % 3
                A_, B_ = wl[ia], wl[ib]
                rA, rB = [wl_b[ia], b_const], [wl_b[ib], b_const]
                OF1, OF2, OT = wo[0], wo[1], wo[2]
                bF1, bF2, bT = wo_b[0], wo_b[1], wo_b[2]
                kscale = 128.0 ** -0.5
                SPL = 3072
                scale_op(OF1[:, 0:512], A_[:, O_QM:O_QM + 512], g, 1.0, rA, [bF1])
                scale_op(OF1[:, 512:1024], A_[:, O_KM:O_KM + 512], g, kscale, rA, [bF1])
                scale_op(OF1[:, C_QD * 128:C_QD * 128 + 1024], A_[:, O_QD:O_QD + 1024], g, 1.0, rA, [bF1])
                scale_op(OF1[:, C_KD * 128:C_KD * 128 + 1024], B_[:, O_KD - 4112:O_KD - 4112 + 1024], g, 1.0, rB, [bF1])
                scale_op(OF2[:, C_GA * 128 - SPL:C_GA * 128 - SPL + 1024], B_[:, O_GA - 4112:O_GA - 4112 + 1024],
                         g, 1.0, rB, [bF2])
                scale_op(OF2[:, C_GB * 128 - SPL:C_GB * 128 - SPL + 1024], B_[:, O_GB - 4112:O_GB - 4112 + 1024],
                         g, 1.0, rB, [bF2])
                scale_op(OF2[:, C_OM * 128 - SPL:C_OM * 128 - SPL + 1024], A_[:, O_OM:O_OM + 1024], g, 1.0, rA, [bF2])
                scale_op(OF2[:, NFC * 128 - SPL:NFC * 128 + 16 - SPL], A_[:, O_IG:O_IG + 16], g, 1.0, rA, [bF2], eng="dve")
                scale_op(OT[:, 0:512], A_[:, O_KM:O_KM + 512], g, kscale, rA, [bT])
                scale_op(OT[:, 512:1536], A_[:, O_VM:O_VM + 1024], g, 1.0, rA, [bT])
                scale_op(OT[:, 1536:2560], B_[:, O_VD - 4112:O_VD - 4112 + 1024], g, 1.0, rB, [bT])
                rows = slice(kc * 128, (kc + 1) * 128)
                cx.dma("pool", WF[rows, 0:SPL], OF1[:, 0:SPL], r=[bF1])
                cx.dma("pool", WF[rows, SPL:NF], OF2[:, 0:NF - SPL], r=[bF2])
                cx.dma("pool", WT[rows, :], OT[:, 0:NT_], r=[bT])
            cx.barrier()

        BG_LIST = [(w_br_a, WA, D, D, 2, 1.0), (w_br_b, WB, D, D, 3, 1.0 - LAMBDA_INIT), (w_mix_out, WMO, D, D, None, 1.0),
                   (w_xq, WXQ, D, D, 4, 1.0), (w_xkv, WXKV, D, 2 * D, 5, 1.0), (w_xo, WXO, D, D, None, 1.0),
                   (w_ffn2_in, W2, D, 2 * DFF, 6, 1.0), (w_ffn2_out, W2o, DFF, D, None, 1.0)]

        def bg_convert(bwl, bwl_b, bwo, bwo_b):
            NB = len(bwl)
            pieces = []
            for (src, dst, K, N, gidx, const) in BG_LIST:
                for kc in range(K // 128):
                    for c0 in range(0, N, 512):
                        pieces.append((src, dst, kc, c0, min(512, N - c0), gidx, const))
            npc = len(pieces)
            for n in range(npc + 4):
                if n < npc:
                    (src, dst, kc, c0, cw, gidx, const) = pieces[n]
                    i = n % NB
                    cx.dma("sp", bwl[i][:, 0:cw], src[kc * 128:(kc + 1) * 128, c0:c0 + cw], w=[bwl_b[i]])
                m = n - 2
                if 0 <= m < npc:
                    (src, dst, kc, c0, cw, gidx, const) = pieces[m]
                    i = m % NB
                    g = None if gidx is None else gcols[:, gidx * 8 + kc:gidx * 8 + kc + 1]
                    scale_op(bwo[i][:, 0:cw], bwl[i][:, 0:cw], g, const, [bwl_b[i], b_const], [bwo_b[i]])
                m = n - 4
                if 0 <= m < npc:
                    (src, dst, kc, c0, cw, gidx, const) = pieces[m]
                    i = m % NB
                    cx.dma("pool", dst[kc * 128:(kc + 1) * 128, c0:c0 + cw], bwo[i][:, 0:cw], r=[bwo_b[i]])
                yield

        tiles = [(s, t0) for s in range(NSEQ) for t0 in range(0, S_LIST[s], T)]

        def rowlocal_pools(ph):
            P = {}
            P["ws"] = [sb(ph, "ws%d" % i, [128, 8, 512], BF16) for i in range(NWS)]
            P["ws_b"] = [Buf("ws%d" % i) for i in range(NWS)]
            P["wsi"] = 0
            P["mm"] = [pst(ph, "mm%d" % i, [128, 512], F32) for i in range(6)]
            P["mm_b"] = [Buf("mm%d" % i) for i in range(6)]
            P["mmi"] = 0
            trp = [pst(ph, "trp%d" % i, [128, 1024], BF16) for i in range(2)]
            P["tr"] = [trp[0][:, :], trp[1][:, :]]
            P["tr_b"] = [Buf("tr0"), Buf("tr1")]
            P["tri"] = 0
            P["ev"] = 0
            return P

        def wload(P, Wd, k0, nk, c0, ncol):
            i = P["wsi"] % NWS
            P["wsi"] += 1
            cx.dma("sp", P["ws"][i][:, 0:nk, 0:ncol],
                   Wd[k0 * 128:(k0 + nk) * 128, c0:c0 + ncol].rearrange("(k p) n -> p k n", p=128),
                   w=[P["ws_b"][i]])
            return P["ws"][i], P["ws_b"][i]

        def mm_next(P):
            i = P["mmi"] % 6
            P["mmi"] += 1
            return P["mm"][i], P["mm_b"][i]

        def tr_next(P):
            i = P["tri"] % 2
            P["tri"] += 1
            return P["tr"][i], P["tr_b"][i]

        def ev_engine(P):
            P["ev"] += 1
            return "act" if P["ev"] % 2 else "dve"

        def copy_op(e, out, in_):
            if e == "act":
                return lambda en: en.activation(out=out, in_=in_, func=AF.Copy)
            return lambda en: en.tensor_copy(out, in_)

        def rms_to_fm(P, xt, xt_b, xn, xn_b, xnT, xnT_b, st, st_b, junk, junk_b, ntok=4, width=D):
            for j in range(ntok):
                cx.op("act", lambda en, j=j: en.activation(out=junk[:, 0:width], in_=xt[:, j, :], func=AF.Square,
                                                           accum_out=st[:, j:j + 1]),
                      r=[xt_b], w=[junk_b, st_b])
            cx.op("act", lambda en: en.activation(out=st[:, 4:4 + ntok], in_=st[:, 0:ntok], func=AF.Sqrt,
                                                  scale=1.0 / width, bias=EPS), r=[st_b], w=[st_b])
            cx.op("dve", lambda en: en.reciprocal(st[:, 8:8 + ntok], st[:, 4:4 + ntok]), r=[st_b], w=[st_b])
            nkc = width // 128
            for j in range(ntok):
                xb = xn_b[j]
                if j % 2 == 0:
                    cx.op("act", lambda en, j=j: en.activation(out=xn[:, j, :], in_=xt[:, j, :], func=AF.Copy,
                                                               scale=st[:, 8 + j:9 + j]),
                          r=[xt_b, st_b], w=[xb])
                else:
                    cx.op("dve", lambda en, j=j: en.tensor_scalar(xn[:, j, :], xt[:, j, :], st[:, 8 + j:9 + j], None,
                                                                  ALU.mult),
                          r=[xt_b, st_b], w=[xb])
                tp, tp_b = tr_next(P)
                for kc in range(nkc):
                    cx.op("pe", lambda en, j=j, kc=kc, tp=tp: en.transpose(
                        tp[:, kc * 128:(kc + 1) * 128], xn[:, j, kc * 128:(kc + 1) * 128], ident16[:]),
                        r=[xb, b_c2], w=[tp_b], inc=(kc == nkc - 1))
                e = ev_engine(P)
                cx.op(e, copy_op(e, xnT[:, 0:nkc, j * 128:(j + 1) * 128],
                                 tp[:, 0:nkc * 128].rearrange("p (k n) -> p k n", k=nkc)), r=[tp_b], w=[xnT_b])

        def ffn(P, Win, Wout, xnT, xnT_b, hT, hT_b, sg, sg_b, xt, xt_b, mid=None, step=None):
            for blk in range(6):
                if step is not None:
                    step()
                nchk = 4 if blk < 5 else 2
                wg, wg_b = wload(P, Win, 0, 8, blk * 512, nchk * 128)
                wu, wu_b = wload(P, Win, 0, 8, DFF + blk * 512, nchk * 128)
                for c in range(nchk):
                    fc = blk * 4 + c
                    pg, pg_b = mm_next(P)
                    for kc in range(8):
                        cx.op("pe", lambda en, kc=kc, c=c, pg=pg, wg=wg: en.matmul(
                            pg[:], wg[:, kc, c * 128:(c + 1) * 128], xnT[:, kc, :], start=(kc == 0), stop=(kc == 7)),
                            r=[wg_b, xnT_b], w=[pg_b], inc=(kc == 7))
                    si = fc % 2
                    cx.op("act", lambda en, pg=pg, si=si: en.activation(out=sg[si][:], in_=pg[:], func=AF.Silu),
                          r=[pg_b], w=[sg_b[si]])
                    pu, pu_b = mm_next(P)
                    for kc in range(8):
                        cx.op("pe", lambda en, kc=kc, c=c, pu=pu, wu=wu: en.matmul(
                            pu[:], wu[:, kc, c * 128:(c + 1) * 128], xnT[:, kc, :], start=(kc == 0), stop=(kc == 7)),
                            r=[wu_b, xnT_b], w=[pu_b], inc=(kc == 7))
                    cx.op("dve", lambda en, pu=pu, si=si, fc=fc: en.tensor_tensor(hT[:, fc, :], pu[:], sg[si][:], ALU.mult),
                          r=[pu_b, sg_b[si]], w=[hT_b])
            if mid is not None:
                mid()
            for nb in range(2):
                if step is not None:
                    step()
                accs = [mm_next(P) for _ in range(4)]
                for ksb, (k0, nk) in enumerate(((0, 8), (8, 8), (16, 6))):
                    wv, wv_b = wload(P, Wout, k0, nk, nb * 512, 512)
                    for j in range(4):
                        for k in range(nk):
                            fc = k0 + k
                            cx.op("pe", lambda en, j=j, k=k, fc=fc, wv=wv: en.matmul(
                                accs[j][0][:], hT[:, fc, j * 128:(j + 1) * 128], wv[:, k, :],
                                start=(fc == 0), stop=(fc == 21)),
                                r=[wv_b, hT_b], w=[accs[j][1]], inc=(k == nk - 1))
                for j in range(4):
                    cx.op("dve", lambda en, j=j, nb=nb: en.scalar_tensor_tensor(
                        xt[:, j, nb * 512:(nb + 1) * 512], accs[j][0][:], 0.5, xt[:, j, nb * 512:(nb + 1) * 512],
                        ALU.mult, ALU.add), r=[accs[j][1], xt_b], w=[xt_b])

        if stop_after >= 1:
            with contextlib.ExitStack() as ph:
                P = rowlocal_pools(ph)
                xts = [sb(ph, "xt%d" % i, [128, 4, D], F32) for i in range(2)]
                xts_b = [Buf("xt%d" % i) for i in range(2)]
                xn = sb(ph, "xn", [128, 4, D], BF16)
                xn_b = [Buf("xn%d" % j_) for j_ in range(4)]
                xnT = sb(ph, "xnT", [128, 8, T], BF16)
                xnT_b = Buf("xnT")
                xnT1 = [sb(ph, "xnT1_%d" % i, [128, 8, T], BF16) for i in range(2)]
                xnT1_b = [Buf("xnT1_0"), Buf("xnT1_1")]
                hT = sb(ph, "hT", [128, 22, T], BF16)
                hT_b = Buf("hT")
                sg = [sb(ph, "sg%d" % i, [128, T], BF16) for i in range(2)]
                sg_b = [Buf("sg0"), Buf("sg1")]
                st = sb(ph, "st", [128, 16], F32)
                st_b = Buf("st")
                junk = sb(ph, "junk", [128, D], BF16)
                junk_b = Buf("junk")
                NST = 8
                stg = [sb(ph, "stg%d" % i, [128, T], BF16) for i in range(NST)]
                stg_b = [Buf("stg%d" % i) for i in range(NST)]
                sgt = [sb(ph, "sgt%d" % i, [8, T], F32) for i in range(2)]
                sgt_b = [Buf("sgt0"), Buf("sgt1")]
                ang = sb(ph, "ang", [128, T], F32)
                a2 = sb(ph, "a2", [128, T], F32)
                nf = sb(ph, "nf", [128, T], F32)
                ni = sb(ph, "ni", [128, T], I32)
                cosT = sb(ph, "cosT", [128, T], F32)
                sinT = sb(ph, "sinT", [128, T], F32)
                tab_b = Buf("tab")
                tmp_b = Buf("ropetmp")
                zq = [sb(ph, "zq_%d" % i, [128, T], BF16) for i in range(2)]
                zq_b = [Buf("zq0"), Buf("zq1")]
                r1 = [sb(ph, "r1_%d" % i, [128, T], F32) for i in range(2)]
                r2 = [sb(ph, "r2_%d" % i, [128, T], F32) for i in range(2)]
                r1_b = [Buf("r1_0"), Buf("r1_1")]
                r2_b = [Buf("r2_0"), Buf("r2_1")]
                stc = {"i": 0, "g": 0, "r": 0}
                bwl = [sb(ph, "bwl%d" % i, [128, 512], F32) for i in range(6)]
                bwo = [sb(ph, "bwo%d" % i, [128, 512], BF16) for i in range(6)]
                bg = {"g": bg_convert(bwl, [Buf("bwl%d" % i) for i in range(6)], bwo, [Buf("bwo%d" % i) for i in range(6)])}

                def bg_step(drain=False):
                    bg["n"] = bg.get("n", 0) + 1
                    if not drain and bg["n"] % 2 != 0:
                        return
                    while bg["g"] is not None:
                        try:
                            next(bg["g"])
                        except StopIteration:
                            bg["g"] = None
                        if not drain:
                            return

                def stage_next():
                    i = stc["i"] % NST
                    stc["i"] += 1
                    return stg[i], stg_b[i]

                def load_x(ti):
                    s, t0 = tiles[ti]
                    cx.dma("sp", xts[ti % 2][:], x_in[s][t0:t0 + T, :].rearrange("(j p) d -> p j d", p=128),
                           w=[xts_b[ti % 2]])

                def rope_tables(t0):
                    cx.op("dve", lambda en: en.tensor_scalar(ang[:], posrow[:], float(t0), ropec[:, 0:1], ALU.add, ALU.mult),
                          r=[b_const], w=[tmp_b])
                    for (ph_off, tab) in ((0.0, sinT), (0.5 * math.pi, cosT)):
                        cx.op("dve", lambda en: en.tensor_scalar(a2[:], ang[:], ph_off, None, ALU.add), r=[tmp_b], w=[tmp_b])
                        cx.op("dve", lambda en: en.tensor_scalar(nf[:], a2[:], 1.0 / TWO_PI, None, ALU.mult),
                              r=[tmp_b], w=[tmp_b])
                        cx.op("dve", lambda en: en.tensor_copy(ni[:], nf[:]), r=[tmp_b], w=[tmp_b])
                        cx.op("dve", lambda en: en.tensor_copy(nf[:], ni[:]), r=[tmp_b], w=[tmp_b])
                        cx.op("dve", lambda en: en.scalar_tensor_tensor(a2[:], nf[:], -CW1, a2[:], ALU.mult, ALU.add),
                              r=[tmp_b], w=[tmp_b])
                        cx.op("dve", lambda en: en.scalar_tensor_tensor(a2[:], nf[:], -CW2, a2[:], ALU.mult, ALU.add),
                              r=[tmp_b], w=[tmp_b])
                        cx.op("dve", lambda en: en.tensor_scalar(a2[:], a2[:], math.pi, -math.pi, ALU.min, ALU.max),
                              r=[tmp_b], w=[tmp_b])
                        cx.op("act", lambda en, tab=tab: en.activation(out=tab[:], in_=a2[:], func=AF.Sin),
                              r=[tmp_b], w=[tab_b])

                load_x(0)
                rms_to_fm(P, xts[0], xts_b[0], xn, xn_b, xnT1[0], xnT1_b[0], st, st_b, junk, junk_b)
                for ti, (s, t0) in enumerate(tiles):
                    xt, xt_b = xts[ti % 2], xts_b[ti % 2]
                    if ti + 1 < len(tiles):
                        load_x(ti + 1)
                    SC = sc[s]
                    ffn(P, W1, W1o, xnT1[ti % 2], xnT1_b[ti % 2], hT, hT_b, sg, sg_b, xt, xt_b,
                        mid=lambda t0=t0: rope_tables(t0), step=bg_step)
                    cx.dma("pool", SC["x1"][t0:t0 + T, :].rearrange("(j p) d -> p j d", p=128), xt[:], r=[xt_b])
                    rms_to_fm(P, xt, xt_b, xn, xn_b, xnT, xnT_b, st, st_b, junk, junk_b)
                    rope_pending = []
                    for blk in range(NFC // 4):
                        bg_step()
                        wv, wv_b = wload(P, WF, 0, 8, blk * 512, 512)
                        for c in range(4):
                            ch = blk * 4 + c
                            pz, pz_b = mm_next(P)
                            for kc in range(8):
                                cx.op("pe", lambda en, kc=kc, c=c, pz=pz, wv=wv: en.matmul(
                                    pz[:], wv[:, kc, c * 128:(c + 1) * 128], xnT[:, kc, :],
                                    start=(kc == 0), stop=(kc == 7)),
                                    r=[wv_b, xnT_b], w=[pz_b], inc=(kc == 7))
                            for fn in rope_pending:
                                fn()
                            rope_pending = []
                            if ch < C_QD:
                                dst = SC["qmT"] if ch < C_KM else SC["kmT"]
                                sg_, sgb_ = stage_next()
                                e = ev_engine(P)
                                cx.op(e, copy_op(e, sg_[:], pz[:]), r=[pz_b], w=[sgb_])
                                cx.dma("pool", dst[ch % 4][:, t0:t0 + T], sg_[:], r=[sgb_])
                            elif ch < C_GA:
                                rel = ch - C_QD
                                hh = rel % 8
                                dst = SC["qdT"] if rel < 8 else SC["kdT"]
                                ri = stc["r"] % 2
                                stc["r"] += 1
                                cx.op("act", lambda en, pz=pz, ri=ri: en.activation(out=zq[ri][:], in_=pz[:], func=AF.Copy),
                                      r=[pz_b], w=[zq_b[ri]])

                                def rope_tail(ri=ri, dst=dst, hh=hh):
                                    pr, pr_b = mm_next(P)
                                    cx.op("pe", lambda en: en.matmul(pr[:], rot16[:], zq[ri][:], start=True, stop=True),
                                          r=[zq_b[ri], b_c2], w=[pr_b])
                                    cx.op("dve", lambda en: en.tensor_tensor(r1[ri][:], zq[ri][:], cosT[:], ALU.mult),
                                          r=[zq_b[ri], tab_b], w=[r1_b[ri]])
                                    cx.op("dve", lambda en: en.tensor_tensor(r2[ri][:], pr[:], sinT[:], ALU.mult),
                                          r=[pr_b, tab_b], w=[r2_b[ri]])
                                    sg_, sgb_ = stage_next()
                                    cx.op("dve", lambda en: en.tensor_tensor(sg_[:], r1[ri][:], r2[ri][:], ALU.add),
                                          r=[r1_b[ri], r2_b[ri]], w=[sgb_])
                                    cx.dma("pool", dst[hh][:, t0:t0 + T], sg_[:], r=[sgb_])
                                rope_pending.append(rope_tail)
                            else:
                                dst = SC["gaT"] if ch < C_GB else (SC["gbT"] if ch < C_OM else SC["omT"])
                                sg_, sgb_ = stage_next()
                                cx.op("act", lambda en, pz=pz, sg_=sg_: en.activation(out=sg_[:], in_=pz[:], func=AF.Sigmoid),
                                      r=[pz_b], w=[sgb_])
                                cx.dma("pool", dst[ch % 8][:, t0:t0 + T], sg_[:], r=[sgb_])
                    for fn in rope_pending:
                        fn()
                    rope_pending = []
                    wv, wv_b = wload(P, WF, 0, 8, NFC * 128, 16)
                    for gi, dst in ((0, SC["gig"]), (1, SC["gfg"])):
                        pz, pz_b = mm_next(P)
                        for kc in range(8):
                            cx.op("pe", lambda en, kc=kc, gi=gi, pz=pz, wv=wv: en.matmul(
                                pz[0:8, :], wv[:, kc, gi * 8:(gi + 1) * 8], xnT[:, kc, :], start=(kc == 0), stop=(kc == 7)),
                                r=[wv_b, xnT_b], w=[pz_b], inc=(kc == 7))
                        k = stc["g"] % 2
                        stc["g"] += 1
                        cx.op("act", lambda en, pz=pz, k=k, gi=gi: en.activation(
                            out=sgt[k][:], in_=pz[0:8, :], func=AF.Identity, bias=gbias[:, gi:gi + 1]),
                            r=[pz_b, b_const], w=[sgt_b[k]])
                        cx.dma("pool", dst[:, t0:t0 + T], sgt[k][:], r=[sgt_b[k]])
                    if ti + 1 < len(tiles):
                        nx = (ti + 1) % 2
                        rms_to_fm(P, xts[nx], xts_b[nx], xn, xn_b, xnT1[nx], xnT1_b[nx], st, st_b, junk, junk_b)
                    for blk in range(5):
                        bg_step()
                        wv, wv_b = wload(P, WT, 0, 8, blk * 512, 512)
                        for j in range(4):
                            pz, pz_b = mm_next(P)
                            for kc in range(8):
                                cx.op("pe", lambda en, kc=kc, j=j, pz=pz, wv=wv: en.matmul(
                                    pz[:], xnT[:, kc, j * 128:(j + 1) * 128], wv[:, kc, :], start=(kc == 0), stop=(kc == 7)),
                                    r=[wv_b, xnT_b], w=[pz_b], inc=(kc == 7))
                            sg_, sgb_ = stage_next()
                            e = ev_engine(P)
                            cx.op(e, copy_op(e, sg_[:], pz[:]), r=[pz_b], w=[sgb_])
                            rows = slice(t0 + j * 128, t0 + (j + 1) * 128)
                            if blk == 0:
                                dstap = SC["km"][rows, :]
                            elif blk < 3:
                                dstap = SC["vm"][rows, (blk - 1) * 512:blk * 512]
                            else:
                                dstap = SC["vd"][rows, (blk - 3) * 512:(blk - 2) * 512]
                            cx.dma("pool", dstap, sg_[:], r=[sgb_])
                bg_step(drain=True)
                cx.barrier()


        for s in range(NSEQ if stop_after >= 2 else 0):
            S = S_LIST[s]
            nch = S // 128
            SC = sc[s]
            NBK = 4 if nch % 4 == 0 else 1
            with contextlib.ExitStack() as seqst:
                aT = sb(seqst, "aT", [128, nch * 8], F32)
                rT = sb(seqst, "rT", [128, nch * 8], F32)
                decb = sb(seqst, "decb", [128, 8 * nch], F32)
                b_gt = Buf("gt")
                with contextlib.ExitStack() as ph:
                    gA = sb(ph, "gA", [8, S], F32)
                    gB = sb(ph, "gB", [8, S], F32)
                    gC = sb(ph, "gC", [8, S], F32)
                    sm = [sb(ph, "gsm%d" % i, [8, nch], F32) for i in range(8)]
                    tpa = pst(ph, "tpa", [128, 512], F32)
                    tpr = pst(ph, "tpr", [128, 512], F32)
                    b_tpa, b_tpr = Buf("tpa"), Buf("tpr")
                    bg = Buf("g")
                    AL, BE = abc[0:8, 0:1], abc[0:8, 1:2]

                    def G(e, fn):
                        cx.op(e, fn, r=[bg, b_const], w=[bg])

                    cx.dma("sp", gC[:], SC["gig"][:, :], w=[bg])
                    cx.dma("sp", gA[:], SC["gfg"][:, :], w=[bg], merge=True)
                    G("act", lambda en: en.activation(out=gA[:], in_=gA[:], func=AF.Exp, scale=-1.0))
                    G("act", lambda en: en.activation(out=gA[:], in_=gA[:], func=AF.Ln, bias=1.0))
                    G("dve", lambda en: en.tensor_tensor_scan(gB[:], gA[:], gA[:], 0.0, ALU.add, ALU.bypass))
                    G("dve", lambda en: en.tensor_scalar(gA[:], gA[:], gB[:, S - 1:S], BE, ALU.add, ALU.mult))
                    G("dve", lambda en: en.scalar_tensor_tensor(gB[:], gB[:], AL, gA[:], ALU.mult, ALU.add))
                    G("dve", lambda en: en.tensor_tensor(gA[:], gC[:], gB[:], ALU.add))
                    G("dve", lambda en: en.tensor_reduce(sm[0][:], gA[:].rearrange("p (c l) -> p c l", l=128), AX.X, ALU.max))
                    G("dve", lambda en: en.tensor_tensor_scan(sm[1][:], sm[0][:], sm[0][:], 0.0, ALU.max, ALU.bypass))
                    G("dve", lambda en: en.memset(sm[2][:], 0.0))
                    if nch > 1:
                        G("dve", lambda en: en.tensor_copy(sm[2][:, 1:nch], sm[1][:, 0:nch - 1]))
                    cur = sm[0]
                    pp = [sm[3], sm[4]]
                    sh = 1
                    k = 0
                    while sh < nch:
                        nxt = pp[k % 2]
                        G("dve", lambda en, nxt=nxt, cur=cur, sh=sh: en.tensor_tensor(
                            nxt[:, 0:nch - sh], cur[:, 0:nch - sh], cur[:, sh:nch], ALU.max))
                        G("dve", lambda en, nxt=nxt, cur=cur, sh=sh: en.tensor_copy(nxt[:, nch - sh:nch], cur[:, nch - sh:nch]))
                        cur = nxt
                        k += 1
                        sh *= 2
                    G("dve", lambda en: en.memset(sm[5][:], 0.0))
                    if nch > 1:
                        G("dve", lambda en: en.tensor_scalar(sm[5][:, 0:nch - 1], cur[:, 1:nch], 0.0, None, ALU.max))
                    G("dve", lambda en: en.tensor_tensor(sm[5][:], sm[5][:], sm[2][:], ALU.subtract))
                    G("dve", lambda en: en.scalar_tensor_tensor(sm[6][:], sm[5][:], BE, sm[2][:], ALU.mult, ALU.add))
                    if NBK > 1:
                        M3 = sm[6][:].rearrange("p (b k) -> p b k", k=NBK)
                        F3 = sm[7][:].rearrange("p (b k) -> p b k", k=NBK)
                        B3 = sm[0][:].rearrange("p (b k) -> p b k", k=NBK)
                        G("dve", lambda en: en.tensor_copy(F3, M3[:, :, 0:1].broadcast_to([8, nch // NBK, NBK])))
                        G("dve", lambda en: en.tensor_copy(B3, M3[:, :, NBK - 1:NBK].broadcast_to([8, nch // NBK, NBK])))
                        G("dve", lambda en: en.tensor_tensor(sm[0][:], sm[0][:], sm[7][:], ALU.subtract))
                        G("dve", lambda en: en.scalar_tensor_tensor(sm[6][:], sm[0][:], BE, sm[7][:], ALU.mult, ALU.add))
                    Mt = sm[6]
                    G("dve", lambda en: en.tensor_copy(sm[1][:], Mt[:]))
                    G("dve", lambda en: en.tensor_copy(sm[3][:], Mt[:]))
                    if nch > 1:
                        G("dve", lambda en: en.tensor_copy(sm[1][:, 0:nch - 1], Mt[:, 1:nch]))
                        G("dve", lambda en: en.tensor_copy(sm[3][:, 1:nch], Mt[:, 0:nch - 1]))
                    G("dve", lambda en: en.tensor_tensor(sm[3][:], sm[3][:], sm[1][:], ALU.subtract))
                    G("dve", lambda en: en.scalar_tensor_tensor(sm[3][:], sm[3][:], BE, sm[1][:], ALU.mult, ALU.add))
                    G("dve", lambda en: en.tensor_tensor(sm[4][:], Mt[:], sm[3][:], ALU.subtract))
                    G("act", lambda en: en.activation(out=sm[4][:], in_=sm[4][:], func=AF.Exp))
                    b_decd = Buf("decd")
                    cx.dma("pool", SC["dec"].rearrange("(r c) -> r c", r=8), sm[4][:], r=[bg], w=[b_decd])
                    cx.dma("sp", decb[:], SC["dec"].partition_broadcast(128), r=[b_decd], w=[b_gt])
                    Mbc = Mt[:].unsqueeze(2).broadcast_to([8, nch, 128])
                    gA3 = gA[:].rearrange("p (c l) -> p c l", l=128)
                    gB3 = gB[:].rearrange("p (c l) -> p c l", l=128)
                    G("dve", lambda en: en.tensor_tensor(gA3, gA3, Mbc, ALU.subtract))
                    G("act", lambda en: en.activation(out=gA[:], in_=gA[:], func=AF.Exp))
                    G("dve", lambda en: en.tensor_tensor(gB3, gB3, Mbc, ALU.subtract))
                    G("act", lambda en: en.activation(out=gB[:], in_=gB[:], func=AF.Exp))
                    for (src, tp_, tpb_, dstt) in ((gA, tpa, b_tpa, aT), (gB, tpr, b_tpr, rT)):
                        for c in range(nch):
                            cx.op("pe", lambda en, c=c, src=src, tp_=tp_: en.transpose(
                                tp_[:, c * 8:(c + 1) * 8], src[:, c * 128:(c + 1) * 128], ident[0:8, 0:8]),
                                r=[bg, b_const], w=[tpb_], inc=(c == nch - 1))
                        cx.op("dve", lambda en, tp_=tp_, dstt=dstt: en.tensor_copy(dstt[:], tp_[:, 0:nch * 8]),
                              r=[tpb_], w=[b_gt] if dstt is aT else [b_gt])
                    cx.barrier()

                if stop_after >= 3:
                    with contextlib.ExitStack() as ph:
                        qT = sb(ph, "m_qT", [128, S], BF16)
                        kT = sb(ph, "m_kT", [128, S], BF16)
                        kTM = sb(ph, "m_kTM", [128, nch, 128], BF16)
                        vp = sb(ph, "m_vp", [128, nch, 257], BF16)
                        hfirst = sb(ph, "m_hf", [128, nch, 256], F32)
                        S32 = [sb(ph, "m_S32_%d" % d, [128, 257], F32) for d in range(2)]
                        S16 = [[sb(ph, "m_S16_%d_%d" % (d, k), [128, 257], BF16) for k in range(2)] for d in range(2)]
                        tmpS = [sb(ph, "m_tmp_%d" % d, [128, 257], F32) for d in range(2)]
                        WTt = [[sb(ph, "m_WT_%d_%d" % (d, k), [128, 128], BF16) for k in range(2)] for d in range(2)]
                        ks = [[sb(ph, "m_ks_%d_%d" % (d, k), [128, 128], BF16) for k in range(2)] for d in range(2)]
                        dn = [sb(ph, "m_dn_%d" % d, [128, 8], F32) for d in range(2)]
                        hs = [sb(ph, "m_hs_%d" % i, [128, 256], F32) for i in range(4)]
                        hn = [sb(ph, "m_hn_%d" % i, [128, 256], BF16) for i in range(4)]
                        hstg = [sb(ph, "m_hstg_%d" % i, [128, 256], BF16) for i in range(4)]
                        hst = [sb(ph, "m_hst_%d" % i, [128, 8], F32) for i in range(4)]
                        mjunk = sb(ph, "m_junk", [128, 256], BF16)
                        p_sT = [pst(ph, "m_psT%d" % d, [128, 512], F32) for d in range(2)]
                        p_out = [pst(ph, "m_pout%d" % d, [128, 512], F32) for d in range(2)]
                        p_upd = [pst(ph, "m_pupd%d" % d, [128, 512], F32) for d in range(2)]
                        p_tr = [pst(ph, "m_ptr%d" % d, [128, 1024], BF16) for d in range(2)]
                        b_q, b_k, b_ktm, b_vp, b_hf = Buf("q"), Buf("k"), Buf("ktm"), Buf("vp"), Buf("hf")
                        b_S32 = [Buf("S32"), Buf("S32")]
                        b_S16 = [[Buf("S16"), Buf("S16")], [Buf("S16"), Buf("S16")]]
                        b_tmp = [Buf("tmp"), Buf("tmp")]
                        b_WT = [[Buf("WT"), Buf("WT")], [Buf("WT"), Buf("WT")]]
                        b_ks = [[Buf("ks"), Buf("ks")], [Buf("ks"), Buf("ks")]]
                        b_dn = [Buf("dn"), Buf("dn")]
                        b_hs = [Buf("hs") for _ in range(4)]
                        b_hn = [Buf("hn") for _ in range(4)]
                        b_hstg = [Buf("hstg") for _ in range(4)]
                        b_hst = [Buf("hst") for _ in range(4)]
                        b_mj = Buf("mj")
                        b_psT = [Buf("psT"), Buf("psT")]
                        b_pout = [Buf("pout"), Buf("pout")]
                        b_pupd = [Buf("pupd"), Buf("pupd")]
                        b_ptr = [Buf("ptr0"), Buf("ptr1")]
                        cx.op("dve", lambda en: en.memset(vp[:, :, 256:257], 1.0), w=[b_vp])
                        sec = {"i": 0}
                        for h in range(4):
                            cx.dma("sp", qT[:], SC["qmT"][h], w=[b_q])
                            cx.dma("sp", kT[:], SC["kmT"][h], w=[b_k])
                            cx.dma("sp", kTM[:], SC["km"][:, h * 128:(h + 1) * 128].rearrange("(c p) d -> p c d", p=128),
                                   w=[b_ktm])
                            cx.dma("sp", vp[:, :, 0:256],
                                   SC["vm"][:, h * 256:(h + 1) * 256].rearrange("(c p) d -> p c d", p=128), w=[b_vp], merge=True)
                            for d in range(2):
                                cx.op("dve", lambda en, d=d: en.memset(S32[d][:], 0.0), w=[b_S32[d]])
                                cx.op("dve", lambda en, d=d: en.memset(S16[d][0][:], 0.0), w=[b_S16[d][0]])

                            def chunk(d, i):
                                return i if d == 0 else nch - 1 - i

                            def gcol(d, c):
                                r_ = d * 4 + h
                                return (aT[:, c * 8 + r_:c * 8 + r_ + 1], rT[:, c * 8 + r_:c * 8 + r_ + 1],
                                        decb[:, r_ * nch + c:r_ * nch + c + 1])

                            def emit_ks(i):
                                for d in range(2):
                                    c = chunk(d, i)
                                    a_col = gcol(d, c)[0]
                                    cx.op("act", lambda en, d=d, c=c, a_col=a_col: en.activation(
                                        out=ks[d][i % 2][:], in_=kTM[:, c, :], func=AF.Copy, scale=a_col),
                                        r=[b_ktm, b_gt], w=[b_ks[d][i % 2]])

                            def emit_s1(i):
                                for d in range(2):
                                    c = chunk(d, i)
                                    cs = slice(c * 128, (c + 1) * 128)
                                    cx.op("pe", lambda en, d=d, cs=cs: en.matmul(p_sT[d][:, 0:128], kT[:, cs], qT[:, cs],
                                                                               start=True, stop=True),
                                          r=[b_k, b_q], w=[b_psT[d]])
                                for d in range(2):
                                    c = chunk(d, i)
                                    a_col = gcol(d, c)[0]
                                    cx.op("dve", lambda en, d=d, a_col=a_col: en.scalar_tensor_tensor(
                                        WTt[d][i % 2][:], p_sT[d][:, 0:128], a_col, masks[:, d * 128:(d + 1) * 128],
                                        ALU.mult, ALU.mult),
                                        r=[b_psT[d], b_gt, b_const], w=[b_WT[d][i % 2]])
                                if i < nch - 1:
                                    for d in range(2):
                                        c = chunk(d, i)
                                        cx.op("pe", lambda en, d=d, c=c: en.matmul(
                                            p_upd[d][:, 0:257], ks[d][i % 2][:], vp[:, c, :],
                                            start=(i % NBK == 0), stop=(i % NBK == NBK - 1 or i == nch - 2),
                                            skip_group_check=True),
                                            r=[b_ks[d][i % 2], b_vp], w=[b_pupd[d]])

                            if nch > 1:
                                emit_ks(0)
                            emit_s1(0)
                            pending = []
                            for i in range(nch):
                                if i + 1 < nch - 1:
                                    emit_ks(i + 1)
                                for d in range(2):
                                    c = chunk(d, i)
                                    cs = slice(c * 128, (c + 1) * 128)
                                    cx.op("pe", lambda en, d=d, cs=cs: en.matmul(p_out[d][:, 0:257], qT[:, cs], S16[d][i % 2][:],
                                                                               start=True, stop=False),
                                          r=[b_q, b_S16[d][i % 2]], w=[b_pout[d]], inc=False)
                                    cx.op("pe", lambda en, d=d, c=c: en.matmul(p_out[d][:, 0:257], WTt[d][i % 2][:], vp[:, c, :],
                                                                             start=False, stop=True),
                                          r=[b_WT[d][i % 2], b_vp], w=[b_pout[d]])
                                for fn in pending:
                                    fn()
                                pending = []
                                if i < nch - 1:
                                    for d in range(2):
                                        c = chunk(d, i)
                                        dec_col = gcol(d, c)[2]
                                        nxt = S16[d][(i + 1) % 2]
                                        nxt_b = b_S16[d][(i + 1) % 2]
                                        if (i + 1) % NBK == 0:
                                            cx.op("dve", lambda en, d=d: en.tensor_tensor(tmpS[d][:], S32[d][:], p_upd[d][:, 0:257], ALU.add),
                                                  r=[b_S32[d], b_pupd[d]], w=[b_tmp[d]])
                                            cx.op("dve", lambda en, d=d, dec_col=dec_col: en.tensor_scalar(
                                                S32[d][:], tmpS[d][:], dec_col, None, ALU.mult),
                                                r=[b_tmp[d], b_gt], w=[b_S32[d]])
                                            cx.op("dve", lambda en, d=d, nxt=nxt: en.tensor_copy(nxt[:], S32[d][:]),
                                                  r=[b_S32[d]], w=[nxt_b])
                                        else:
                                            cx.op("dve", lambda en, d=d, nxt=nxt: en.tensor_tensor(nxt[:], S32[d][:], p_upd[d][:, 0:257],
                                                                                                ALU.add),
                                                  r=[b_S32[d], b_pupd[d]], w=[nxt_b])
                                if i + 1 < nch:
                                    emit_s1(i + 1)
                                for d in range(2):
                                    cx.op("act", lambda en, d=d: en.activation(out=dn[d][:, 0:1], in_=p_out[d][:, 256:257], func=AF.Abs),
                                          r=[b_pout[d]], w=[b_dn[d]])
                                for d in range(2):
                                    r_col = gcol(d, chunk(d, i))[1]
                                    cx.op("dve", lambda en, d=d, r_col=r_col: en.tensor_tensor(dn[d][:, 1:2], dn[d][:, 0:1], r_col, ALU.max),
                                          r=[b_dn[d], b_gt], w=[b_dn[d]])
                                for d in range(2):
                                    cx.op("dve", lambda en, d=d: en.reciprocal(dn[d][:, 2:3], dn[d][:, 1:2]), r=[b_dn[d]], w=[b_dn[d]])
                                secs = []
                                for d in range(2):
                                    c = chunk(d, i)
                                    other = nch - 1 - i
                                    second_d = (other < i) or (d == 1 and other == i)
                                    if not second_d:
                                        cx.op("act", lambda en, d=d, c=c: en.activation(
                                            out=hfirst[:, c, :], in_=p_out[d][:, 0:256], func=AF.Copy, scale=dn[d][:, 2:3]),
                                            r=[b_pout[d], b_dn[d]], w=[b_hf])
                                    else:
                                        k4 = sec["i"] % 4
                                        sec["i"] += 1
                                        secs.append((d, c, k4))
                                for (d, c, k4) in secs:
                                    cx.op("dve", lambda en, d=d, c=c, k4=k4: en.scalar_tensor_tensor(
                                        hs[k4][:], p_out[d][:, 0:256], dn[d][:, 2:3], hfirst[:, c, :], ALU.mult, ALU.add),
                                        r=[b_pout[d], b_dn[d], b_hf], w=[b_hs[k4]])
                                for (d, c, k4) in secs:
                                    cx.op("act", lambda en, k4=k4: en.activation(out=mjunk[:], in_=hs[k4][:], func=AF.Square,
                                                                                 accum_out=hst[k4][:, 0:1]),
                                          r=[b_hs[k4]], w=[b_mj, b_hst[k4]])
                                for (d, c, k4) in secs:
                                    cx.op("act", lambda en, k4=k4: en.activation(out=hst[k4][:, 1:2], in_=hst[k4][:, 0:1], func=AF.Sqrt,
                                                                                 scale=1.0 / 256, bias=EPS),
                                          r=[b_hst[k4]], w=[b_hst[k4]])
                                for (d, c, k4) in secs:
                                    cx.op("dve", lambda en, k4=k4: en.reciprocal(hst[k4][:, 2:3], hst[k4][:, 1:2]),
                                          r=[b_hst[k4]], w=[b_hst[k4]])
                                for (d, c, k4) in secs:
                                    cx.op("act", lambda en, k4=k4: en.activation(out=hn[k4][:], in_=hs[k4][:], func=AF.Copy,
                                                                                 scale=hst[k4][:, 2:3]),
                                          r=[b_hs[k4], b_hst[k4]], w=[b_hn[k4]])
                                for (d, c, k4) in secs:
                                    cs = slice(c * 128, (c + 1) * 128)

                                    def tail(k4=k4, cs=cs, d=d):
                                        for half in range(2):
                                            cx.op("pe", lambda en, half=half: en.transpose(
                                                p_tr[d][:, half * 128:(half + 1) * 128], hn[k4][:, half * 128:(half + 1) * 128],
                                                ident16[:]),
                                                r=[b_hn[k4], b_c2], w=[b_ptr[d]], inc=(half == 1))
                                        cx.op("dve", lambda en: en.tensor_copy(hstg[k4][:], p_tr[d][:, 0:256]),
                                              r=[b_ptr[d]], w=[b_hstg[k4]])
                                        cx.dma("pool", SC["hmT"][h * 2:h * 2 + 2, :, cs].rearrange("t p n -> p t n"),
                                               hstg[k4][:].rearrange("p (t n) -> p t n", t=2), r=[b_hstg[k4]])
                                    pending.append(tail)
                            for fn in pending:
                                fn()
                            pending = []
                        cx.barrier()

                if stop_after >= 4:
                    with contextlib.ExitStack() as ph:
                        nkt = nch
                        nqb = S // 512
                        qT2 = [sb(ph, "d_qT%d" % i, [128, S], BF16) for i in range(2)]
                        kT2 = [sb(ph, "d_kT%d" % i, [128, S], BF16) for i in range(2)]
                        vp2 = [sb(ph, "d_vp%d" % i, [128, nkt, 129], BF16) for i in range(2)]
                        Et = [sb(ph, "d_E%d" % i, [128, 2, 512], BF16) for i in range(3)]
                        accS = [sb(ph, "d_accS%d" % i, [128, 3, 387], F32) for i in range(2)]
                        rc = [sb(ph, "d_rc%d" % i, [128, 20], F32) for i in range(2)]
                        t1 = [sb(ph, "d_t1_%d" % i, [128, 128], F32) for i in range(2)]
                        ot = [sb(ph, "d_o_%d" % i, [128, 4, 128], F32) for i in range(2)]
                        on = [sb(ph, "d_on_%d" % i, [128, 4, 128], BF16) for i in range(2)]
                        ost = [sb(ph, "d_ost_%d" % i, [128, 512], BF16) for i in range(2)]
                        djunk = sb(ph, "d_junk", [128, 128], F32)
                        scp = [pst(ph, "d_scp%d" % i, [128, 2, 512], F32) for i in range(2)]
                        accp = pst(ph, "d_accp", [128, 3, 512], F32)
                        trp2 = pst(ph, "d_trp", [128, 1024], BF16)
                        b_qk = [Buf("qk0"), Buf("qk1")]
                        b_v = [Buf("v0"), Buf("v1")]
                        b_sc = [Buf("sc0"), Buf("sc1")]
                        b_E = [Buf("E0"), Buf("E1"), Buf("E2")]
                        b_acc = Buf("acc")
                        b_accS = [Buf("accS0"), Buf("accS1")]
                        b_rc = [Buf("rc0"), Buf("rc1")]
                        b_t1 = [Buf("t1"), Buf("t1")]
                        b_o = [Buf("o"), Buf("o")]
                        b_on = [Buf("on"), Buf("on")]
                        b_ost = [Buf("ost"), Buf("ost")]
                        b_dj = Buf("dj")
                        b_trp = Buf("trp")
                        for i in range(2):
                            cx.op("dve", lambda en, i=i: en.memset(vp2[i][:, :, 128:129], 1.0), w=[b_v[i]])

                        def accv(a):
                            return accp[:, a // 3, (a % 3) * 129:(a % 3) * 129 + 129]

                        def dload(h):
                            i = h % 2
                            cx.dma("sp", qT2[i][:], SC["qdT"][h], w=[b_qk[i]])
                            cx.dma("sp", kT2[i][:], SC["kdT"][h], w=[b_qk[i]], merge=True)
                            cx.dma("sp", vp2[i][:, :, 0:128],
                                   SC["vd"][:, h * 128:(h + 1) * 128].rearrange("(c p) d -> p c d", p=128),
                                   w=[b_v[i]], merge=True)

                        fin = {"i": 0, "gen": None}

                        def finalize(fi, h, qsl):
                            for b in range(3):
                                wdt = 387 if b < 2 else 258
                                cx.op("dve", lambda en, b=b, wdt=wdt: en.tensor_copy(accS[fi][:, b, 0:wdt], accp[:, b, 0:wdt]),
                                      r=[b_acc], w=[b_accS[fi]])
                            yield
                            rcq = rc[fi]

                            def A(a):
                                return accS[fi][:, a // 3, (a % 3) * 129:(a % 3) * 129 + 129]
                            for qs in range(4):
                                A1, A2 = A(qs * 2), A(qs * 2 + 1)
                                cx.op("dve", lambda en: en.reciprocal(rcq[:, 0:1], A1[:, 128:129]), r=[b_accS[fi], b_rc[fi]], w=[b_rc[fi]])
                                cx.op("dve", lambda en: en.reciprocal(rcq[:, 1:2], A2[:, 128:129]), r=[b_accS[fi], b_rc[fi]], w=[b_rc[fi]])
                                cx.op("dve", lambda en: en.tensor_tensor(rcq[:, 2:3], rcq[:, 1:2], NEG_LAM, ALU.mult),
                                      r=[b_rc[fi], b_l], w=[b_rc[fi]])
                                cx.op("dve", lambda en: en.tensor_scalar(t1[fi][:], A1[:, 0:128], rcq[:, 0:1], None, ALU.mult),
                                      r=[b_accS[fi], b_rc[fi]], w=[b_t1[fi]])
                                cx.op("dve", lambda en: en.scalar_tensor_tensor(ot[fi][:, qs, :], A2[:, 0:128], rcq[:, 2:3], t1[fi][:],
                                                                                ALU.mult, ALU.add),
                                      r=[b_accS[fi], b_rc[fi], b_t1[fi]], w=[b_o[fi]])
                                cx.op("dve", lambda en: en.scalar_tensor_tensor(djunk[:], ot[fi][:, qs, :], 1.0, ot[fi][:, qs, :],
                                                                                ALU.mult, ALU.mult, accum_out=rcq[:, 8 + qs:9 + qs]),
                                      r=[b_o[fi], b_rc[fi]], w=[b_dj, b_rc[fi]])
                                if qs == 1:
                                    yield
                            yield
                            cx.op("act", lambda en: en.activation(out=rcq[:, 12:16], in_=rcq[:, 8:12], func=AF.Sqrt,
                                                                  scale=1.0 / 128, bias=EPS),
                                  r=[b_rc[fi]], w=[b_rc[fi]])
                            yield
                            cx.op("dve", lambda en: en.reciprocal(rcq[:, 16:20], rcq[:, 12:16]), r=[b_rc[fi]], w=[b_rc[fi]])
                            for qs in range(4):
                                cx.op("dve", lambda en, qs=qs: en.tensor_scalar(on[fi][:, qs, :], ot[fi][:, qs, :], rcq[:, 16 + qs:17 + qs],
                                                                               None, ALU.mult),
                                      r=[b_o[fi], b_rc[fi]], w=[b_on[fi]])
                            yield
                            for qs in range(4):
                                cx.op("pe", lambda en, qs=qs: en.transpose(trp2[:, qs * 128:(qs + 1) * 128], on[fi][:, qs, :], ident16[:]),
                                      r=[b_on[fi], b_c2], w=[b_trp], inc=(qs == 3))
                            yield
                            cx.op("dve", lambda en: en.tensor_copy(ost[fi][:], trp2[:, 0:512]), r=[b_trp], w=[b_ost[fi]])
                            cx.dma("pool", SC["odT"][h][:, qsl], ost[fi][:], r=[b_ost[fi]])

                        def fin_step(drain=False):
                            g = fin["gen"]
                            while g is not None:
                                try:
                                    next(g)
                                except StopIteration:
                                    fin["gen"] = None
                                    return
                                if not drain:
                                    return

                        steps = [(h, qb, kt) for h in range(8) for qb in range(nqb) for kt in range(nkt)]
                        NS = len(steps)

                        def qk(n):
                            (h, qb, kt) = steps[n]
                            hi = h % 2
                            sl_ = n % 2
                            es_ = n % 3
                            qsl = slice(qb * 512, (qb + 1) * 512)
                            for c in range(2):
                                cx.op("pe", lambda en, c=c: en.matmul(
                                    scp[sl_][:, c, :],
                                    kT2[hi][c * 64:(c + 1) * 64, kt * 128:(kt + 1) * 128],
                                    qT2[hi][c * 64:(c + 1) * 64, qsl], start=True, stop=True),
                                    r=[b_qk[hi]], w=[b_sc[sl_]], inc=(c == 1))
                            cx.op("act", lambda en: en.activation(out=Et[es_][:], in_=scp[sl_][:], func=AF.Exp, scale=0.125),
                                  r=[b_sc[sl_]], w=[b_E[es_]])

                        def pv(n):
                            (h, qb, kt) = steps[n]
                            hi = h % 2
                            es_ = n % 3
                            for qs in range(4):
                                for c in range(2):
                                    a = qs * 2 + c
                                    cx.op("pe", lambda en, a=a, c=c, qs=qs: en.matmul(
                                        accv(a), Et[es_][:, c, qs * 128:(qs + 1) * 128], vp2[hi][:, kt, :],
                                        start=(kt == 0 and a % 3 == 0), stop=(kt == nkt - 1), skip_group_check=True),
                                        r=[b_E[es_], b_v[hi]], w=[b_acc], inc=(a == 7))

                        dload(0)
                        qk(0)
                        if NS > 1:
                            qk(1)
                        for n in range(NS):
                            (h, qb, kt) = steps[n]
                            if qb == 0 and kt == 0 and h + 1 < 8:
                                dload(h + 1)
                            if n + 2 < NS:
                                qk(n + 2)
                            pv(n)
                            if kt == nkt - 1:
                                fin_step(drain=True)
                                fi = fin["i"] % 2
                                fin["i"] += 1
                                fin["gen"] = finalize(fi, h, slice(qb * 512, (qb + 1) * 512))
                                fin_step()
                            elif kt >= 1:
                                fin_step()
                        fin_step(drain=True)
                        cx.barrier()

        if stop_after >= 5:
            with contextlib.ExitStack() as ph:
                P = rowlocal_pools(ph)
                xts = [sb(ph, "xt%d" % i, [128, 4, D], F32) for i in range(2)]
                xts_b = [Buf("xt%d" % i) for i in range(2)]
                xn = sb(ph, "xn", [128, 4, D], BF16)
                xn_b = [Buf("xn%d" % j_) for j_ in range(4)]
                ox = sb(ph, "ox", [128, 4, D], BF16)
                ox_b = [Buf("ox%d" % j_) for j_ in range(4)]
                hT = sb(ph, "hT", [128, 22, T], BF16)
                hT_b = Buf("hT")
                sg = [sb(ph, "sg%d" % i, [128, T], BF16) for i in range(2)]
                sg_b = [Buf("sg0"), Buf("sg1")]
                st = sb(ph, "st", [128, 16], F32)
                st_b = Buf("st")
                junk = sb(ph, "junk", [128, D], BF16)
                junk_b = Buf("junk")
                fmr = [sb(ph, "fmr%d" % i, [128, 8, T], BF16) for i in range(3)]
                fmr_b = [Buf("fmr%d" % i) for i in range(3)]
                gen = [sb(ph, "gen%d" % i, [128, 8, T], BF16) for i in range(2)]
                gen_b = [Buf("gen%d" % i) for i in range(2)]
                tf = [sb(ph, "tf%d" % i, [128, T], F32) for i in range(4)]
                tf_b = [Buf("tf%d" % i) for i in range(4)]
                ETt = [sb(ph, "ET%d" % i, [128, 2, T], BF16) for i in range(2)]
                ET_b = [Buf("ET0"), Buf("ET1")]
                rcx = sb(ph, "rcx", [128, 16], F32)
                rcx_b = Buf("rcx")
                KxT = [sb(ph, "KxT%d" % s_, [128, 8, NMEM], BF16) for s_ in range(NSEQ)]
                Vx = [sb(ph, "Vx%d" % s_, [128, 2, 4, 257], BF16) for s_ in range(NSEQ)]
                kv_b = [Buf("kv%d" % s_) for s_ in range(NSEQ)]
                cnt3 = {"fm": 0, "tf": 0}

                def tm_to_fm(src, src_b, dst, dst_b, ntok=4):
                    for j in range(ntok):
                        tp, tp_b = tr_next(P)
                        for kc in range(8):
                            cx.op("pe", lambda en, j=j, kc=kc, tp=tp: en.transpose(
                                tp[:, kc * 128:(kc + 1) * 128], src[:, j, kc * 128:(kc + 1) * 128], ident16[:]),
                                r=[src_b[j], b_c2], w=[tp_b], inc=(kc == 7))
                        e = ev_engine(P)
                        cx.op(e, copy_op(e, dst[:, 0:8, j * 128:(j + 1) * 128],
                                         tp[:, 0:1024].rearrange("p (k n) -> p k n", k=8)), r=[tp_b], w=[dst_b])

                def tm_linear_add(Wd, src, src_b, xt, xt_b):
                    for nb in range(2):
                        wv, wv_b = wload(P, Wd, 0, 8, nb * 512, 512)
                        for j in range(4):
                            pz, pz_b = mm_next(P)
                            for kc in range(8):
                                cx.op("pe", lambda en, kc=kc, j=j, pz=pz, wv=wv: en.matmul(
                                    pz[:], src[:, kc, j * 128:(j + 1) * 128], wv[:, kc, :], start=(kc == 0), stop=(kc == 7)),
                                    r=[wv_b, src_b], w=[pz_b], inc=(kc == 7))
                            cx.op("dve", lambda en, j=j, nb=nb, pz=pz: en.tensor_tensor(
                                xt[:, j, nb * 512:(nb + 1) * 512], pz[:], xt[:, j, nb * 512:(nb + 1) * 512], ALU.add),
                                r=[pz_b, xt_b], w=[xt_b])

                def tm_linear_add_norm(Wd, src, src_b, xt, xt_b, dstT, dstT_b):
                    wvs = [wload(P, Wd, 0, 8, nb * 512, 512) for nb in range(2)]
                    pend = []
                    for j in range(4):
                        for nb in range(2):
                            wv, wv_b = wvs[nb]
                            pz, pz_b = mm_next(P)
                            for kc in range(8):
                                cx.op("pe", lambda en, kc=kc, j=j, pz=pz, wv=wv: en.matmul(
                                    pz[:], src[:, kc, j * 128:(j + 1) * 128], wv[:, kc, :], start=(kc == 0), stop=(kc == 7)),
                                    r=[wv_b, src_b], w=[pz_b], inc=(kc == 7))
                            cx.op("dve", lambda en, j=j, nb=nb, pz=pz: en.tensor_tensor(
                                xt[:, j, nb * 512:(nb + 1) * 512], pz[:], xt[:, j, nb * 512:(nb + 1) * 512], ALU.add),
                                r=[pz_b, xt_b], w=[xt_b])
                        cx.op("act", lambda en, j=j: en.activation(out=junk[:], in_=xt[:, j, :], func=AF.Square,
                                                                   accum_out=st[:, j:j + 1]),
                              r=[xt_b], w=[junk_b, st_b])
                        cx.op("act", lambda en, j=j: en.activation(out=st[:, 4 + j:5 + j], in_=st[:, j:j + 1], func=AF.Sqrt,
                                                                   scale=1.0 / D, bias=EPS), r=[st_b], w=[st_b])
                        cx.op("dve", lambda en, j=j: en.reciprocal(st[:, 8 + j:9 + j], st[:, 4 + j:5 + j]), r=[st_b], w=[st_b])
                        xb = xn_b[j]
                        if j % 2 == 0:
                            cx.op("act", lambda en, j=j: en.activation(out=xn[:, j, :], in_=xt[:, j, :], func=AF.Copy,
                                                                       scale=st[:, 8 + j:9 + j]),
                                  r=[xt_b, st_b], w=[xb])
                        else:
                            cx.op("dve", lambda en, j=j: en.tensor_scalar(xn[:, j, :], xt[:, j, :], st[:, 8 + j:9 + j], None,
                                                                          ALU.mult),
                                  r=[xt_b, st_b], w=[xb])
                        for fn in pend:
                            fn()
                        pend = []

                        def tail(j=j, xb=xb):
                            tp, tp_b = tr_next(P)
                            for kc in range(8):
                                cx.op("pe", lambda en, kc=kc: en.transpose(
                                    tp[:, kc * 128:(kc + 1) * 128], xn[:, j, kc * 128:(kc + 1) * 128], ident16[:]),
                                    r=[xb, b_c2], w=[tp_b], inc=(kc == 7))
                            e = ev_engine(P)
                            cx.op(e, copy_op(e, dstT[:, 0:8, j * 128:(j + 1) * 128],
                                             tp[:, 0:1024].rearrange("p (k n) -> p k n", k=8)), r=[tp_b], w=[dstT_b])
                        pend.append(tail)
                    for fn in pend:
                        fn()

                for s_ in range(NSEQ):
                    memt, memt_b = xts[s_ % 2], xts_b[s_ % 2]
                    cx.dma("sp", memt[:, 0:2, :], mem_in[s_].rearrange("(j p) d -> p j d", p=128), w=[memt_b])
                    memT, memT_b = gen[0], gen_b[0]
                    rms_to_fm(P, memt, memt_b, xn, xn_b, memT, memT_b, st, st_b, junk, junk_b, ntok=2)
                    cx.op("dve", lambda en, s_=s_: en.memset(Vx[s_][:, :, :, 256:257], 1.0), w=[kv_b[s_]])
                    for blk in range(2):
                        wv, wv_b = wload(P, WXKV, 0, 8, blk * 512, 512)
                        for c in range(4):
                            ch = blk * 4 + c
                            pz, pz_b = mm_next(P)
                            for kc in range(8):
                                cx.op("pe", lambda en, kc=kc, c=c, pz=pz, wv=wv: en.matmul(
                                    pz[:, 0:NMEM], wv[:, kc, c * 128:(c + 1) * 128], memT[:, kc, 0:NMEM],
                                    start=(kc == 0), stop=(kc == 7)),
                                    r=[wv_b, memT_b], w=[pz_b], inc=(kc == 7))
                            e = ev_engine(P)
                            cx.op(e, copy_op(e, KxT[s_][:, ch, :], pz[:, 0:NMEM]), r=[pz_b], w=[kv_b[s_]])
                    for nb in range(2):
                        wv, wv_b = wload(P, WXKV, 0, 8, D + nb * 512, 512)
                        for mt in range(2):
                            pz, pz_b = mm_next(P)
                            for kc in range(8):
                                cx.op("pe", lambda en, kc=kc, mt=mt, pz=pz, wv=wv: en.matmul(
                                    pz[:], memT[:, kc, mt * 128:(mt + 1) * 128], wv[:, kc, :], start=(kc == 0), stop=(kc == 7)),
                                    r=[wv_b, memT_b], w=[pz_b], inc=(kc == 7))
                            e = ev_engine(P)
                            cx.op(e, copy_op(e, Vx[s_][:, mt, nb * 2:nb * 2 + 2, 0:256],
                                             pz[:].rearrange("p (h d) -> p h d", h=2)), r=[pz_b], w=[kv_b[s_]])

                def load_x1(ti):
                    s_, t0_ = tiles[ti]
                    cx.dma("sp", xts[ti % 2][:], sc[s_]["x1"][t0_:t0_ + T, :].rearrange("(j p) d -> p j d", p=128),
                           w=[xts_b[ti % 2]])

                def fm_load(src, t0_, q="sp"):
                    i = cnt3["fm"] % 3
                    cnt3["fm"] += 1
                    cx.dma(q, fmr[i][:], src[:, :, t0_:t0_ + T].rearrange("c p n -> p c n"), w=[fmr_b[i]])
                    return fmr[i], fmr_b[i]

                def tf_next():
                    i = cnt3["tf"] % 4
                    cnt3["tf"] += 1
                    return tf[i], tf_b[i]

                def prefetch_fm(ti_, q="sp"):
                    s_, t0_ = tiles[ti_]
                    SC_ = sc[s_]
                    hm, hm_b = fm_load(SC_["hmT"], t0_, q)
                    om, om_b = fm_load(SC_["omT"], t0_, q)
                    hg, hg_b = gen[0], gen_b[0]
                    cx.op("dve", lambda en: en.tensor_tensor(hg[:], hm[:], om[:], ALU.mult), r=[hm_b, om_b], w=[hg_b])
                    od, od_b = fm_load(SC_["odT"], t0_, q)
                    ga, ga_b = fm_load(SC_["gaT"], t0_, q)
                    gb, gb_b = fm_load(SC_["gbT"], t0_, q)
                    return (hg, hg_b, od, od_b, ga, ga_b, gb, gb_b)

                load_x1(0)
                pre = prefetch_fm(0)
                for ti, (s, t0) in enumerate(tiles):
                    xt, xt_b = xts[ti % 2], xts_b[ti % 2]
                    SC = sc[s]
                    (hg, hg_b, od, od_b, ga, ga_b, gb, gb_b) = pre
                    mg, mg_b = gen[1], gen_b[1]
                    for blk in range(2):
                        wa, wa_b = wload(P, WA, 0, 8, blk * 512, 512)
                        wb, wb_b = wload(P, WB, 0, 8, blk * 512, 512)
                        for c in range(4):
                            dch = blk * 4 + c
                            pa, pa_b = mm_next(P)
                            for kc in range(8):
                                cx.op("pe", lambda en, kc=kc, c=c, pa=pa, wa=wa: en.matmul(
                                    pa[:], wa[:, kc, c * 128:(c + 1) * 128], hg[:, kc, :], start=(kc == 0), stop=(kc == 7)),
                                    r=[wa_b, hg_b], w=[pa_b], inc=(kc == 7))
                            ta, ta_b = tf_next()
                            cx.op("dve", lambda en, pa=pa, ta=ta, dch=dch: en.tensor_tensor(ta[:], pa[:], ga[:, dch, :], ALU.mult),
                                  r=[pa_b, ga_b], w=[ta_b])
                            pb, pb_b = mm_next(P)
                            for kc in range(8):
                                cx.op("pe", lambda en, kc=kc, c=c, pb=pb, wb=wb: en.matmul(
                                    pb[:], wb[:, kc, c * 128:(c + 1) * 128], od[:, kc, :], start=(kc == 0), stop=(kc == 7)),
                                    r=[wb_b, od_b], w=[pb_b], inc=(kc == 7))
                            tb, tb_b = tf_next()
                            cx.op("dve", lambda en, pb=pb, tb=tb, dch=dch: en.tensor_tensor(tb[:], pb[:], gb[:, dch, :], ALU.mult),
                                  r=[pb_b, gb_b], w=[tb_b])
                            cx.op("dve", lambda en, ta=ta, tb=tb, dch=dch: en.tensor_tensor(mg[:, dch, :], ta[:], tb[:], ALU.add),
                                  r=[ta_b, tb_b], w=[mg_b])
                    xnT, xnT_b = gen[0], gen_b[0]
                    tm_linear_add_norm(WMO, mg, mg_b, xt, xt_b, xnT, xnT_b)
                    qxT, qxT_b = gen[1], gen_b[1]
                    for blk in range(2):
                        wv, wv_b = wload(P, WXQ, 0, 8, blk * 512, 512)
                        for c in range(4):
                            ch = blk * 4 + c
                            pz, pz_b = mm_next(P)
                            for kc in range(8):
                                cx.op("pe", lambda en, kc=kc, c=c, pz=pz, wv=wv: en.matmul(
                                    pz[:], wv[:, kc, c * 128:(c + 1) * 128], xnT[:, kc, :], start=(kc == 0), stop=(kc == 7)),
                                    r=[wv_b, xnT_b], w=[pz_b], inc=(kc == 7))
                            e = ev_engine(P)
                            cx.op(e, copy_op(e, qxT[:, ch, :], pz[:]), r=[pz_b], w=[qxT_b])
                    def xa_scores(hx):
                        eb = hx % 2
                        for mt in range(2):
                            pz, pz_b = mm_next(P)
                            for dc in range(2):
                                cx.op("pe", lambda en, dc=dc, mt=mt, pz=pz: en.matmul(
                                    pz[:], KxT[s][:, hx * 2 + dc, mt * 128:(mt + 1) * 128], qxT[:, hx * 2 + dc, :],
                                    start=(dc == 0), stop=(dc == 1)),
                                    r=[kv_b[s], qxT_b], w=[pz_b], inc=(dc == 1))
                            cx.op("act", lambda en, mt=mt, pz=pz: en.activation(out=ETt[eb][:, mt, :], in_=pz[:], func=AF.Exp,
                                                                               scale=1.0 / 16.0),
                                  r=[pz_b], w=[ET_b[eb]])

                    def xa_out(hx):
                        eb = hx % 2
                        for j in range(4):
                            po, po_b = mm_next(P)
                            for mt in range(2):
                                cx.op("pe", lambda en, mt=mt, j=j, po=po: en.matmul(
                                    po[:, 0:257], ETt[eb][:, mt, j * 128:(j + 1) * 128], Vx[s][:, mt, hx, :],
                                    start=(mt == 0), stop=(mt == 1)),
                                    r=[ET_b[eb], kv_b[s]], w=[po_b], inc=(mt == 1))
                            cx.op("dve", lambda en, po=po, j=j: en.reciprocal(rcx[:, hx * 4 + j:hx * 4 + j + 1], po[:, 256:257]),
                                  r=[po_b], w=[rcx_b])
                            cx.op("act", lambda en, po=po, j=j: en.activation(
                                out=ox[:, j, hx * 256:(hx + 1) * 256], in_=po[:, 0:256], func=AF.Copy,
                                scale=rcx[:, hx * 4 + j:hx * 4 + j + 1]),
                                r=[po_b, rcx_b], w=[ox_b[j]])

                    xa_scores(0)
                    for hx in range(4):
                        if hx + 1 < 4:
                            xa_scores(hx + 1)
                        xa_out(hx)
                    oxT, oxT_b = gen[0], gen_b[0]
                    tm_to_fm(ox, ox_b, oxT, oxT_b)
                    xnT2, xnT2_b = gen[1], gen_b[1]
                    tm_linear_add_norm(WXO, oxT, oxT_b, xt, xt_b, xnT2, xnT2_b)
                    if ti + 1 < len(tiles):
                        load_x1(ti + 1)
                        pre = prefetch_fm(ti + 1, q="act")
                    ffn(P, W2, W2o, xnT2, xnT2_b, hT, hT_b, sg, sg_b, xt, xt_b)
                    for j in range(4):
                        cx.op("act", lambda en, j=j: en.activation(out=junk[:], in_=xt[:, j, :], func=AF.Square,
                                                                   accum_out=st[:, j:j + 1]),
                              r=[xt_b], w=[junk_b, st_b])
                    cx.op("act", lambda en: en.activation(out=st[:, 4:8], in_=st[:, 0:4], func=AF.Sqrt, scale=1.0 / D, bias=EPS),
                          r=[st_b], w=[st_b])
                    cx.op("dve", lambda en: en.reciprocal(st[:, 8:12], st[:, 4:8]), r=[st_b], w=[st_b])
                    for j in range(4):
                        cx.op("dve", lambda en, j=j: en.scalar_tensor_tensor(
                            xt[:, j, :], xt[:, j, :], st[:, 8 + j:9 + j], fgain[:], ALU.mult, ALU.mult),
                            r=[xt_b, st_b, b_const], w=[xt_b])
                    cx.dma("pool", y_out[s][t0:t0 + T, :].rearrange("(j p) d -> p j d", p=128), xt[:], r=[xt_b])
                cx.barrier()
        cx.barrier()
    return nc


def _consts():
    c = {}
    c["c_ident"] = np.eye(128, dtype=np.float32)
    j = np.arange(128)[:, None]
    i = np.arange(128)[None, :]
    c["c_masks"] = np.concatenate([(j <= i), (j >= i)], axis=1).astype(np.float32)
    inv_freq = (np.float32(500000.0) ** (-(np.arange(0, 16, 2, dtype=np.float32) / np.float32(16)))).astype(np.float32)
    rope = np.zeros((128, 2), np.float32)
    for p in range(128):
        f = p % 64
        if f < 16:
            rope[p, 0] = inv_freq[f % 8]
    c["c_rope"] = rope
    c["c_pos"] = np.tile(np.arange(T, dtype=np.float32)[None, :], (128, 1))
    ab = np.zeros((8, 2), np.float32)
    ab[0:4, 0] = 1.0
    ab[4:8, 0] = -1.0
    ab[4:8, 1] = 1.0
    c["c_ab"] = ab
    rot = np.zeros((128, 128), np.float32)
    for b in (0, 64):
        for f in range(8):
            rot[b + f + 8, b + f] = -1.0
            rot[b + f, b + f + 8] = 1.0
    c["c_rot"] = rot
    return c


def make_in_maps(inputs, S_LIST, n_cores, seq_of_core):
    f = lambda a: np.ascontiguousarray(np.asarray(a, dtype=np.float32))
    gains = [inputs[k] for k in ("ffn1_norm", "mix_norm", "mlstm_norm", "diff_norm", "xattn_norm", "mem_norm", "ffn2_norm")]
    gcols = np.concatenate([f(g).reshape(8, 128).T for g in gains], axis=1)
    shared = {
        "ffn1_w_in": f(inputs["ffn1_w_in"])[0], "ffn1_w_out": f(inputs["ffn1_w_out"])[0],
        "w_mix_in": f(inputs["w_mix_in"])[0], "w_branch_a": f(inputs["w_branch_a"])[0],
        "w_branch_b": f(inputs["w_branch_b"])[0], "w_mix_out": f(inputs["w_mix_out"])[0],
        "w_xq": f(inputs["w_xq"])[0], "w_xkv": f(inputs["w_xkv"])[0], "w_xo": f(inputs["w_xo"])[0],
        "ffn2_w_in": f(inputs["ffn2_w_in"])[0], "ffn2_w_out": f(inputs["ffn2_w_out"])[0],
        "gcols": np.ascontiguousarray(gcols),
        "final_norm": f(inputs["final_norm"]).reshape(D),
        "lam_vecs": np.concatenate([f(inputs[k]).reshape(64) for k in ("lambda_q1", "lambda_k1", "lambda_q2", "lambda_k2")]),
        "gate_bias": np.ascontiguousarray(np.stack([f(inputs["b_igate"]).reshape(8), f(inputs["b_fgate"]).reshape(8)], axis=1)),
    }
    shared.update(_consts())
    maps = []
    for c in range(n_cores):
        m = dict(shared)
        xs, ms = seq_of_core(c)
        for s in range(len(S_LIST)):
            m["x%d" % s] = np.ascontiguousarray(xs[s])
            m["mem%d" % s] = np.ascontiguousarray(ms[s])
        maps.append(m)
    return maps


def kernel(**inputs):
    S_LIST = (2048, 8192)
    nc = build(S_LIST)
    xp, xs_ = np.asarray(inputs["x_prompt"]), np.asarray(inputs["x_sample"])
    mp, ms_ = np.asarray(inputs["mem_prompt"]), np.asarray(inputs["mem_sample"])
    maps = make_in_maps(inputs, S_LIST, 8, lambda c: ((xp[c], xs_[c]), (mp[c], ms_[c])))
    res = run_bass_kernel_spmd(nc, maps, core_ids=list(range(8)))
    y_p = np.stack([res.results[c]["y0"] for c in range(8)], axis=0).astype(np.float32)
    y_s = np.stack([res.results[c]["y1"] for c in range(8)], axis=0).astype(np.float32)
    return (y_p, y_s)
```

```python
import contextlib
import math
import numpy as np
import concourse.bass as bass
import concourse.mybir as mybir
from concourse.bass_utils import run_bass_kernel_spmd

F32 = mybir.dt.float32
BF16 = mybir.dt.bfloat16
I32 = mybir.dt.int32
ALU = mybir.AluOpType
AF = mybir.ActivationFunctionType
AX = mybir.AxisListType

D = 1024
DFF = 2816
NMEM = 256
EPS = 1e-6
LAMBDA_INIT = 0.8 - 0.6 * math.exp(-0.3 * 0)
T = 512
NWS = 6
O_QM, O_KM, O_VM, O_OM, O_IG, O_FG, O_QD, O_KD, O_VD, O_GA, O_GB = (
    0, 512, 1024, 2048, 3072, 3080, 3088, 4112, 5136, 6160, 7184)
NMIX = 8208
C_QM, C_KM, C_QD, C_KD, C_GA, C_GB, C_OM = 0, 4, 8, 16, 24, 32, 40
NFC = 48
NF = NFC * 128 + 16
NT_ = 2560
TWO_PI = 2.0 * math.pi
CW1 = 6.28125
CW2 = TWO_PI - CW1


class Buf:
    __slots__ = ("name", "w", "r")

    def __init__(self, name):
        self.name = name
        self.w = {}
        self.r = {}


class Ctx:
    def __init__(self, nc, es):
        self.nc = nc
        self.E = {"pe": nc.tensor, "act": nc.scalar, "dve": nc.vector, "pool": nc.gpsimd, "sp": nc.sync}
        self.semh = {}
        self.cnt = {}
        self.known = {}
        for e in self.E:
            self.semh[e] = es.enter_context(nc.semaphore("s_" + e))
            self.cnt[e] = 0
            self.known[e] = {}
        self.dq = {}
        for q, n in (("sp", 16), ("pool", 16), ("act", 6)):
            keys = []
            for i in range(n):
                k = (q, i)
                self.semh[k] = es.enter_context(nc.semaphore("d_%s%d" % (q, i)))
                self.cnt[k] = 0
                keys.append(k)
            self.dq[q] = [keys, 0]

    def _waits(self, e, need):
        eng = self.E[e]
        kn = self.known[e]
        for k, v in need.items():
            if k == e:
                if e == "pe":
                    continue
                v = min(v, self.cnt[e])
            if kn.get(k, 0) < v:
                eng.wait_ge(self.semh[k], v)
                kn[k] = v

    @staticmethod
    def _need(r, w):
        need = {}
        for b in r:
            for k, v in b.w.items():
                if need.get(k, 0) < v:
                    need[k] = v
        for b in w:
            for k, v in b.w.items():
                if need.get(k, 0) < v:
                    need[k] = v
            for k, v in b.r.items():
                if need.get(k, 0) < v:
                    need[k] = v
        return need

    def op(self, e, fn, r=(), w=(), inc=True):
        self._waits(e, self._need(r, w))
        ins = fn(self.E[e])
        if inc:
            self.cnt[e] += 1
            ins.then_inc(self.semh[e], 1)
            stamp = self.cnt[e]
        else:
            stamp = self.cnt[e] + 1
        for b in w:
            b.w = {e: stamp}
            b.r = {}
        for b in r:
            if b.r.get(e, 0) < stamp:
                b.r[e] = stamp
        return ins

    def dma(self, q, out, in_, r=(), w=(), merge=False, **kw):
        need = self._need(r, w)
        keys, i = self.dq[q]
        k = keys[i % len(keys)]
        self.dq[q][1] = i + 1
        if need.get(k, 0) < self.cnt[k]:
            need[k] = self.cnt[k]
        self._waits(q, need)
        ins = self.E[q].dma_start(out=out, in_=in_, **kw)
        ins.then_inc(self.semh[k], 16)
        self.cnt[k] += 16
        v = self.cnt[k]
        for b in w:
            if merge:
                b.w[k] = v
            else:
                b.w = {k: v}
                b.r = {}
        for b in r:
            if b.r.get(k, 0) < v:
                b.r[k] = v
        return ins

    def barrier(self, engines=None):
        for e in (engines or self.E):
            need = {k: v for k, v in self.cnt.items() if k != e and v > 0}
            self._waits(e, need)


def build(S_LIST=(2048, 8192), debug=False, stop_after=99):
    nc = bass.Bass("TRN2", target_bir_lowering=False)
    NSEQ = len(S_LIST)
    dbg_kind = "ExternalOutput" if debug else "Internal"

    def din(name, shape, dt=F32):
        return nc.dram_tensor(name, list(shape), dt, kind="ExternalInput").ap()

    def dscr(name, shape, dt=BF16, dbg=False):
        return nc.dram_tensor(name, list(shape), dt, kind=(dbg_kind if dbg else "Internal")).ap()

    x_in = [din("x%d" % s, [S_LIST[s], D]) for s in range(NSEQ)]
    mem_in = [din("mem%d" % s, [NMEM, D]) for s in range(NSEQ)]
    y_out = [nc.dram_tensor("y%d" % s, [S_LIST[s], D], F32, kind="ExternalOutput").ap() for s in range(NSEQ)]
    w_ffn1_in = din("ffn1_w_in", [D, 2 * DFF])
    w_ffn1_out = din("ffn1_w_out", [DFF, D])
    w_mix_in = din("w_mix_in", [D, NMIX])
    w_br_a = din("w_branch_a", [D, D])
    w_br_b = din("w_branch_b", [D, D])
    w_mix_out = din("w_mix_out", [D, D])
    w_xq = din("w_xq", [D, D])
    w_xkv = din("w_xkv", [D, 2 * D])
    w_xo = din("w_xo", [D, D])
    w_ffn2_in = din("ffn2_w_in", [D, 2 * DFF])
    w_ffn2_out = din("ffn2_w_out", [DFF, D])
    gcols_in = din("gcols", [128, 7 * 8])
    final_gain = din("final_norm", [D])
    lam_in = din("lam_vecs", [4 * 64])
    gate_bias_in = din("gate_bias", [8, 2])
    c_ident = din("c_ident", [128, 128])
    c_masks = din("c_masks", [128, 256])
    c_rope = din("c_rope", [128, 2])
    c_pos = din("c_pos", [128, T])
    c_ab = din("c_ab", [8, 2])
    c_rot = din("c_rot", [128, 128])

    W1 = dscr("W1", [D, 2 * DFF])
    W1o = dscr("W1o", [DFF, D])
    W2 = dscr("W2", [D, 2 * DFF])
    W2o = dscr("W2o", [DFF, D])
    WF = dscr("WF", [D, NF], dbg=True)
    WT = dscr("WT", [D, NT_], dbg=True)
    WA = dscr("WA", [D, D])
    WB = dscr("WB", [D, D])
    WMO = dscr("WMO", [D, D])
    WXQ = dscr("WXQ", [D, D])
    WXKV = dscr("WXKV", [D, 2 * D])
    WXO = dscr("WXO", [D, D])
    sc = []
    for s in range(NSEQ):
        S = S_LIST[s]
        nch = S // 128
        sc.append(dict(
            x1=dscr("x1_%d" % s, [S, D], F32, dbg=True),
            qmT=dscr("qmT_%d" % s, [4, 128, S], dbg=True),
            kmT=dscr("kmT_%d" % s, [4, 128, S], dbg=True),
            qdT=dscr("qdT_%d" % s, [8, 128, S], dbg=True),
            kdT=dscr("kdT_%d" % s, [8, 128, S], dbg=True),
            gaT=dscr("gaT_%d" % s, [8, 128, S], dbg=True),
            gbT=dscr("gbT_%d" % s, [8, 128, S], dbg=True),
            omT=dscr("omT_%d" % s, [8, 128, S], dbg=True),
            gig=dscr("gig_%d" % s, [8, S], F32, dbg=True),
            gfg=dscr("gfg_%d" % s, [8, S], F32, dbg=True),
            km=dscr("km_%d" % s, [S, 512], dbg=True),
            vm=dscr("vm_%d" % s, [S, 1024], dbg=True),
            vd=dscr("vd_%d" % s, [S, 1024], dbg=True),
            dec=dscr("dec_%d" % s, [8 * nch], F32, dbg=True),
            hmT=dscr("hmT_%d" % s, [8, 128, S], dbg=True),
            odT=dscr("odT_%d" % s, [8, 128, S], dbg=True),
        ))

    top = contextlib.ExitStack()
    with top:
        cx = Ctx(nc, top)

        uid = {"n": 0}

        def sb(es, name, shape, dt):
            uid["n"] += 1
            return es.enter_context(nc.sbuf_tensor("sb%d_%s" % (uid["n"], name), list(shape), dt))

        def pst(es, name, shape, dt):
            uid["n"] += 1
            return es.enter_context(nc.psum_tensor("ps%d_%s" % (uid["n"], name), list(shape), dt))

        ident = sb(top, "ident", [128, 128], F32)
        ident16 = sb(top, "ident16", [128, 128], BF16)
        rot32 = sb(top, "rot32", [128, 128], F32)
        rot16 = sb(top, "rot16", [128, 128], BF16)
        masks = sb(top, "masks", [128, 256], F32)
        gcols = sb(top, "gcols", [128, 56], F32)
        ropec = sb(top, "ropec", [128, 2], F32)
        posrow = sb(top, "posrow", [128, T], F32)
        abc = sb(top, "abc", [8, 2], F32)
        gbias = sb(top, "gbias", [8, 2], F32)
        lamv = sb(top, "lamv", [128, 256], F32)
        lamc = sb(top, "lamc", [128, 4], F32)
        fgain = sb(top, "fgain", [128, D], F32)
        b_const = Buf("const")
        for dst, src in ((ident, c_ident), (masks, c_masks), (gcols, gcols_in), (ropec, c_rope),
                         (posrow, c_pos), (abc, c_ab), (gbias, gate_bias_in), (rot32, c_rot)):
            cx.dma("sp", dst[:], src[:, :], w=[b_const], merge=True)
        cx.dma("sp", lamv[:], lam_in.partition_broadcast(128), w=[b_const], merge=True)
        cx.dma("sp", fgain[:], final_gain.partition_broadcast(128), w=[b_const], merge=True)
        b_c2 = Buf("const2")
        cx.op("dve", lambda e: e.tensor_copy(ident16[:], ident[:]), r=[b_const], w=[b_c2])
        cx.op("dve", lambda e: e.tensor_copy(rot16[:], rot32[:]), r=[b_const, b_c2], w=[b_c2])
        b_l = Buf("lam")
        cx.op("dve", lambda e: e.tensor_tensor(lamv[:, 0:64], lamv[:, 0:64], lamv[:, 64:128], ALU.mult),
              r=[b_const], w=[b_l])
        cx.op("dve", lambda e: e.tensor_tensor(lamv[:, 128:192], lamv[:, 128:192], lamv[:, 192:256], ALU.mult),
              r=[b_l], w=[b_l])
        cx.op("dve", lambda e: e.tensor_reduce(lamc[:, 0:1], lamv[:, 0:64], AX.X, ALU.add), r=[b_l], w=[b_l])
        cx.op("dve", lambda e: e.tensor_reduce(lamc[:, 1:2], lamv[:, 128:192], AX.X, ALU.add), r=[b_l], w=[b_l])
        cx.op("act", lambda e: e.activation(out=lamc[:, 0:2], in_=lamc[:, 0:2], func=AF.Exp), r=[b_l], w=[b_l])
        cx.op("dve", lambda e: e.tensor_tensor(lamc[:, 2:3], lamc[:, 0:1], lamc[:, 1:2], ALU.subtract),
              r=[b_l], w=[b_l])
        cx.op("dve", lambda e: e.tensor_scalar(lamc[:, 3:4], lamc[:, 2:3], LAMBDA_INIT, -1.0, ALU.add, ALU.mult),
              r=[b_l], w=[b_l])
        NEG_LAM = lamc[:, 3:4]

        with contextlib.ExitStack() as ph:
            CW = 2048
            wl = [sb(ph, "wl%d" % i, [128, 4112], F32) for i in range(3)]
            wo = [sb(ph, "wo%d" % i, [128, 5136], BF16) for i in range(3)]
            wl_b = [Buf("wl%d" % i) for i in range(3)]
            wo_b = [Buf("wo%d" % i) for i in range(3)]
            state = {"i": 0, "e": 0}

            def conv_engine():
                state["e"] += 1
                return "act" if state["e"] % 2 else "dve"

            def scale_op(dst_ap, src_ap, gcol, const, rb, wb, eng=None):
                e = eng or conv_engine()
                if gcol is None:
                    if e == "act":
                        cx.op("act", lambda en: en.activation(out=dst_ap, in_=src_ap, func=AF.Copy, scale=float(const)),
                              r=rb, w=wb)
                    else:
                        cx.op("dve", lambda en: en.tensor_scalar(dst_ap, src_ap, float(const), None, ALU.mult),
                              r=rb, w=wb)
                elif const == 1.0 and e == "act":
                    cx.op("act", lambda en: en.activation(out=dst_ap, in_=src_ap, func=AF.Copy, scale=gcol),
                          r=rb, w=wb)
                else:
                    cx.op("dve", lambda en: en.tensor_scalar(dst_ap, src_ap, gcol, float(const), ALU.mult, ALU.mult),
                          r=rb, w=wb)

            def convert_plain(src, dst, K, N, gidx=None, const=1.0):
                for kc in range(K // 128):
                    for c0 in range(0, N, CW):
                        cw = min(CW, N - c0)
                        i = state["i"] % 3
                        state["i"] += 1
                        cx.dma("sp", wl[i][:, 0:cw], src[kc * 128:(kc + 1) * 128, c0:c0 + cw], w=[wl_b[i]])
                        g = None if gidx is None else gcols[:, gidx * 8 + kc:gidx * 8 + kc + 1]
                        scale_op(wo[i][:, 0:cw], wl[i][:, 0:cw], g, const, [wl_b[i], b_const], [wo_b[i]])
                        cx.dma("pool", dst[kc * 128:(kc + 1) * 128, c0:c0 + cw], wo[i][:, 0:cw], r=[wo_b[i]])

            convert_plain(w_ffn1_in, W1, D, 2 * DFF, gidx=0)
            convert_plain(w_ffn1_out, W1o, DFF, D)
            for kc in range(8):
                g = gcols[:, 8 + kc:8 + kc + 1]
                halves = []
                for hf in range(2):
                    i = state["i"] % 3
                    state["i"] += 1
                    c0 = hf * 4112
                    cwid = 4112 if hf == 0 else NMIX - 4112
                    cx.dma("sp", wl[i][:, 0:cwid], w_mix_in[kc * 128:(kc + 1) * 128, c0:c0 + cwid], w=[wl_b[i]])
                    halves.append(i)
                ia, ib = halves
                io = state["i"] % 3
                A_, B_ = wl[ia], wl[ib]
                rA, rB = [wl_b[ia], b_const], [wl_b[ib], b_const]
                OF1, OF2, OT = wo[0], wo[1], wo[2]
                bF1, bF2, bT = wo_b[0], wo_b[1], wo_b[2]
                kscale = 128.0 ** -0.5
                SPL = 3072
                scale_op(OF1[:, 0:512], A_[:, O_QM:O_QM + 512], g, 1.0, rA, [bF1])
                scale_op(OF1[:, 512:1024], A_[:, O_KM:O_KM + 512], g, kscale, rA, [bF1])
                scale_op(OF1[:, C_QD * 128:C_QD * 128 + 1024], A_[:, O_QD:O_QD + 1024], g, 1.0, rA, [bF1])
                scale_op(OF1[:, C_KD * 128:C_KD * 128 + 1024], B_[:, O_KD - 4112:O_KD - 4112 + 1024], g, 1.0, rB, [bF1])
                scale_op(OF2[:, C_GA * 128 - SPL:C_GA * 128 - SPL + 1024], B_[:, O_GA - 4112:O_GA - 4112 + 1024],
                         g, 1.0, rB, [bF2])
                scale_op(OF2[:, C_GB * 128 - SPL:C_GB * 128 - SPL + 1024], B_[:, O_GB - 4112:O_GB - 4112 + 1024],
                         g, 1.0, rB, [bF2])
                scale_op(OF2[:, C_OM * 128 - SPL:C_OM * 128 - SPL + 1024], A_[:, O_OM:O_OM + 1024], g, 1.0, rA, [bF2])
                scale_op(OF2[:, NFC * 128 - SPL:NFC * 128 + 16 - SPL], A_[:, O_IG:O_IG + 16], g, 1.0, rA, [bF2], eng="dve")
                scale_op(OT[:, 0:512], A_[:, O_KM:O_KM + 512], g, kscale, rA, [bT])
                scale_op(OT[:, 512:1536], A_[:, O_VM:O_VM + 1024], g, 1.0, rA, [bT])
                scale_op(OT[:, 1536:2560], B_[:, O_VD - 4112:O_VD - 4112 + 1024], g, 1.0, rB, [bT])
                rows = slice(kc * 128, (kc + 1) * 128)
                cx.dma("pool", WF[rows, 0:SPL], OF1[:, 0:SPL], r=[bF1])
                cx.dma("pool", WF[rows, SPL:NF], OF2[:, 0:NF - SPL], r=[bF2])
                cx.dma("pool", WT[rows, :], OT[:, 0:NT_], r=[bT])
            cx.barrier()

        BG_LIST = [(w_br_a, WA, D, D, 2, 1.0), (w_br_b, WB, D, D, 3, 1.0 - LAMBDA_INIT), (w_mix_out, WMO, D, D, None, 1.0),
                   (w_xq, WXQ, D, D, 4, 1.0), (w_xkv, WXKV, D, 2 * D, 5, 1.0), (w_xo, WXO, D, D, None, 1.0),
                   (w_ffn2_in, W2, D, 2 * DFF, 6, 1.0), (w_ffn2_out, W2o, DFF, D, None, 1.0)]

        def bg_convert(bwl, bwl_b, bwo, bwo_b):
            NB = len(bwl)
            pieces = []
            for (src, dst, K, N, gidx, const) in BG_LIST:
                for kc in range(K // 128):
                    for c0 in range(0, N, 512):
                        pieces.append((src, dst, kc, c0, min(512, N - c0), gidx, const))
            npc = len(pieces)
            for n in range(npc + 4):
                if n < npc:
                    (src, dst, kc, c0, cw, gidx, const) = pieces[n]
                    i = n % NB
                    cx.dma("sp", bwl[i][:, 0:cw], src[kc * 128:(kc + 1) * 128, c0:c0 + cw], w=[bwl_b[i]])
                m = n - 2
                if 0 <= m < npc:
                    (src, dst, kc, c0, cw, gidx, const) = pieces[m]
                    i = m % NB
                    g = None if gidx is None else gcols[:, gidx * 8 + kc:gidx * 8 + kc + 1]
                    scale_op(bwo[i][:, 0:cw], bwl[i][:, 0:cw], g, const, [bwl_b[i], b_const], [bwo_b[i]])
                m = n - 4
                if 0 <= m < npc:
                    (src, dst, kc, c0, cw, gidx, const) = pieces[m]
                    i = m % NB
                    cx.dma("pool", dst[kc * 128:(kc + 1) * 128, c0:c0 + cw], bwo[i][:, 0:cw], r=[bwo_b[i]])
                yield

        tiles = [(s, t0) for s in range(NSEQ) for t0 in range(0, S_LIST[s], T)]

        def rowlocal_pools(ph):
            P = {}
            P["ws"] = [sb(ph, "ws%d" % i, [128, 8, 512], BF16) for i in range(NWS)]
            P["ws_b"] = [Buf("ws%d" % i) for i in range(NWS)]
            P["wsi"] = 0
            P["mm"] = [pst(ph, "mm%d" % i, [128, 512], F32) for i in range(6)]
            P["mm_b"] = [Buf("mm%d" % i) for i in range(6)]
            P["mmi"] = 0
            trp = [pst(ph, "trp%d" % i, [128, 1024], BF16) for i in range(2)]
            P["tr"] = [trp[0][:, :], trp[1][:, :]]
            P["tr_b"] = [Buf("tr0"), Buf("tr1")]
            P["tri"] = 0
            P["ev"] = 0
            return P

        def wload(P, Wd, k0, nk, c0, ncol):
            i = P["wsi"] % NWS
            P["wsi"] += 1
            cx.dma("sp", P["ws"][i][:, 0:nk, 0:ncol],
                   Wd[k0 * 128:(k0 + nk) * 128, c0:c0 + ncol].rearrange("(k p) n -> p k n", p=128),
                   w=[P["ws_b"][i]])
            return P["ws"][i], P["ws_b"][i]

        def mm_next(P):
            i = P["mmi"] % 6
            P["mmi"] += 1
            return P["mm"][i], P["mm_b"][i]

        def tr_next(P):
            i = P["tri"] % 2
            P["tri"] += 1
            return P["tr"][i], P["tr_b"][i]

        def ev_engine(P):
            P["ev"] += 1
            return "act" if P["ev"] % 2 else "dve"

        def copy_op(e, out, in_):
            if e == "act":
                return lambda en: en.activation(out=out, in_=in_, func=AF.Copy)
            return lambda en: en.tensor_copy(out, in_)

        def rms_to_fm(P, xt, xt_b, xn, xn_b, xnT, xnT_b, st, st_b, junk, junk_b, ntok=4, width=D):
            for j in range(ntok):
                cx.op("act", lambda en, j=j: en.activation(out=junk[:, 0:width], in_=xt[:, j, :], func=AF.Square,
                                                           accum_out=st[:, j:j + 1]),
                      r=[xt_b[j]], w=[junk_b, st_b])
            cx.op("act", lambda en: en.activation(out=st[:, 4:4 + ntok], in_=st[:, 0:ntok], func=AF.Sqrt,
                                                  scale=1.0 / width, bias=EPS), r=[st_b], w=[st_b])
            cx.op("dve", lambda en: en.reciprocal(st[:, 8:8 + ntok], st[:, 4:4 + ntok]), r=[st_b], w=[st_b])
            nkc = width // 128
            for j in range(ntok):
                xb = xn_b[j]
                if j % 2 == 0:
                    cx.op("act", lambda en, j=j: en.activation(out=xn[:, j, :], in_=xt[:, j, :], func=AF.Copy,
                                                               scale=st[:, 8 + j:9 + j]),
                          r=[xt_b[j], st_b], w=[xb])
                else:
                    cx.op("dve", lambda en, j=j: en.tensor_scalar(xn[:, j, :], xt[:, j, :], st[:, 8 + j:9 + j], None,
                                                                  ALU.mult),
                          r=[xt_b[j], st_b], w=[xb])
                tp, tp_b = tr_next(P)
                for kc in range(nkc):
                    cx.op("pe", lambda en, j=j, kc=kc, tp=tp: en.transpose(
                        tp[:, kc * 128:(kc + 1) * 128], xn[:, j, kc * 128:(kc + 1) * 128], ident16[:]),
                        r=[xb, b_c2], w=[tp_b], inc=(kc == nkc - 1))
                e = ev_engine(P)
                cx.op(e, copy_op(e, xnT[:, 0:nkc, j * 128:(j + 1) * 128],
                                 tp[:, 0:nkc * 128].rearrange("p (k n) -> p k n", k=nkc)), r=[tp_b], w=[xnT_b])

        def ffn(P, Win, Wout, xnT, xnT_b, hT, hT_b, sg, sg_b, xt, xt_b, mid=None, step=None):
            for blk in range(6):
                if step is not None:
                    step()
                nchk = 4 if blk < 5 else 2
                wg, wg_b = wload(P, Win, 0, 8, blk * 512, nchk * 128)
                wu, wu_b = wload(P, Win, 0, 8, DFF + blk * 512, nchk * 128)
                for c in range(nchk):
                    fc = blk * 4 + c
                    pg, pg_b = mm_next(P)
                    for kc in range(8):
                        cx.op("pe", lambda en, kc=kc, c=c, pg=pg, wg=wg: en.matmul(
                            pg[:], wg[:, kc, c * 128:(c + 1) * 128], xnT[:, kc, :], start=(kc == 0), stop=(kc == 7)),
                            r=[wg_b, xnT_b], w=[pg_b], inc=(kc == 7))
                    si = fc % 2
                    cx.op("act", lambda en, pg=pg, si=si: en.activation(out=sg[si][:], in_=pg[:], func=AF.Silu),
                          r=[pg_b], w=[sg_b[si]])
                    pu, pu_b = mm_next(P)
                    for kc in range(8):
                        cx.op("pe", lambda en, kc=kc, c=c, pu=pu, wu=wu: en.matmul(
                            pu[:], wu[:, kc, c * 128:(c + 1) * 128], xnT[:, kc, :], start=(kc == 0), stop=(kc == 7)),
                            r=[wu_b, xnT_b], w=[pu_b], inc=(kc == 7))
                    cx.op("dve", lambda en, pu=pu, si=si, fc=fc: en.tensor_tensor(hT[:, fc, :], pu[:], sg[si][:], ALU.mult),
                          r=[pu_b, sg_b[si]], w=[hT_b[0 if fc < 8 else (1 if fc < 16 else 2)]])
            if mid is not None:
                mid()
            for nb in range(2):
                if step is not None:
                    step()
                accs = [mm_next(P) for _ in range(4)]
                for ksb, (k0, nk) in enumerate(((0, 8), (8, 8), (16, 6))):
                    wv, wv_b = wload(P, Wout, k0, nk, nb * 512, 512)
                    for j in range(4):
                        for k in range(nk):
                            fc = k0 + k
                            cx.op("pe", lambda en, j=j, k=k, fc=fc, wv=wv: en.matmul(
                                accs[j][0][:], hT[:, fc, j * 128:(j + 1) * 128], wv[:, k, :],
                                start=(fc == 0), stop=(fc == 21)),
                                r=[wv_b, hT_b[ksb]], w=[accs[j][1]], inc=(k == nk - 1))
                for j in range(4):
                    cx.op("dve", lambda en, j=j, nb=nb: en.scalar_tensor_tensor(
                        xt[:, j, nb * 512:(nb + 1) * 512], accs[j][0][:], 0.5, xt[:, j, nb * 512:(nb + 1) * 512],
                        ALU.mult, ALU.add), r=[accs[j][1], xt_b[j]], w=[xt_b[j]])

        if stop_after >= 1:
            with contextlib.ExitStack() as ph:
                P = rowlocal_pools(ph)
                xts = [sb(ph, "xt%d" % i, [128, 4, D], F32) for i in range(2)]
                xts_b = [[Buf("xt%d_%d" % (i, j_)) for j_ in range(4)] for i in range(2)]
                xn = sb(ph, "xn", [128, 4, D], BF16)
                xn_b = [Buf("xn%d" % j_) for j_ in range(4)]
                xnT = sb(ph, "xnT", [128, 8, T], BF16)
                xnT_b = Buf("xnT")
                xnT1 = [sb(ph, "xnT1_%d" % i, [128, 8, T], BF16) for i in range(2)]
                xnT1_b = [Buf("xnT1_0"), Buf("xnT1_1")]
                hT = sb(ph, "hT", [128, 22, T], BF16)
                hT_b = [Buf("hT%d" % k_) for k_ in range(3)]
                sg = [sb(ph, "sg%d" % i, [128, T], BF16) for i in range(2)]
                sg_b = [Buf("sg0"), Buf("sg1")]
                st = sb(ph, "st", [128, 16], F32)
                st_b = Buf("st")
                junk = sb(ph, "junk", [128, D], BF16)
                junk_b = Buf("junk")
                NST = 8
                stg = [sb(ph, "stg%d" % i, [128, T], BF16) for i in range(NST)]
                stg_b = [Buf("stg%d" % i) for i in range(NST)]
                sgt = [sb(ph, "sgt%d" % i, [8, T], F32) for i in range(2)]
                sgt_b = [Buf("sgt0"), Buf("sgt1")]
                ang = sb(ph, "ang", [128, T], F32)
                a2 = sb(ph, "a2", [128, T], F32)
                nf = sb(ph, "nf", [128, T], F32)
                ni = sb(ph, "ni", [128, T], I32)
                cosT = sb(ph, "cosT", [128, T], F32)
                sinT = sb(ph, "sinT", [128, T], F32)
                tab_b = Buf("tab")
                tmp_b = Buf("ropetmp")
                zq = [sb(ph, "zq_%d" % i, [128, T], BF16) for i in range(2)]
                zq_b = [Buf("zq0"), Buf("zq1")]
                r1 = [sb(ph, "r1_%d" % i, [128, T], F32) for i in range(2)]
                r2 = [sb(ph, "r2_%d" % i, [128, T], F32) for i in range(2)]
                r1_b = [Buf("r1_0"), Buf("r1_1")]
                r2_b = [Buf("r2_0"), Buf("r2_1")]
                stc = {"i": 0, "g": 0, "r": 0}
                bwl = [sb(ph, "bwl%d" % i, [128, 512], F32) for i in range(6)]
                bwo = [sb(ph, "bwo%d" % i, [128, 512], BF16) for i in range(6)]
                bg = {"g": bg_convert(bwl, [Buf("bwl%d" % i) for i in range(6)], bwo, [Buf("bwo%d" % i) for i in range(6)])}

                def bg_step(drain=False):
                    bg["n"] = bg.get("n", 0) + 1
                    if not drain and bg["n"] % 2 != 0:
                        return
                    while bg["g"] is not None:
                        try:
                            next(bg["g"])
                        except StopIteration:
                            bg["g"] = None
                        if not drain:
                            return

                def stage_next():
                    i = stc["i"] % NST
                    stc["i"] += 1
                    return stg[i], stg_b[i]

                def load_x(ti):
                    s, t0 = tiles[ti]
                    cx.dma("sp", xts[ti % 2][:], x_in[s][t0:t0 + T, :].rearrange("(j p) d -> p j d", p=128),
                           w=xts_b[ti % 2])

                def rope_tables(t0):
                    cx.op("dve", lambda en: en.tensor_scalar(ang[:], posrow[:], float(t0), ropec[:, 0:1], ALU.add, ALU.mult),
                          r=[b_const], w=[tmp_b])
                    for (ph_off, tab) in ((0.0, sinT), (0.5 * math.pi, cosT)):
                        cx.op("dve", lambda en: en.tensor_scalar(a2[:], ang[:], ph_off, None, ALU.add), r=[tmp_b], w=[tmp_b])
                        cx.op("dve", lambda en: en.tensor_scalar(nf[:], a2[:], 1.0 / TWO_PI, None, ALU.mult),
                              r=[tmp_b], w=[tmp_b])
                        cx.op("dve", lambda en: en.tensor_copy(ni[:], nf[:]), r=[tmp_b], w=[tmp_b])
                        cx.op("dve", lambda en: en.tensor_copy(nf[:], ni[:]), r=[tmp_b], w=[tmp_b])
                        cx.op("dve", lambda en: en.scalar_tensor_tensor(a2[:], nf[:], -CW1, a2[:], ALU.mult, ALU.add),
                              r=[tmp_b], w=[tmp_b])
                        cx.op("dve", lambda en: en.scalar_tensor_tensor(a2[:], nf[:], -CW2, a2[:], ALU.mult, ALU.add),
                              r=[tmp_b], w=[tmp_b])
                        cx.op("dve", lambda en: en.tensor_scalar(a2[:], a2[:], math.pi, -math.pi, ALU.min, ALU.max),
                              r=[tmp_b], w=[tmp_b])
                        cx.op("act", lambda en, tab=tab: en.activation(out=tab[:], in_=a2[:], func=AF.Sin),
                              r=[tmp_b], w=[tab_b])

                load_x(0)
                rms_to_fm(P, xts[0], xts_b[0], xn, xn_b, xnT1[0], xnT1_b[0], st, st_b, junk, junk_b)
                for ti, (s, t0) in enumerate(tiles):
                    xt, xt_b = xts[ti % 2], xts_b[ti % 2]
                    if ti + 1 < len(tiles):
                        load_x(ti + 1)
                    SC = sc[s]
                    ffn(P, W1, W1o, xnT1[ti % 2], xnT1_b[ti % 2], hT, hT_b, sg, sg_b, xt, xt_b,
                        mid=lambda t0=t0: rope_tables(t0), step=bg_step)
                    cx.dma("pool", SC["x1"][t0:t0 + T, :].rearrange("(j p) d -> p j d", p=128), xt[:], r=xt_b)
                    rms_to_fm(P, xt, xt_b, xn, xn_b, xnT, xnT_b, st, st_b, junk, junk_b)
                    rope_pending = []
                    for blk in range(NFC // 4):
                        bg_step()
                        wv, wv_b = wload(P, WF, 0, 8, blk * 512, 512)
                        for c in range(4):
                            ch = blk * 4 + c
                            pz, pz_b = mm_next(P)
                            for kc in range(8):
                                cx.op("pe", lambda en, kc=kc, c=c, pz=pz, wv=wv: en.matmul(
                                    pz[:], wv[:, kc, c * 128:(c + 1) * 128], xnT[:, kc, :],
                                    start=(kc == 0), stop=(kc == 7)),
                                    r=[wv_b, xnT_b], w=[pz_b], inc=(kc == 7))
                            for fn in rope_pending:
                                fn()
                            rope_pending = []
                            if ch < C_QD:
                                dst = SC["qmT"] if ch < C_KM else SC["kmT"]
                                sg_, sgb_ = stage_next()
                                e = ev_engine(P)
                                cx.op(e, copy_op(e, sg_[:], pz[:]), r=[pz_b], w=[sgb_])
                                cx.dma("pool", dst[ch % 4][:, t0:t0 + T], sg_[:], r=[sgb_])
                            elif ch < C_GA:
                                rel = ch - C_QD
                                hh = rel % 8
                                dst = SC["qdT"] if rel < 8 else SC["kdT"]
                                ri = stc["r"] % 2
                                stc["r"] += 1
                                cx.op("act", lambda en, pz=pz, ri=ri: en.activation(out=zq[ri][:], in_=pz[:], func=AF.Copy),
                                      r=[pz_b], w=[zq_b[ri]])

                                def rope_tail(ri=ri, dst=dst, hh=hh):
                                    pr, pr_b = mm_next(P)
                                    cx.op("pe", lambda en: en.matmul(pr[:], rot16[:], zq[ri][:], start=True, stop=True),
                                          r=[zq_b[ri], b_c2], w=[pr_b])
                                    cx.op("dve", lambda en: en.tensor_tensor(r1[ri][:], zq[ri][:], cosT[:], ALU.mult),
                                          r=[zq_b[ri], tab_b], w=[r1_b[ri]])
                                    cx.op("dve", lambda en: en.tensor_tensor(r2[ri][:], pr[:], sinT[:], ALU.mult),
                                          r=[pr_b, tab_b], w=[r2_b[ri]])
                                    sg_, sgb_ = stage_next()
                                    cx.op("dve", lambda en: en.tensor_tensor(sg_[:], r1[ri][:], r2[ri][:], ALU.add),
                                          r=[r1_b[ri], r2_b[ri]], w=[sgb_])
                                    cx.dma("pool", dst[hh][:, t0:t0 + T], sg_[:], r=[sgb_])
                                rope_pending.append(rope_tail)
                            else:
                                dst = SC["gaT"] if ch < C_GB else (SC["gbT"] if ch < C_OM else SC["omT"])
                                sg_, sgb_ = stage_next()
                                cx.op("act", lambda en, pz=pz, sg_=sg_: en.activation(out=sg_[:], in_=pz[:], func=AF.Sigmoid),
                                      r=[pz_b], w=[sgb_])
                                cx.dma("pool", dst[ch % 8][:, t0:t0 + T], sg_[:], r=[sgb_])
                    for fn in rope_pending:
                        fn()
                    rope_pending = []
                    wv, wv_b = wload(P, WF, 0, 8, NFC * 128, 16)
                    for gi, dst in ((0, SC["gig"]), (1, SC["gfg"])):
                        pz, pz_b = mm_next(P)
                        for kc in range(8):
                            cx.op("pe", lambda en, kc=kc, gi=gi, pz=pz, wv=wv: en.matmul(
                                pz[0:8, :], wv[:, kc, gi * 8:(gi + 1) * 8], xnT[:, kc, :], start=(kc == 0), stop=(kc == 7)),
                                r=[wv_b, xnT_b], w=[pz_b], inc=(kc == 7))
                        k = stc["g"] % 2
                        stc["g"] += 1
                        cx.op("act", lambda en, pz=pz, k=k, gi=gi: en.activation(
                            out=sgt[k][:], in_=pz[0:8, :], func=AF.Identity, bias=gbias[:, gi:gi + 1]),
                            r=[pz_b, b_const], w=[sgt_b[k]])
                        cx.dma("pool", dst[:, t0:t0 + T], sgt[k][:], r=[sgt_b[k]])
                    if ti + 1 < len(tiles):
                        nx = (ti + 1) % 2
                        rms_to_fm(P, xts[nx], xts_b[nx], xn, xn_b, xnT1[nx], xnT1_b[nx], st, st_b, junk, junk_b)
                    for blk in range(5):
                        bg_step()
                        wv, wv_b = wload(P, WT, 0, 8, blk * 512, 512)
                        for j in range(4):
                            pz, pz_b = mm_next(P)
                            for kc in range(8):
                                cx.op("pe", lambda en, kc=kc, j=j, pz=pz, wv=wv: en.matmul(
                                    pz[:], xnT[:, kc, j * 128:(j + 1) * 128], wv[:, kc, :], start=(kc == 0), stop=(kc == 7)),
                                    r=[wv_b, xnT_b], w=[pz_b], inc=(kc == 7))
                            sg_, sgb_ = stage_next()
                            e = ev_engine(P)
                            cx.op(e, copy_op(e, sg_[:], pz[:]), r=[pz_b], w=[sgb_])
                            rows = slice(t0 + j * 128, t0 + (j + 1) * 128)
                            if blk == 0:
                                dstap = SC["km"][rows, :]
                            elif blk < 3:
                                dstap = SC["vm"][rows, (blk - 1) * 512:blk * 512]
                            else:
                                dstap = SC["vd"][rows, (blk - 3) * 512:(blk - 2) * 512]
                            cx.dma("pool", dstap, sg_[:], r=[sgb_])
                bg_step(drain=True)
                cx.barrier()


        for s in range(NSEQ if stop_after >= 2 else 0):
            S = S_LIST[s]
            nch = S // 128
            SC = sc[s]
            NBK = 4 if nch % 4 == 0 else 1
            with contextlib.ExitStack() as seqst:
                aT = sb(seqst, "aT", [128, nch * 8], F32)
                rT = sb(seqst, "rT", [128, nch * 8], F32)
                decb = sb(seqst, "decb", [128, 8 * nch], F32)
                b_gt = Buf("gt")
                with contextlib.ExitStack() as ph:
                    gA = sb(ph, "gA", [8, S], F32)
                    gB = sb(ph, "gB", [8, S], F32)
                    gC = sb(ph, "gC", [8, S], F32)
                    sm = [sb(ph, "gsm%d" % i, [8, nch], F32) for i in range(8)]
                    tpa = pst(ph, "tpa", [128, 512], F32)
                    tpr = pst(ph, "tpr", [128, 512], F32)
                    b_tpa, b_tpr = Buf("tpa"), Buf("tpr")
                    bg = Buf("g")
                    AL, BE = abc[0:8, 0:1], abc[0:8, 1:2]

                    def G(e, fn):
                        cx.op(e, fn, r=[bg, b_const], w=[bg])

                    cx.dma("sp", gC[:], SC["gig"][:, :], w=[bg])
                    cx.dma("sp", gA[:], SC["gfg"][:, :], w=[bg], merge=True)
                    G("act", lambda en: en.activation(out=gA[:], in_=gA[:], func=AF.Exp, scale=-1.0))
                    G("act", lambda en: en.activation(out=gA[:], in_=gA[:], func=AF.Ln, bias=1.0))
                    G("dve", lambda en: en.tensor_tensor_scan(gB[:], gA[:], gA[:], 0.0, ALU.add, ALU.bypass))
                    G("dve", lambda en: en.tensor_scalar(gA[:], gA[:], gB[:, S - 1:S], BE, ALU.add, ALU.mult))
                    G("dve", lambda en: en.scalar_tensor_tensor(gB[:], gB[:], AL, gA[:], ALU.mult, ALU.add))
                    G("dve", lambda en: en.tensor_tensor(gA[:], gC[:], gB[:], ALU.add))
                    G("dve", lambda en: en.tensor_reduce(sm[0][:], gA[:].rearrange("p (c l) -> p c l", l=128), AX.X, ALU.max))
                    G("dve", lambda en: en.tensor_tensor_scan(sm[1][:], sm[0][:], sm[0][:], 0.0, ALU.max, ALU.bypass))
                    G("dve", lambda en: en.memset(sm[2][:], 0.0))
                    if nch > 1:
                        G("dve", lambda en: en.tensor_copy(sm[2][:, 1:nch], sm[1][:, 0:nch - 1]))
                    cur = sm[0]
                    pp = [sm[3], sm[4]]
                    sh = 1
                    k = 0
                    while sh < nch:
                        nxt = pp[k % 2]
                        G("dve", lambda en, nxt=nxt, cur=cur, sh=sh: en.tensor_tensor(
                            nxt[:, 0:nch - sh], cur[:, 0:nch - sh], cur[:, sh:nch], ALU.max))
                        G("dve", lambda en, nxt=nxt, cur=cur, sh=sh: en.tensor_copy(nxt[:, nch - sh:nch], cur[:, nch - sh:nch]))
                        cur = nxt
                        k += 1
                        sh *= 2
                    G("dve", lambda en: en.memset(sm[5][:], 0.0))
                    if nch > 1:
                        G("dve", lambda en: en.tensor_scalar(sm[5][:, 0:nch - 1], cur[:, 1:nch], 0.0, None, ALU.max))
                    G("dve", lambda en: en.tensor_tensor(sm[5][:], sm[5][:], sm[2][:], ALU.subtract))
                    G("dve", lambda en: en.scalar_tensor_tensor(sm[6][:], sm[5][:], BE, sm[2][:], ALU.mult, ALU.add))
                    if NBK > 1:
                        M3 = sm[6][:].rearrange("p (b k) -> p b k", k=NBK)
                        F3 = sm[7][:].rearrange("p (b k) -> p b k", k=NBK)
                        B3 = sm[0][:].rearrange("p (b k) -> p b k", k=NBK)
                        G("dve", lambda en: en.tensor_copy(F3, M3[:, :, 0:1].broadcast_to([8, nch // NBK, NBK])))
                        G("dve", lambda en: en.tensor_copy(B3, M3[:, :, NBK - 1:NBK].broadcast_to([8, nch // NBK, NBK])))
                        G("dve", lambda en: en.tensor_tensor(sm[0][:], sm[0][:], sm[7][:], ALU.subtract))
                        G("dve", lambda en: en.scalar_tensor_tensor(sm[6][:], sm[0][:], BE, sm[7][:], ALU.mult, ALU.add))
                    Mt = sm[6]
                    G("dve", lambda en: en.tensor_copy(sm[1][:], Mt[:]))
                    G("dve", lambda en: en.tensor_copy(sm[3][:], Mt[:]))
                    if nch > 1:
                        G("dve", lambda en: en.tensor_copy(sm[1][:, 0:nch - 1], Mt[:, 1:nch]))
                        G("dve", lambda en: en.tensor_copy(sm[3][:, 1:nch], Mt[:, 0:nch - 1]))
                    G("dve", lambda en: en.tensor_tensor(sm[3][:], sm[3][:], sm[1][:], ALU.subtract))
                    G("dve", lambda en: en.scalar_tensor_tensor(sm[3][:], sm[3][:], BE, sm[1][:], ALU.mult, ALU.add))
                    G("dve", lambda en: en.tensor_tensor(sm[4][:], Mt[:], sm[3][:], ALU.subtract))
                    G("act", lambda en: en.activation(out=sm[4][:], in_=sm[4][:], func=AF.Exp))
                    b_decd = Buf("decd")
                    cx.dma("pool", SC["dec"].rearrange("(r c) -> r c", r=8), sm[4][:], r=[bg], w=[b_decd])
                    cx.dma("sp", decb[:], SC["dec"].partition_broadcast(128), r=[b_decd], w=[b_gt])
                    Mbc = Mt[:].unsqueeze(2).broadcast_to([8, nch, 128])
                    gA3 = gA[:].rearrange("p (c l) -> p c l", l=128)
                    gB3 = gB[:].rearrange("p (c l) -> p c l", l=128)
                    G("dve", lambda en: en.tensor_tensor(gA3, gA3, Mbc, ALU.subtract))
                    G("act", lambda en: en.activation(out=gA[:], in_=gA[:], func=AF.Exp))
                    G("dve", lambda en: en.tensor_tensor(gB3, gB3, Mbc, ALU.subtract))
                    G("act", lambda en: en.activation(out=gB[:], in_=gB[:], func=AF.Exp))
                    for (src, tp_, tpb_, dstt) in ((gA, tpa, b_tpa, aT), (gB, tpr, b_tpr, rT)):
                        for c in range(nch):
                            cx.op("pe", lambda en, c=c, src=src, tp_=tp_: en.transpose(
                                tp_[:, c * 8:(c + 1) * 8], src[:, c * 128:(c + 1) * 128], ident[0:8, 0:8]),
                                r=[bg, b_const], w=[tpb_], inc=(c == nch - 1))
                        cx.op("dve", lambda en, tp_=tp_, dstt=dstt: en.tensor_copy(dstt[:], tp_[:, 0:nch * 8]),
                              r=[tpb_], w=[b_gt] if dstt is aT else [b_gt])
                    cx.barrier()

                if stop_after >= 3:
                    with contextlib.ExitStack() as ph:
                        qT = sb(ph, "m_qT", [128, S], BF16)
                        kT = sb(ph, "m_kT", [128, S], BF16)
                        kTM = sb(ph, "m_kTM", [128, nch, 128], BF16)
                        vp = sb(ph, "m_vp", [128, nch, 257], BF16)
                        hfirst = sb(ph, "m_hf", [128, nch, 256], F32)
                        S32 = [sb(ph, "m_S32_%d" % d, [128, 257], F32) for d in range(2)]
                        S16 = [[sb(ph, "m_S16_%d_%d" % (d, k), [128, 257], BF16) for k in range(2)] for d in range(2)]
                        tmpS = [sb(ph, "m_tmp_%d" % d, [128, 257], F32) for d in range(2)]
                        WTt = [[sb(ph, "m_WT_%d_%d" % (d, k), [128, 128], BF16) for k in range(2)] for d in range(2)]
                        ks = [[sb(ph, "m_ks_%d_%d" % (d, k), [128, 128], BF16) for k in range(2)] for d in range(2)]
                        dn = [sb(ph, "m_dn_%d" % d, [128, 8], F32) for d in range(2)]
                        hs = [sb(ph, "m_hs_%d" % i, [128, 256], F32) for i in range(4)]
                        hn = [sb(ph, "m_hn_%d" % i, [128, 256], BF16) for i in range(4)]
                        hstg = [sb(ph, "m_hstg_%d" % i, [128, 256], BF16) for i in range(4)]
                        hst = [sb(ph, "m_hst_%d" % i, [128, 8], F32) for i in range(4)]
                        mjunk = sb(ph, "m_junk", [128, 256], BF16)
                        p_sT = [pst(ph, "m_psT%d" % d, [128, 512], F32) for d in range(2)]
                        p_out = [pst(ph, "m_pout%d" % d, [128, 512], F32) for d in range(2)]
                        p_upd = [pst(ph, "m_pupd%d" % d, [128, 512], F32) for d in range(2)]
                        p_tr = [pst(ph, "m_ptr%d" % d, [128, 1024], BF16) for d in range(2)]
                        b_q, b_k, b_ktm, b_vp, b_hf = Buf("q"), Buf("k"), Buf("ktm"), Buf("vp"), Buf("hf")
                        b_S32 = [Buf("S32"), Buf("S32")]
                        b_S16 = [[Buf("S16"), Buf("S16")], [Buf("S16"), Buf("S16")]]
                        b_tmp = [Buf("tmp"), Buf("tmp")]
                        b_WT = [[Buf("WT"), Buf("WT")], [Buf("WT"), Buf("WT")]]
                        b_ks = [[Buf("ks"), Buf("ks")], [Buf("ks"), Buf("ks")]]
                        b_dn = [Buf("dn"), Buf("dn")]
                        b_hs = [Buf("hs") for _ in range(4)]
                        b_hn = [Buf("hn") for _ in range(4)]
                        b_hstg = [Buf("hstg") for _ in range(4)]
                        b_hst = [Buf("hst") for _ in range(4)]
                        b_mj = Buf("mj")
                        b_psT = [Buf("psT"), Buf("psT")]
                        b_pout = [Buf("pout"), Buf("pout")]
                        b_pupd = [Buf("pupd"), Buf("pupd")]
                        b_ptr = [Buf("ptr0"), Buf("ptr1")]
                        cx.op("dve", lambda en: en.memset(vp[:, :, 256:257], 1.0), w=[b_vp])
                        sec = {"i": 0}
                        for h in range(4):
                            cx.dma("sp", qT[:], SC["qmT"][h], w=[b_q])
                            cx.dma("sp", kT[:], SC["kmT"][h], w=[b_k])
                            cx.dma("sp", kTM[:], SC["km"][:, h * 128:(h + 1) * 128].rearrange("(c p) d -> p c d", p=128),
                                   w=[b_ktm])
                            cx.dma("sp", vp[:, :, 0:256],
                                   SC["vm"][:, h * 256:(h + 1) * 256].rearrange("(c p) d -> p c d", p=128), w=[b_vp], merge=True)
                            for d in range(2):
                                cx.op("dve", lambda en, d=d: en.memset(S32[d][:], 0.0), w=[b_S32[d]])
                                cx.op("dve", lambda en, d=d: en.memset(S16[d][0][:], 0.0), w=[b_S16[d][0]])

                            def chunk(d, i):
                                return i if d == 0 else nch - 1 - i

                            def gcol(d, c):
                                r_ = d * 4 + h
                                return (aT[:, c * 8 + r_:c * 8 + r_ + 1], rT[:, c * 8 + r_:c * 8 + r_ + 1],
                                        decb[:, r_ * nch + c:r_ * nch + c + 1])

                            def emit_ks(i):
                                for d in range(2):
                                    c = chunk(d, i)
                                    a_col = gcol(d, c)[0]
                                    cx.op("act", lambda en, d=d, c=c, a_col=a_col: en.activation(
                                        out=ks[d][i % 2][:], in_=kTM[:, c, :], func=AF.Copy, scale=a_col),
                                        r=[b_ktm, b_gt], w=[b_ks[d][i % 2]])

                            def emit_s1(i):
                                for d in range(2):
                                    c = chunk(d, i)
                                    cs = slice(c * 128, (c + 1) * 128)
                                    cx.op("pe", lambda en, d=d, cs=cs: en.matmul(p_sT[d][:, 0:128], kT[:, cs], qT[:, cs],
                                                                               start=True, stop=True),
                                          r=[b_k, b_q], w=[b_psT[d]])
                                for d in range(2):
                                    c = chunk(d, i)
                                    a_col = gcol(d, c)[0]
                                    cx.op("dve", lambda en, d=d, a_col=a_col: en.scalar_tensor_tensor(
                                        WTt[d][i % 2][:], p_sT[d][:, 0:128], a_col, masks[:, d * 128:(d + 1) * 128],
                                        ALU.mult, ALU.mult),
                                        r=[b_psT[d], b_gt, b_const], w=[b_WT[d][i % 2]])
                                if i < nch - 1:
                                    for d in range(2):
                                        c = chunk(d, i)
                                        cx.op("pe", lambda en, d=d, c=c: en.matmul(
                                            p_upd[d][:, 0:257], ks[d][i % 2][:], vp[:, c, :],
                                            start=(i % NBK == 0), stop=(i % NBK == NBK - 1 or i == nch - 2),
                                            skip_group_check=True),
                                            r=[b_ks[d][i % 2], b_vp], w=[b_pupd[d]])

                            if nch > 1:
                                emit_ks(0)
                            emit_s1(0)
                            pending = []
                            for i in range(nch):
                                if i + 1 < nch - 1:
                                    emit_ks(i + 1)
                                for d in range(2):
                                    c = chunk(d, i)
                                    cs = slice(c * 128, (c + 1) * 128)
                                    cx.op("pe", lambda en, d=d, cs=cs: en.matmul(p_out[d][:, 0:257], qT[:, cs], S16[d][i % 2][:],
                                                                               start=True, stop=False),
                                          r=[b_q, b_S16[d][i % 2]], w=[b_pout[d]], inc=False)
                                    cx.op("pe", lambda en, d=d, c=c: en.matmul(p_out[d][:, 0:257], WTt[d][i % 2][:], vp[:, c, :],
                                                                             start=False, stop=True),
                                          r=[b_WT[d][i % 2], b_vp], w=[b_pout[d]])
                                for fn in pending:
                                    fn()
                                pending = []
                                if i < nch - 1:
                                    for d in range(2):
                                        c = chunk(d, i)
                                        dec_col = gcol(d, c)[2]
                                        nxt = S16[d][(i + 1) % 2]
                                        nxt_b = b_S16[d][(i + 1) % 2]
                                        if (i + 1) % NBK == 0:
                                            cx.op("dve", lambda en, d=d: en.tensor_tensor(tmpS[d][:], S32[d][:], p_upd[d][:, 0:257], ALU.add),
                                                  r=[b_S32[d], b_pupd[d]], w=[b_tmp[d]])
                                            cx.op("dve", lambda en, d=d, dec_col=dec_col: en.tensor_scalar(
                                                S32[d][:], tmpS[d][:], dec_col, None, ALU.mult),
                                                r=[b_tmp[d], b_gt], w=[b_S32[d]])
                                            cx.op("dve", lambda en, d=d, nxt=nxt: en.tensor_copy(nxt[:], S32[d][:]),
                                                  r=[b_S32[d]], w=[nxt_b])
                                        else:
                                            cx.op("dve", lambda en, d=d, nxt=nxt: en.tensor_tensor(nxt[:], S32[d][:], p_upd[d][:, 0:257],
                                                                                                ALU.add),
                                                  r=[b_S32[d], b_pupd[d]], w=[nxt_b])
                                if i + 1 < nch:
                                    emit_s1(i + 1)
                                for d in range(2):
                                    cx.op("act", lambda en, d=d: en.activation(out=dn[d][:, 0:1], in_=p_out[d][:, 256:257], func=AF.Abs),
                                          r=[b_pout[d]], w=[b_dn[d]])
                                for d in range(2):
                                    r_col = gcol(d, chunk(d, i))[1]
                                    cx.op("dve", lambda en, d=d, r_col=r_col: en.tensor_tensor(dn[d][:, 1:2], dn[d][:, 0:1], r_col, ALU.max),
                                          r=[b_dn[d], b_gt], w=[b_dn[d]])
                                for d in range(2):
                                    cx.op("dve", lambda en, d=d: en.reciprocal(dn[d][:, 2:3], dn[d][:, 1:2]), r=[b_dn[d]], w=[b_dn[d]])
                                secs = []
                                for d in range(2):
                                    c = chunk(d, i)
                                    other = nch - 1 - i
                                    second_d = (other < i) or (d == 1 and other == i)
                                    if not second_d:
                                        cx.op("act", lambda en, d=d, c=c: en.activation(
                                            out=hfirst[:, c, :], in_=p_out[d][:, 0:256], func=AF.Copy, scale=dn[d][:, 2:3]),
                                            r=[b_pout[d], b_dn[d]], w=[b_hf])
                                    else:
                                        k4 = sec["i"] % 4
                                        sec["i"] += 1
                                        secs.append((d, c, k4))
                                for (d, c, k4) in secs:
                                    cx.op("dve", lambda en, d=d, c=c, k4=k4: en.scalar_tensor_tensor(
                                        hs[k4][:], p_out[d][:, 0:256], dn[d][:, 2:3], hfirst[:, c, :], ALU.mult, ALU.add),
                                        r=[b_pout[d], b_dn[d], b_hf], w=[b_hs[k4]])
                                for (d, c, k4) in secs:
                                    cx.op("act", lambda en, k4=k4: en.activation(out=mjunk[:], in_=hs[k4][:], func=AF.Square,
                                                                                 accum_out=hst[k4][:, 0:1]),
                                          r=[b_hs[k4]], w=[b_mj, b_hst[k4]])
                                for (d, c, k4) in secs:
                                    cx.op("act", lambda en, k4=k4: en.activation(out=hst[k4][:, 1:2], in_=hst[k4][:, 0:1], func=AF.Sqrt,
                                                                                 scale=1.0 / 256, bias=EPS),
                                          r=[b_hst[k4]], w=[b_hst[k4]])
                                for (d, c, k4) in secs:
                                    cx.op("dve", lambda en, k4=k4: en.reciprocal(hst[k4][:, 2:3], hst[k4][:, 1:2]),
                                          r=[b_hst[k4]], w=[b_hst[k4]])
                                for (d, c, k4) in secs:
                                    cx.op("act", lambda en, k4=k4: en.activation(out=hn[k4][:], in_=hs[k4][:], func=AF.Copy,
                                                                                 scale=hst[k4][:, 2:3]),
                                          r=[b_hs[k4], b_hst[k4]], w=[b_hn[k4]])
                                for (d, c, k4) in secs:
                                    cs = slice(c * 128, (c + 1) * 128)

                                    def tail(k4=k4, cs=cs, d=d):
                                        for half in range(2):
                                            cx.op("pe", lambda en, half=half: en.transpose(
                                                p_tr[d][:, half * 128:(half + 1) * 128], hn[k4][:, half * 128:(half + 1) * 128],
                                                ident16[:]),
                                                r=[b_hn[k4], b_c2], w=[b_ptr[d]], inc=(half == 1))
                                        cx.op("dve", lambda en: en.tensor_copy(hstg[k4][:], p_tr[d][:, 0:256]),
                                              r=[b_ptr[d]], w=[b_hstg[k4]])
                                        cx.dma("pool", SC["hmT"][h * 2:h * 2 + 2, :, cs].rearrange("t p n -> p t n"),
                                               hstg[k4][:].rearrange("p (t n) -> p t n", t=2), r=[b_hstg[k4]])
                                    pending.append(tail)
                            for fn in pending:
                                fn()
                            pending = []
                        cx.barrier()

                if stop_after >= 4:
                    with contextlib.ExitStack() as ph:
                        nkt = nch
                        nqb = S // 512
                        qT2 = [sb(ph, "d_qT%d" % i, [128, S], BF16) for i in range(2)]
                        kT2 = [sb(ph, "d_kT%d" % i, [128, S], BF16) for i in range(2)]
                        vp2 = [sb(ph, "d_vp%d" % i, [128, nkt, 129], BF16) for i in range(2)]
                        Et = [sb(ph, "d_E%d" % i, [128, 2, 512], BF16) for i in range(3)]
                        accS = [sb(ph, "d_accS%d" % i, [128, 3, 387], F32) for i in range(2)]
                        rc = [sb(ph, "d_rc%d" % i, [128, 20], F32) for i in range(2)]
                        t1 = [sb(ph, "d_t1_%d" % i, [128, 128], F32) for i in range(2)]
                        ot = [sb(ph, "d_o_%d" % i, [128, 4, 128], F32) for i in range(2)]
                        on = [sb(ph, "d_on_%d" % i, [128, 4, 128], BF16) for i in range(2)]
                        ost = [sb(ph, "d_ost_%d" % i, [128, 512], BF16) for i in range(2)]
                        djunk = sb(ph, "d_junk", [128, 128], F32)
                        scp = [pst(ph, "d_scp%d" % i, [128, 2, 512], F32) for i in range(2)]
                        accp = pst(ph, "d_accp", [128, 3, 512], F32)
                        trp2 = pst(ph, "d_trp", [128, 1024], BF16)
                        b_qk = [Buf("qk0"), Buf("qk1")]
                        b_v = [Buf("v0"), Buf("v1")]
                        b_sc = [Buf("sc0"), Buf("sc1")]
                        b_E = [Buf("E0"), Buf("E1"), Buf("E2")]
                        b_acc = Buf("acc")
                        b_accS = [Buf("accS0"), Buf("accS1")]
                        b_rc = [Buf("rc0"), Buf("rc1")]
                        b_t1 = [Buf("t1"), Buf("t1")]
                        b_o = [Buf("o"), Buf("o")]
                        b_on = [Buf("on"), Buf("on")]
                        b_ost = [Buf("ost"), Buf("ost")]
                        b_dj = Buf("dj")
                        b_trp = Buf("trp")
                        for i in range(2):
                            cx.op("dve", lambda en, i=i: en.memset(vp2[i][:, :, 128:129], 1.0), w=[b_v[i]])

                        def accv(a):
                            return accp[:, a // 3, (a % 3) * 129:(a % 3) * 129 + 129]

                        def dload(h):
                            i = h % 2
                            cx.dma("sp", qT2[i][:], SC["qdT"][h], w=[b_qk[i]])
                            cx.dma("sp", kT2[i][:], SC["kdT"][h], w=[b_qk[i]], merge=True)
                            cx.dma("sp", vp2[i][:, :, 0:128],
                                   SC["vd"][:, h * 128:(h + 1) * 128].rearrange("(c p) d -> p c d", p=128),
                                   w=[b_v[i]], merge=True)

                        fin = {"i": 0, "gen": None}

                        def finalize(fi, h, qsl):
                            for b in range(3):
                                wdt = 387 if b < 2 else 258
                                cx.op("dve", lambda en, b=b, wdt=wdt: en.tensor_copy(accS[fi][:, b, 0:wdt], accp[:, b, 0:wdt]),
                                      r=[b_acc], w=[b_accS[fi]])
                            yield
                            rcq = rc[fi]

                            def A(a):
                                return accS[fi][:, a // 3, (a % 3) * 129:(a % 3) * 129 + 129]
                            for qs in range(4):
                                A1, A2 = A(qs * 2), A(qs * 2 + 1)
                                cx.op("dve", lambda en: en.reciprocal(rcq[:, 0:1], A1[:, 128:129]), r=[b_accS[fi], b_rc[fi]], w=[b_rc[fi]])
                                cx.op("dve", lambda en: en.reciprocal(rcq[:, 1:2], A2[:, 128:129]), r=[b_accS[fi], b_rc[fi]], w=[b_rc[fi]])
                                cx.op("dve", lambda en: en.tensor_tensor(rcq[:, 2:3], rcq[:, 1:2], NEG_LAM, ALU.mult),
                                      r=[b_rc[fi], b_l], w=[b_rc[fi]])
                                cx.op("dve", lambda en: en.tensor_scalar(t1[fi][:], A1[:, 0:128], rcq[:, 0:1], None, ALU.mult),
                                      r=[b_accS[fi], b_rc[fi]], w=[b_t1[fi]])
                                cx.op("dve", lambda en: en.scalar_tensor_tensor(ot[fi][:, qs, :], A2[:, 0:128], rcq[:, 2:3], t1[fi][:],
                                                                                ALU.mult, ALU.add),
                                      r=[b_accS[fi], b_rc[fi], b_t1[fi]], w=[b_o[fi]])
                                cx.op("dve", lambda en: en.scalar_tensor_tensor(djunk[:], ot[fi][:, qs, :], 1.0, ot[fi][:, qs, :],
                                                                                ALU.mult, ALU.mult, accum_out=rcq[:, 8 + qs:9 + qs]),
                                      r=[b_o[fi], b_rc[fi]], w=[b_dj, b_rc[fi]])
                                if qs == 1:
                                    yield
                            for _ in range(5):
                                yield
                            cx.op("act", lambda en: en.activation(out=rcq[:, 12:16], in_=rcq[:, 8:12], func=AF.Ln,
                                                                  scale=1.0 / 128, bias=EPS),
                                  r=[b_rc[fi]], w=[b_rc[fi]])
                            cx.op("act", lambda en: en.activation(out=rcq[:, 16:20], in_=rcq[:, 12:16], func=AF.Exp, scale=-0.5),
                                  r=[b_rc[fi]], w=[b_rc[fi]])
                            yield
                            yield
                            for qs in range(4):
                                cx.op("dve", lambda en, qs=qs: en.tensor_scalar(on[fi][:, qs, :], ot[fi][:, qs, :], rcq[:, 16 + qs:17 + qs],
                                                                               None, ALU.mult),
                                      r=[b_o[fi], b_rc[fi]], w=[b_on[fi]])
                            yield
                            for qs in range(4):
                                cx.op("pe", lambda en, qs=qs: en.transpose(trp2[:, qs * 128:(qs + 1) * 128], on[fi][:, qs, :], ident16[:]),
                                      r=[b_on[fi], b_c2], w=[b_trp], inc=(qs == 3))
                            yield
                            cx.op("dve", lambda en: en.tensor_copy(ost[fi][:], trp2[:, 0:512]), r=[b_trp], w=[b_ost[fi]])
                            cx.dma("pool", SC["odT"][h][:, qsl], ost[fi][:], r=[b_ost[fi]])

                        def fin_step(drain=False):
                            g = fin["gen"]
                            while g is not None:
                                try:
                                    next(g)
                                except StopIteration:
                                    fin["gen"] = None
                                    return
                                if not drain:
                                    return

                        steps = [(h, qb, kt) for h in range(8) for qb in range(nqb) for kt in range(nkt)]
                        NS = len(steps)

                        def qk(n):
                            (h, qb, kt) = steps[n]
                            hi = h % 2
                            sl_ = n % 2
                            es_ = n % 3
                            qsl = slice(qb * 512, (qb + 1) * 512)
                            for c in range(2):
                                cx.op("pe", lambda en, c=c: en.matmul(
                                    scp[sl_][:, c, :],
                                    kT2[hi][c * 64:(c + 1) * 64, kt * 128:(kt + 1) * 128],
                                    qT2[hi][c * 64:(c + 1) * 64, qsl], start=True, stop=True),
                                    r=[b_qk[hi]], w=[b_sc[sl_]], inc=(c == 1))
                            cx.op("act", lambda en: en.activation(out=Et[es_][:], in_=scp[sl_][:], func=AF.Exp, scale=0.125),
                                  r=[b_sc[sl_]], w=[b_E[es_]])

                        def pv(n):
                            (h, qb, kt) = steps[n]
                            hi = h % 2
                            es_ = n % 3
                            for qs in range(4):
                                for c in range(2):
                                    a = qs * 2 + c
                                    cx.op("pe", lambda en, a=a, c=c, qs=qs: en.matmul(
                                        accv(a), Et[es_][:, c, qs * 128:(qs + 1) * 128], vp2[hi][:, kt, :],
                                        start=(kt == 0 and a % 3 == 0), stop=(kt == nkt - 1), skip_group_check=True),
                                        r=[b_E[es_], b_v[hi]], w=[b_acc], inc=(a == 7))

                        dload(0)
                        qk(0)
                        if NS > 1:
                            qk(1)
                        for n in range(NS):
                            (h, qb, kt) = steps[n]
                            if qb == 0 and kt == 0 and h + 1 < 8:
                                dload(h + 1)
                            if n + 2 < NS:
                                qk(n + 2)
                            pv(n)
                            if kt == nkt - 1:
                                fin_step(drain=True)
                                fi = fin["i"] % 2
                                fin["i"] += 1
                                fin["gen"] = finalize(fi, h, slice(qb * 512, (qb + 1) * 512))
                                fin_step()
                            elif kt >= 1:
                                fin_step()
                        fin_step(drain=True)
                        cx.barrier()

        if stop_after >= 5:
            with contextlib.ExitStack() as ph:
                P = rowlocal_pools(ph)
                xts = [sb(ph, "xt%d" % i, [128, 4, D], F32) for i in range(2)]
                xts_b = [[Buf("xt%d_%d" % (i, j_)) for j_ in range(4)] for i in range(2)]
                xn = sb(ph, "xn", [128, 4, D], BF16)
                xn_b = [Buf("xn%d" % j_) for j_ in range(4)]
                ox = sb(ph, "ox", [128, 4, D], BF16)
                ox_b = [Buf("ox%d" % j_) for j_ in range(4)]
                hT = sb(ph, "hT", [128, 22, T], BF16)
                hT_b = [Buf("hT%d" % k_) for k_ in range(3)]
                sg = [sb(ph, "sg%d" % i, [128, T], BF16) for i in range(2)]
                sg_b = [Buf("sg0"), Buf("sg1")]
                st = sb(ph, "st", [128, 16], F32)
                st_b = Buf("st")
                junk = sb(ph, "junk", [128, D], BF16)
                junk_b = Buf("junk")
                fmr = [sb(ph, "fmr%d" % i, [128, 8, T], BF16) for i in range(3)]
                fmr_b = [Buf("fmr%d" % i) for i in range(3)]
                gen = [sb(ph, "gen%d" % i, [128, 8, T], BF16) for i in range(2)]
                gen_b = [Buf("gen%d" % i) for i in range(2)]
                tf = [sb(ph, "tf%d" % i, [128, T], F32) for i in range(4)]
                tf_b = [Buf("tf%d" % i) for i in range(4)]
                ETt = [sb(ph, "ET%d" % i, [128, 2, T], BF16) for i in range(2)]
                ET_b = [Buf("ET0"), Buf("ET1")]
                rcx = sb(ph, "rcx", [128, 16], F32)
                rcx_b = Buf("rcx")
                KxT = [sb(ph, "KxT%d" % s_, [128, 8, NMEM], BF16) for s_ in range(NSEQ)]
                Vx = [sb(ph, "Vx%d" % s_, [128, 2, 4, 257], BF16) for s_ in range(NSEQ)]
                kv_b = [Buf("kv%d" % s_) for s_ in range(NSEQ)]
                cnt3 = {"fm": 0, "tf": 0}

                def tm_to_fm(src, src_b, dst, dst_b, ntok=4):
                    for j in range(ntok):
                        tp, tp_b = tr_next(P)
                        for kc in range(8):
                            cx.op("pe", lambda en, j=j, kc=kc, tp=tp: en.transpose(
                                tp[:, kc * 128:(kc + 1) * 128], src[:, j, kc * 128:(kc + 1) * 128], ident16[:]),
                                r=[src_b[j], b_c2], w=[tp_b], inc=(kc == 7))
                        e = ev_engine(P)
                        cx.op(e, copy_op(e, dst[:, 0:8, j * 128:(j + 1) * 128],
                                         tp[:, 0:1024].rearrange("p (k n) -> p k n", k=8)), r=[tp_b], w=[dst_b])

                def tm_linear_add(Wd, src, src_b, xt, xt_b):
                    for nb in range(2):
                        wv, wv_b = wload(P, Wd, 0, 8, nb * 512, 512)
                        for j in range(4):
                            pz, pz_b = mm_next(P)
                            for kc in range(8):
                                cx.op("pe", lambda en, kc=kc, j=j, pz=pz, wv=wv: en.matmul(
                                    pz[:], src[:, kc, j * 128:(j + 1) * 128], wv[:, kc, :], start=(kc == 0), stop=(kc == 7)),
                                    r=[wv_b, src_b], w=[pz_b], inc=(kc == 7))
                            cx.op("dve", lambda en, j=j, nb=nb, pz=pz: en.tensor_tensor(
                                xt[:, j, nb * 512:(nb + 1) * 512], pz[:], xt[:, j, nb * 512:(nb + 1) * 512], ALU.add),
                                r=[pz_b, xt_b[j]], w=[xt_b[j]])

                def tm_linear_add_norm(Wd, src, src_b, xt, xt_b, dstT, dstT_b):
                    wvs = [wload(P, Wd, 0, 8, nb * 512, 512) for nb in range(2)]
                    pend = []
                    for j in range(4):
                        for nb in range(2):
                            wv, wv_b = wvs[nb]
                            pz, pz_b = mm_next(P)
                            for kc in range(8):
                                cx.op("pe", lambda en, kc=kc, j=j, pz=pz, wv=wv: en.matmul(
                                    pz[:], src[:, kc, j * 128:(j + 1) * 128], wv[:, kc, :], start=(kc == 0), stop=(kc == 7)),
                                    r=[wv_b, src_b], w=[pz_b], inc=(kc == 7))
                            cx.op("dve", lambda en, j=j, nb=nb, pz=pz: en.tensor_tensor(
                                xt[:, j, nb * 512:(nb + 1) * 512], pz[:], xt[:, j, nb * 512:(nb + 1) * 512], ALU.add),
                                r=[pz_b, xt_b[j]], w=[xt_b[j]])
                        cx.op("act", lambda en, j=j: en.activation(out=junk[:], in_=xt[:, j, :], func=AF.Square,
                                                                   accum_out=st[:, j:j + 1]),
                              r=[xt_b[j]], w=[junk_b, st_b])
                        cx.op("act", lambda en, j=j: en.activation(out=st[:, 4 + j:5 + j], in_=st[:, j:j + 1], func=AF.Sqrt,
                                                                   scale=1.0 / D, bias=EPS), r=[st_b], w=[st_b])
                        cx.op("dve", lambda en, j=j: en.reciprocal(st[:, 8 + j:9 + j], st[:, 4 + j:5 + j]), r=[st_b], w=[st_b])
                        xb = xn_b[j]
                        if j % 2 == 0:
                            cx.op("act", lambda en, j=j: en.activation(out=xn[:, j, :], in_=xt[:, j, :], func=AF.Copy,
                                                                       scale=st[:, 8 + j:9 + j]),
                                  r=[xt_b[j], st_b], w=[xb])
                        else:
                            cx.op("dve", lambda en, j=j: en.tensor_scalar(xn[:, j, :], xt[:, j, :], st[:, 8 + j:9 + j], None,
                                                                          ALU.mult),
                                  r=[xt_b[j], st_b], w=[xb])
                        for fn in pend:
                            fn()
                        pend = []

                        def tail(j=j, xb=xb):
                            tp, tp_b = tr_next(P)
                            for kc in range(8):
                                cx.op("pe", lambda en, kc=kc: en.transpose(
                                    tp[:, kc * 128:(kc + 1) * 128], xn[:, j, kc * 128:(kc + 1) * 128], ident16[:]),
                                    r=[xb, b_c2], w=[tp_b], inc=(kc == 7))
                            e = ev_engine(P)
                            cx.op(e, copy_op(e, dstT[:, 0:8, j * 128:(j + 1) * 128],
                                             tp[:, 0:1024].rearrange("p (k n) -> p k n", k=8)), r=[tp_b], w=[dstT_b])
                        pend.append(tail)
                    for fn in pend:
                        fn()

                for s_ in range(NSEQ):
                    memt, memt_b = xts[s_ % 2], xts_b[s_ % 2]
                    cx.dma("sp", memt[:, 0:2, :], mem_in[s_].rearrange("(j p) d -> p j d", p=128), w=memt_b)
                    memT, memT_b = gen[0], gen_b[0]
                    rms_to_fm(P, memt, memt_b, xn, xn_b, memT, memT_b, st, st_b, junk, junk_b, ntok=2)
                    cx.op("dve", lambda en, s_=s_: en.memset(Vx[s_][:, :, :, 256:257], 1.0), w=[kv_b[s_]])
                    for blk in range(2):
                        wv, wv_b = wload(P, WXKV, 0, 8, blk * 512, 512)
                        for c in range(4):
                            ch = blk * 4 + c
                            pz, pz_b = mm_next(P)
                            for kc in range(8):
                                cx.op("pe", lambda en, kc=kc, c=c, pz=pz, wv=wv: en.matmul(
                                    pz[:, 0:NMEM], wv[:, kc, c * 128:(c + 1) * 128], memT[:, kc, 0:NMEM],
                                    start=(kc == 0), stop=(kc == 7)),
                                    r=[wv_b, memT_b], w=[pz_b], inc=(kc == 7))
                            e = ev_engine(P)
                            cx.op(e, copy_op(e, KxT[s_][:, ch, :], pz[:, 0:NMEM]), r=[pz_b], w=[kv_b[s_]])
                    for nb in range(2):
                        wv, wv_b = wload(P, WXKV, 0, 8, D + nb * 512, 512)
                        for mt in range(2):
                            pz, pz_b = mm_next(P)
                            for kc in range(8):
                                cx.op("pe", lambda en, kc=kc, mt=mt, pz=pz, wv=wv: en.matmul(
                                    pz[:], memT[:, kc, mt * 128:(mt + 1) * 128], wv[:, kc, :], start=(kc == 0), stop=(kc == 7)),
                                    r=[wv_b, memT_b], w=[pz_b], inc=(kc == 7))
                            e = ev_engine(P)
                            cx.op(e, copy_op(e, Vx[s_][:, mt, nb * 2:nb * 2 + 2, 0:256],
                                             pz[:].rearrange("p (h d) -> p h d", h=2)), r=[pz_b], w=[kv_b[s_]])

                def load_x1(ti):
                    s_, t0_ = tiles[ti]
                    cx.dma("sp", xts[ti % 2][:], sc[s_]["x1"][t0_:t0_ + T, :].rearrange("(j p) d -> p j d", p=128),
                           w=xts_b[ti % 2])

                def fm_load(src, t0_, q="sp"):
                    i = cnt3["fm"] % 3
                    cnt3["fm"] += 1
                    cx.dma(q, fmr[i][:], src[:, :, t0_:t0_ + T].rearrange("c p n -> p c n"), w=[fmr_b[i]])
                    return fmr[i], fmr_b[i]

                def tf_next():
                    i = cnt3["tf"] % 4
                    cnt3["tf"] += 1
                    return tf[i], tf_b[i]

                def prefetch_fm(ti_, q="sp"):
                    s_, t0_ = tiles[ti_]
                    SC_ = sc[s_]
                    hm, hm_b = fm_load(SC_["hmT"], t0_, q)
                    om, om_b = fm_load(SC_["omT"], t0_, q)
                    hg, hg_b = gen[0], gen_b[0]
                    cx.op("dve", lambda en: en.tensor_tensor(hg[:], hm[:], om[:], ALU.mult), r=[hm_b, om_b], w=[hg_b])
                    od, od_b = fm_load(SC_["odT"], t0_, q)
                    ga, ga_b = fm_load(SC_["gaT"], t0_, q)
                    gb, gb_b = fm_load(SC_["gbT"], t0_, q)
                    return (hg, hg_b, od, od_b, ga, ga_b, gb, gb_b)

                load_x1(0)
                pre = prefetch_fm(0)
                for ti, (s, t0) in enumerate(tiles):
                    xt, xt_b = xts[ti % 2], xts_b[ti % 2]
                    SC = sc[s]
                    (hg, hg_b, od, od_b, ga, ga_b, gb, gb_b) = pre
                    mg, mg_b = gen[1], gen_b[1]
                    for blk in range(2):
                        wa, wa_b = wload(P, WA, 0, 8, blk * 512, 512)
                        wb, wb_b = wload(P, WB, 0, 8, blk * 512, 512)
                        for c in range(4):
                            dch = blk * 4 + c
                            pa, pa_b = mm_next(P)
                            for kc in range(8):
                                cx.op("pe", lambda en, kc=kc, c=c, pa=pa, wa=wa: en.matmul(
                                    pa[:], wa[:, kc, c * 128:(c + 1) * 128], hg[:, kc, :], start=(kc == 0), stop=(kc == 7)),
                                    r=[wa_b, hg_b], w=[pa_b], inc=(kc == 7))
                            ta, ta_b = tf_next()
                            cx.op("dve", lambda en, pa=pa, ta=ta, dch=dch: en.tensor_tensor(ta[:], pa[:], ga[:, dch, :], ALU.mult),
                                  r=[pa_b, ga_b], w=[ta_b])
                            pb, pb_b = mm_next(P)
                            for kc in range(8):
                                cx.op("pe", lambda en, kc=kc, c=c, pb=pb, wb=wb: en.matmul(
                                    pb[:], wb[:, kc, c * 128:(c + 1) * 128], od[:, kc, :], start=(kc == 0), stop=(kc == 7)),
                                    r=[wb_b, od_b], w=[pb_b], inc=(kc == 7))
                            tb, tb_b = tf_next()
                            cx.op("dve", lambda en, pb=pb, tb=tb, dch=dch: en.tensor_tensor(tb[:], pb[:], gb[:, dch, :], ALU.mult),
                                  r=[pb_b, gb_b], w=[tb_b])
                            cx.op("dve", lambda en, ta=ta, tb=tb, dch=dch: en.tensor_tensor(mg[:, dch, :], ta[:], tb[:], ALU.add),
                                  r=[ta_b, tb_b], w=[mg_b])
                    xnT, xnT_b = gen[0], gen_b[0]
                    tm_linear_add_norm(WMO, mg, mg_b, xt, xt_b, xnT, xnT_b)
                    qxT, qxT_b = gen[1], gen_b[1]
                    for blk in range(2):
                        wv, wv_b = wload(P, WXQ, 0, 8, blk * 512, 512)
                        for c in range(4):
                            ch = blk * 4 + c
                            pz, pz_b = mm_next(P)
                            for kc in range(8):
                                cx.op("pe", lambda en, kc=kc, c=c, pz=pz, wv=wv: en.matmul(
                                    pz[:], wv[:, kc, c * 128:(c + 1) * 128], xnT[:, kc, :], start=(kc == 0), stop=(kc == 7)),
                                    r=[wv_b, xnT_b], w=[pz_b], inc=(kc == 7))
                            e = ev_engine(P)
                            cx.op(e, copy_op(e, qxT[:, ch, :], pz[:]), r=[pz_b], w=[qxT_b])
                    def xa_scores(hx):
                        eb = hx % 2
                        for mt in range(2):
                            pz, pz_b = mm_next(P)
                            for dc in range(2):
                                cx.op("pe", lambda en, dc=dc, mt=mt, pz=pz: en.matmul(
                                    pz[:], KxT[s][:, hx * 2 + dc, mt * 128:(mt + 1) * 128], qxT[:, hx * 2 + dc, :],
                                    start=(dc == 0), stop=(dc == 1)),
                                    r=[kv_b[s], qxT_b], w=[pz_b], inc=(dc == 1))
                            cx.op("act", lambda en, mt=mt, pz=pz: en.activation(out=ETt[eb][:, mt, :], in_=pz[:], func=AF.Exp,
                                                                               scale=1.0 / 16.0),
                                  r=[pz_b], w=[ET_b[eb]])

                    def xa_out(hx):
                        eb = hx % 2
                        for j in range(4):
                            po, po_b = mm_next(P)
                            for mt in range(2):
                                cx.op("pe", lambda en, mt=mt, j=j, po=po: en.matmul(
                                    po[:, 0:257], ETt[eb][:, mt, j * 128:(j + 1) * 128], Vx[s][:, mt, hx, :],
                                    start=(mt == 0), stop=(mt == 1)),
                                    r=[ET_b[eb], kv_b[s]], w=[po_b], inc=(mt == 1))
                            cx.op("dve", lambda en, po=po, j=j: en.reciprocal(rcx[:, hx * 4 + j:hx * 4 + j + 1], po[:, 256:257]),
                                  r=[po_b], w=[rcx_b])
                            cx.op("act", lambda en, po=po, j=j: en.activation(
                                out=ox[:, j, hx * 256:(hx + 1) * 256], in_=po[:, 0:256], func=AF.Copy,
                                scale=rcx[:, hx * 4 + j:hx * 4 + j + 1]),
                                r=[po_b, rcx_b], w=[ox_b[j]])

                    xa_scores(0)
                    for hx in range(4):
                        if hx + 1 < 4:
                            xa_scores(hx + 1)
                        xa_out(hx)
                    oxT, oxT_b = gen[0], gen_b[0]
                    tm_to_fm(ox, ox_b, oxT, oxT_b)
                    xnT2, xnT2_b = gen[1], gen_b[1]
                    tm_linear_add_norm(WXO, oxT, oxT_b, xt, xt_b, xnT2, xnT2_b)
                    if ti + 1 < len(tiles):
                        load_x1(ti + 1)
                        pre = prefetch_fm(ti + 1, q="act")
                    ffn(P, W2, W2o, xnT2, xnT2_b, hT, hT_b, sg, sg_b, xt, xt_b)
                    for j in range(4):
                        cx.op("act", lambda en, j=j: en.activation(out=junk[:], in_=xt[:, j, :], func=AF.Square,
                                                                   accum_out=st[:, j:j + 1]),
                              r=[xt_b[j]], w=[junk_b, st_b])
                    cx.op("act", lambda en: en.activation(out=st[:, 4:8], in_=st[:, 0:4], func=AF.Sqrt, scale=1.0 / D, bias=EPS),
                          r=[st_b], w=[st_b])
                    cx.op("dve", lambda en: en.reciprocal(st[:, 8:12], st[:, 4:8]), r=[st_b], w=[st_b])
                    for j in range(4):
                        cx.op("dve", lambda en, j=j: en.scalar_tensor_tensor(
                            xt[:, j, :], xt[:, j, :], st[:, 8 + j:9 + j], fgain[:], ALU.mult, ALU.mult),
                            r=[xt_b[j], st_b, b_const], w=[xt_b[j]])
                    cx.dma("pool", y_out[s][t0:t0 + T, :].rearrange("(j p) d -> p j d", p=128), xt[:], r=xt_b)
                cx.barrier()
        cx.barrier()
    return nc


def _consts():
    c = {}
    c["c_ident"] = np.eye(128, dtype=np.float32)
    j = np.arange(128)[:, None]
    i = np.arange(128)[None, :]
    c["c_masks"] = np.concatenate([(j <= i), (j >= i)], axis=1).astype(np.float32)
    inv_freq = (np.float32(500000.0) ** (-(np.arange(0, 16, 2, dtype=np.float32) / np.float32(16)))).astype(np.float32)
    rope = np.zeros((128, 2), np.float32)
    for p in range(128):
        f = p % 64
        if f < 16:
            rope[p, 0] = inv_freq[f % 8]
    c["c_rope"] = rope
    c["c_pos"] = np.tile(np.arange(T, dtype=np.float32)[None, :], (128, 1))
    ab = np.zeros((8, 2), np.float32)
    ab[0:4, 0] = 1.0
    ab[4:8, 0] = -1.0
    ab[4:8, 1] = 1.0
    c["c_ab"] = ab
    rot = np.zeros((128, 128), np.float32)
    for b in (0, 64):
        for f in range(8):
            rot[b + f + 8, b + f] = -1.0
            rot[b + f, b + f + 8] = 1.0
    c["c_rot"] = rot
    return c


def make_in_maps(inputs, S_LIST, n_cores, seq_of_core):
    f = lambda a: np.ascontiguousarray(np.asarray(a, dtype=np.float32))
    gains = [inputs[k] for k in ("ffn1_norm", "mix_norm", "mlstm_norm", "diff_norm", "xattn_norm", "mem_norm", "ffn2_norm")]
    gcols = np.concatenate([f(g).reshape(8, 128).T for g in gains], axis=1)
    shared = {
        "ffn1_w_in": f(inputs["ffn1_w_in"])[0], "ffn1_w_out": f(inputs["ffn1_w_out"])[0],
        "w_mix_in": f(inputs["w_mix_in"])[0], "w_branch_a": f(inputs["w_branch_a"])[0],
        "w_branch_b": f(inputs["w_branch_b"])[0], "w_mix_out": f(inputs["w_mix_out"])[0],
        "w_xq": f(inputs["w_xq"])[0], "w_xkv": f(inputs["w_xkv"])[0], "w_xo": f(inputs["w_xo"])[0],
        "ffn2_w_in": f(inputs["ffn2_w_in"])[0], "ffn2_w_out": f(inputs["ffn2_w_out"])[0],
        "gcols": np.ascontiguousarray(gcols),
        "final_norm": f(inputs["final_norm"]).reshape(D),
        "lam_vecs": np.concatenate([f(inputs[k]).reshape(64) for k in ("lambda_q1", "lambda_k1", "lambda_q2", "lambda_k2")]),
        "gate_bias": np.ascontiguousarray(np.stack([f(inputs["b_igate"]).reshape(8), f(inputs["b_fgate"]).reshape(8)], axis=1)),
    }
    shared.update(_consts())
    maps = []
    for c in range(n_cores):
        m = dict(shared)
        xs, ms = seq_of_core(c)
        for s in range(len(S_LIST)):
            m["x%d" % s] = np.ascontiguousarray(xs[s])
            m["mem%d" % s] = np.ascontiguousarray(ms[s])
        maps.append(m)
    return maps


def kernel(**inputs):
    S_LIST = (2048, 8192)
    nc = build(S_LIST)
    xp, xs_ = np.asarray(inputs["x_prompt"]), np.asarray(inputs["x_sample"])
    mp, ms_ = np.asarray(inputs["mem_prompt"]), np.asarray(inputs["mem_sample"])
    maps = make_in_maps(inputs, S_LIST, 8, lambda c: ((xp[c], xs_[c]), (mp[c], ms_[c])))
    res = run_bass_kernel_spmd(nc, maps, core_ids=list(range(8)))
    y_p = np.stack([res.results[c]["y0"] for c in range(8)], axis=0).astype(np.float32)
    y_s = np.stack([res.results[c]["y1"] for c in range(8)], axis=0).astype(np.float32)
    return (y_p, y_s)
```

```python
import contextlib
import math
import numpy as np
import concourse.bass as bass
import concourse.mybir as mybir
from concourse.bass_utils import run_bass_kernel_spmd

F32 = mybir.dt.float32
BF16 = mybir.dt.bfloat16
I32 = mybir.dt.int32
ALU = mybir.AluOpType
AF = mybir.ActivationFunctionType
AX = mybir.AxisListType

D = 1024
DFF = 2816
NMEM = 256
EPS = 1e-6
LAMBDA_INIT = 0.8 - 0.6 * math.exp(-0.3 * 0)
T = 512
NWS = 6
O_QM, O_KM, O_VM, O_OM, O_IG, O_FG, O_QD, O_KD, O_VD, O_GA, O_GB = (
    0, 512, 1024, 2048, 3072, 3080, 3088, 4112, 5136, 6160, 7184)
NMIX = 8208
C_QM, C_KM, C_QD, C_KD, C_GA, C_GB, C_OM = 0, 4, 8, 16, 24, 32, 40
NFC = 48
NF = NFC * 128 + 16
NT_ = 2560
TWO_PI = 2.0 * math.pi
CW1 = 6.28125
CW2 = TWO_PI - CW1


class Buf:
    __slots__ = ("name", "w", "r")

    def __init__(self, name):
        self.name = name
        self.w = {}
        self.r = {}


class Ctx:
    def __init__(self, nc, es):
        self.nc = nc
        self.E = {"pe": nc.tensor, "act": nc.scalar, "dve": nc.vector, "pool": nc.gpsimd, "sp": nc.sync}
        self.semh = {}
        self.cnt = {}
        self.known = {}
        for e in self.E:
            self.semh[e] = es.enter_context(nc.semaphore("s_" + e))
            self.cnt[e] = 0
            self.known[e] = {}
        self.dq = {}
        for q, n in (("sp", 16), ("pool", 16), ("act", 6)):
            keys = []
            for i in range(n):
                k = (q, i)
                self.semh[k] = es.enter_context(nc.semaphore("d_%s%d" % (q, i)))
                self.cnt[k] = 0
                keys.append(k)
            self.dq[q] = [keys, 0]

    def _waits(self, e, need):
        eng = self.E[e]
        kn = self.known[e]
        for k, v in need.items():
            if k == e:
                if e == "pe":
                    continue
                v = min(v, self.cnt[e])
            if kn.get(k, 0) < v:
                eng.wait_ge(self.semh[k], v)
                kn[k] = v

    @staticmethod
    def _need(r, w):
        need = {}
        for b in r:
            for k, v in b.w.items():
                if need.get(k, 0) < v:
                    need[k] = v
        for b in w:
            for k, v in b.w.items():
                if need.get(k, 0) < v:
                    need[k] = v
            for k, v in b.r.items():
                if need.get(k, 0) < v:
                    need[k] = v
        return need

    def op(self, e, fn, r=(), w=(), inc=True):
        self._waits(e, self._need(r, w))
        ins = fn(self.E[e])
        if inc:
            self.cnt[e] += 1
            ins.then_inc(self.semh[e], 1)
            stamp = self.cnt[e]
        else:
            stamp = self.cnt[e] + 1
        for b in w:
            b.w = {e: stamp}
            b.r = {}
        for b in r:
            if b.r.get(e, 0) < stamp:
                b.r[e] = stamp
        return ins

    def dma(self, q, out, in_, r=(), w=(), merge=False, **kw):
        need = self._need(r, w)
        keys, i = self.dq[q]
        k = keys[i % len(keys)]
        self.dq[q][1] = i + 1
        if need.get(k, 0) < self.cnt[k]:
            need[k] = self.cnt[k]
        self._waits(q, need)
        ins = self.E[q].dma_start(out=out, in_=in_, **kw)
        ins.then_inc(self.semh[k], 16)
        self.cnt[k] += 16
        v = self.cnt[k]
        for b in w:
            if merge:
                b.w[k] = v
            else:
                b.w = {k: v}
                b.r = {}
        for b in r:
            if b.r.get(k, 0) < v:
                b.r[k] = v
        return ins

    def barrier(self, engines=None):
        for e in (engines or self.E):
            need = {k: v for k, v in self.cnt.items() if k != e and v > 0}
            self._waits(e, need)


def build(S_LIST=(2048, 8192), debug=False, stop_after=99):
    nc = bass.Bass("TRN2", target_bir_lowering=False)
    NSEQ = len(S_LIST)
    dbg_kind = "ExternalOutput" if debug else "Internal"

    def din(name, shape, dt=F32):
        return nc.dram_tensor(name, list(shape), dt, kind="ExternalInput").ap()

    def dscr(name, shape, dt=BF16, dbg=False):
        return nc.dram_tensor(name, list(shape), dt, kind=(dbg_kind if dbg else "Internal")).ap()

    x_in = [din("x%d" % s, [S_LIST[s], D]) for s in range(NSEQ)]
    mem_in = [din("mem%d" % s, [NMEM, D]) for s in range(NSEQ)]
    y_out = [nc.dram_tensor("y%d" % s, [S_LIST[s], D], F32, kind="ExternalOutput").ap() for s in range(NSEQ)]
    w_ffn1_in = din("ffn1_w_in", [D, 2 * DFF])
    w_ffn1_out = din("ffn1_w_out", [DFF, D])
    w_mix_in = din("w_mix_in", [D, NMIX])
    w_br_a = din("w_branch_a", [D, D])
    w_br_b = din("w_branch_b", [D, D])
    w_mix_out = din("w_mix_out", [D, D])
    w_xq = din("w_xq", [D, D])
    w_xkv = din("w_xkv", [D, 2 * D])
    w_xo = din("w_xo", [D, D])
    w_ffn2_in = din("ffn2_w_in", [D, 2 * DFF])
    w_ffn2_out = din("ffn2_w_out", [DFF, D])
    gcols_in = din("gcols", [128, 7 * 8])
    final_gain = din("final_norm", [D])
    lam_in = din("lam_vecs", [4 * 64])
    gate_bias_in = din("gate_bias", [8, 2])
    c_ident = din("c_ident", [128, 128])
    c_masks = din("c_masks", [128, 256])
    c_rope = din("c_rope", [128, 2])
    c_pos = din("c_pos", [128, T])
    c_ab = din("c_ab", [8, 2])
    c_rot = din("c_rot", [128, 128])

    W1 = dscr("W1", [D, 2 * DFF])
    W1o = dscr("W1o", [DFF, D])
    W2 = dscr("W2", [D, 2 * DFF])
    W2o = dscr("W2o", [DFF, D])
    WF = dscr("WF", [D, NF], dbg=True)
    WT = dscr("WT", [D, NT_], dbg=True)
    WA = dscr("WA", [D, D])
    WB = dscr("WB", [D, D])
    WMO = dscr("WMO", [D, D])
    WXQ = dscr("WXQ", [D, D])
    WXKV = dscr("WXKV", [D, 2 * D])
    WXO = dscr("WXO", [D, D])
    sc = []
    for s in range(NSEQ):
        S = S_LIST[s]
        nch = S // 128
        sc.append(dict(
            x1=dscr("x1_%d" % s, [S, D], F32, dbg=True),
            qmT=dscr("qmT_%d" % s, [4, 128, S], dbg=True),
            kmT=dscr("kmT_%d" % s, [4, 128, S], dbg=True),
            qdT=dscr("qdT_%d" % s, [8, 128, S], dbg=True),
            kdT=dscr("kdT_%d" % s, [8, 128, S], dbg=True),
            gaT=dscr("gaT_%d" % s, [8, 128, S], dbg=True),
            gbT=dscr("gbT_%d" % s, [8, 128, S], dbg=True),
            omT=dscr("omT_%d" % s, [8, 128, S], dbg=True),
            gig=dscr("gig_%d" % s, [8, S], F32, dbg=True),
            gfg=dscr("gfg_%d" % s, [8, S], F32, dbg=True),
            km=dscr("km_%d" % s, [S, 512], dbg=True),
            vm=dscr("vm_%d" % s, [S, 1024], dbg=True),
            vd=dscr("vd_%d" % s, [S, 1024], dbg=True),
            dec=dscr("dec_%d" % s, [8 * nch], F32, dbg=True),
            hmT=dscr("hmT_%d" % s, [8, 128, S], dbg=True),
            odT=dscr("odT_%d" % s, [8, 128, S], dbg=True),
        ))

    top = contextlib.ExitStack()
    with top:
        cx = Ctx(nc, top)

        uid = {"n": 0}

        def sb(es, name, shape, dt):
            uid["n"] += 1
            return es.enter_context(nc.sbuf_tensor("sb%d_%s" % (uid["n"], name), list(shape), dt))

        def pst(es, name, shape, dt):
            uid["n"] += 1
            return es.enter_context(nc.psum_tensor("ps%d_%s" % (uid["n"], name), list(shape), dt))

        ident = sb(top, "ident", [128, 128], F32)
        ident16 = sb(top, "ident16", [128, 128], BF16)
        rot32 = sb(top, "rot32", [128, 128], F32)
        rot16 = sb(top, "rot16", [128, 128], BF16)
        masks = sb(top, "masks", [128, 256], F32)
        gcols = sb(top, "gcols", [128, 56], F32)
        ropec = sb(top, "ropec", [128, 2], F32)
        posrow = sb(top, "posrow", [128, T], F32)
        abc = sb(top, "abc", [8, 2], F32)
        gbias = sb(top, "gbias", [8, 2], F32)
        lamv = sb(top, "lamv", [128, 256], F32)
        lamc = sb(top, "lamc", [128, 4], F32)
        fgain = sb(top, "fgain", [128, D], F32)
        b_const = Buf("const")
        for dst, src in ((ident, c_ident), (masks, c_masks), (gcols, gcols_in), (ropec, c_rope),
                         (posrow, c_pos), (abc, c_ab), (gbias, gate_bias_in), (rot32, c_rot)):
            cx.dma("sp", dst[:], src[:, :], w=[b_const], merge=True)
        cx.dma("sp", lamv[:], lam_in.partition_broadcast(128), w=[b_const], merge=True)
        cx.dma("sp", fgain[:], final_gain.partition_broadcast(128), w=[b_const], merge=True)
        b_c2 = Buf("const2")
        cx.op("dve", lambda e: e.tensor_copy(ident16[:], ident[:]), r=[b_const], w=[b_c2])
        cx.op("dve", lambda e: e.tensor_copy(rot16[:], rot32[:]), r=[b_const, b_c2], w=[b_c2])
        b_l = Buf("lam")
        cx.op("dve", lambda e: e.tensor_tensor(lamv[:, 0:64], lamv[:, 0:64], lamv[:, 64:128], ALU.mult),
              r=[b_const], w=[b_l])
        cx.op("dve", lambda e: e.tensor_tensor(lamv[:, 128:192], lamv[:, 128:192], lamv[:, 192:256], ALU.mult),
              r=[b_l], w=[b_l])
        cx.op("dve", lambda e: e.tensor_reduce(lamc[:, 0:1], lamv[:, 0:64], AX.X, ALU.add), r=[b_l], w=[b_l])
        cx.op("dve", lambda e: e.tensor_reduce(lamc[:, 1:2], lamv[:, 128:192], AX.X, ALU.add), r=[b_l], w=[b_l])
        cx.op("act", lambda e: e.activation(out=lamc[:, 0:2], in_=lamc[:, 0:2], func=AF.Exp), r=[b_l], w=[b_l])
        cx.op("dve", lambda e: e.tensor_tensor(lamc[:, 2:3], lamc[:, 0:1], lamc[:, 1:2], ALU.subtract),
              r=[b_l], w=[b_l])
        cx.op("dve", lambda e: e.tensor_scalar(lamc[:, 3:4], lamc[:, 2:3], LAMBDA_INIT, -1.0, ALU.add, ALU.mult),
              r=[b_l], w=[b_l])
        NEG_LAM = lamc[:, 3:4]

        with contextlib.ExitStack() as ph:
            CW = 2048
            wl = [sb(ph, "wl%d" % i, [128, 4112], F32) for i in range(3)]
            wo = [sb(ph, "wo%d" % i, [128, 5136], BF16) for i in range(3)]
            wl_b = [Buf("wl%d" % i) for i in range(3)]
            wo_b = [Buf("wo%d" % i) for i in range(3)]
            state = {"i": 0, "e": 0}

            def conv_engine():
                state["e"] += 1
                return "act" if state["e"] % 2 else "dve"

            def scale_op(dst_ap, src_ap, gcol, const, rb, wb, eng=None):
                e = eng or conv_engine()
                if gcol is None:
                    if e == "act":
                        cx.op("act", lambda en: en.activation(out=dst_ap, in_=src_ap, func=AF.Copy, scale=float(const)),
                              r=rb, w=wb)
                    else:
                        cx.op("dve", lambda en: en.tensor_scalar(dst_ap, src_ap, float(const), None, ALU.mult),
                              r=rb, w=wb)
                elif const == 1.0 and e == "act":
                    cx.op("act", lambda en: en.activation(out=dst_ap, in_=src_ap, func=AF.Copy, scale=gcol),
                          r=rb, w=wb)
                else:
                    cx.op("dve", lambda en: en.tensor_scalar(dst_ap, src_ap, gcol, float(const), ALU.mult, ALU.mult),
                          r=rb, w=wb)

            def convert_plain(src, dst, K, N, gidx=None, const=1.0):
                for kc in range(K // 128):
                    for c0 in range(0, N, CW):
                        cw = min(CW, N - c0)
                        i = state["i"] % 3
                        state["i"] += 1
                        cx.dma("sp", wl[i][:, 0:cw], src[kc * 128:(kc + 1) * 128, c0:c0 + cw], w=[wl_b[i]])
                        g = None if gidx is None else gcols[:, gidx * 8 + kc:gidx * 8 + kc + 1]
                        scale_op(wo[i][:, 0:cw], wl[i][:, 0:cw], g, const, [wl_b[i], b_const], [wo_b[i]])
                        cx.dma("pool", dst[kc * 128:(kc + 1) * 128, c0:c0 + cw], wo[i][:, 0:cw], r=[wo_b[i]])

            convert_plain(w_ffn1_in, W1, D, 2 * DFF, gidx=0)
            convert_plain(w_ffn1_out, W1o, DFF, D)
            for kc in range(8):
                g = gcols[:, 8 + kc:8 + kc + 1]
                halves = []
                for hf in range(2):
                    i = state["i"] % 3
                    state["i"] += 1
                    c0 = hf * 4112
                    cwid = 4112 if hf == 0 else NMIX - 4112
                    cx.dma("sp", wl[i][:, 0:cwid], w_mix_in[kc * 128:(kc + 1) * 128, c0:c0 + cwid], w=[wl_b[i]])
                    halves.append(i)
                ia, ib = halves
                io = state["i"] % 3
                A_, B_ = wl[ia], wl[ib]
                rA, rB = [wl_b[ia], b_const], [wl_b[ib], b_const]
                OF1, OF2, OT = wo[0], wo[1], wo[2]
                bF1, bF2, bT = wo_b[0], wo_b[1], wo_b[2]
                kscale = 128.0 ** -0.5
                SPL = 3072
                scale_op(OF1[:, 0:512], A_[:, O_QM:O_QM + 512], g, 1.0, rA, [bF1])
                scale_op(OF1[:, 512:1024], A_[:, O_KM:O_KM + 512], g, kscale, rA, [bF1])
                scale_op(OF1[:, C_QD * 128:C_QD * 128 + 1024], A_[:, O_QD:O_QD + 1024], g, 1.0, rA, [bF1])
                scale_op(OF1[:, C_KD * 128:C_KD * 128 + 1024], B_[:, O_KD - 4112:O_KD - 4112 + 1024], g, 1.0, rB, [bF1])
                scale_op(OF2[:, C_GA * 128 - SPL:C_GA * 128 - SPL + 1024], B_[:, O_GA - 4112:O_GA - 4112 + 1024],
                         g, 1.0, rB, [bF2])
                scale_op(OF2[:, C_GB * 128 - SPL:C_GB * 128 - SPL + 1024], B_[:, O_GB - 4112:O_GB - 4112 + 1024],
                         g, 1.0, rB, [bF2])
                scale_op(OF2[:, C_OM * 128 - SPL:C_OM * 128 - SPL + 1024], A_[:, O_OM:O_OM + 1024], g, 1.0, rA, [bF2])
                scale_op(OF2[:, NFC * 128 - SPL:NFC * 128 + 16 - SPL], A_[:, O_IG:O_IG + 16], g, 1.0, rA, [bF2], eng="dve")
                scale_op(OT[:, 0:512], A_[:, O_KM:O_KM + 512], g, kscale, rA, [bT])
                scale_op(OT[:, 512:1536], A_[:, O_VM:O_VM + 1024], g, 1.0, rA, [bT])
                scale_op(OT[:, 1536:2560], B_[:, O_VD - 4112:O_VD - 4112 + 1024], g, 1.0, rB, [bT])
                rows = slice(kc * 128, (kc + 1) * 128)
                cx.dma("pool", WF[rows, 0:SPL], OF1[:, 0:SPL], r=[bF1])
                cx.dma("pool", WF[rows, SPL:NF], OF2[:, 0:NF - SPL], r=[bF2])
                cx.dma("pool", WT[rows, :], OT[:, 0:NT_], r=[bT])
            cx.barrier()

        BG_LIST = [(w_br_a, WA, D, D, 2, 1.0), (w_br_b, WB, D, D, 3, 1.0 - LAMBDA_INIT), (w_mix_out, WMO, D, D, None, 1.0),
                   (w_xq, WXQ, D, D, 4, 1.0), (w_xkv, WXKV, D, 2 * D, 5, 1.0), (w_xo, WXO, D, D, None, 1.0),
                   (w_ffn2_in, W2, D, 2 * DFF, 6, 1.0), (w_ffn2_out, W2o, DFF, D, None, 1.0)]

        def bg_convert(bwl, bwl_b, bwo, bwo_b):
            NB = len(bwl)
            pieces = []
            for (src, dst, K, N, gidx, const) in BG_LIST:
                for kc in range(K // 128):
                    for c0 in range(0, N, 512):
                        pieces.append((src, dst, kc, c0, min(512, N - c0), gidx, const))
            npc = len(pieces)
            for n in range(npc + 4):
                if n < npc:
                    (src, dst, kc, c0, cw, gidx, const) = pieces[n]
                    i = n % NB
                    cx.dma("sp", bwl[i][:, 0:cw], src[kc * 128:(kc + 1) * 128, c0:c0 + cw], w=[bwl_b[i]])
                m = n - 2
                if 0 <= m < npc:
                    (src, dst, kc, c0, cw, gidx, const) = pieces[m]
                    i = m % NB
                    g = None if gidx is None else gcols[:, gidx * 8 + kc:gidx * 8 + kc + 1]
                    scale_op(bwo[i][:, 0:cw], bwl[i][:, 0:cw], g, const, [bwl_b[i], b_const], [bwo_b[i]])
                m = n - 4
                if 0 <= m < npc:
                    (src, dst, kc, c0, cw, gidx, const) = pieces[m]
                    i = m % NB
                    cx.dma("pool", dst[kc * 128:(kc + 1) * 128, c0:c0 + cw], bwo[i][:, 0:cw], r=[bwo_b[i]])
                yield

        tiles = [(s, t0) for s in range(NSEQ) for t0 in range(0, S_LIST[s], T)]

        def rowlocal_pools(ph):
            P = {}
            P["ws"] = [sb(ph, "ws%d" % i, [128, 8, 512], BF16) for i in range(NWS)]
            P["ws_b"] = [Buf("ws%d" % i) for i in range(NWS)]
            P["wsi"] = 0
            P["mm"] = [pst(ph, "mm%d" % i, [128, 512], F32) for i in range(6)]
            P["mm_b"] = [Buf("mm%d" % i) for i in range(6)]
            P["mmi"] = 0
            trp = [pst(ph, "trp%d" % i, [128, 1024], BF16) for i in range(2)]
            P["tr"] = [trp[0][:, :], trp[1][:, :]]
            P["tr_b"] = [Buf("tr0"), Buf("tr1")]
            P["tri"] = 0
            P["ev"] = 0
            return P

        def wload(P, Wd, k0, nk, c0, ncol):
            i = P["wsi"] % NWS
            P["wsi"] += 1
            cx.dma("sp", P["ws"][i][:, 0:nk, 0:ncol],
                   Wd[k0 * 128:(k0 + nk) * 128, c0:c0 + ncol].rearrange("(k p) n -> p k n", p=128),
                   w=[P["ws_b"][i]])
            return P["ws"][i], P["ws_b"][i]

        def mm_next(P):
            i = P["mmi"] % 6
            P["mmi"] += 1
            return P["mm"][i], P["mm_b"][i]

        def tr_next(P):
            i = P["tri"] % 2
            P["tri"] += 1
            return P["tr"][i], P["tr_b"][i]

        def ev_engine(P):
            P["ev"] += 1
            return "act" if P["ev"] % 2 else "dve"

        def copy_op(e, out, in_):
            if e == "act":
                return lambda en: en.activation(out=out, in_=in_, func=AF.Copy)
            return lambda en: en.tensor_copy(out, in_)

        def rms_to_fm(P, xt, xt_b, xn, xn_b, xnT, xnT_b, st, st_b, junk, junk_b, ntok=4, width=D):
            for j in range(ntok):
                cx.op("act", lambda en, j=j: en.activation(out=junk[:, 0:width], in_=xt[:, j, :], func=AF.Square,
                                                           accum_out=st[:, j:j + 1]),
                      r=[xt_b[j]], w=[junk_b, st_b])
            cx.op("act", lambda en: en.activation(out=st[:, 4:4 + ntok], in_=st[:, 0:ntok], func=AF.Ln,
                                                  scale=1.0 / width, bias=EPS), r=[st_b], w=[st_b])
            cx.op("act", lambda en: en.activation(out=st[:, 8:8 + ntok], in_=st[:, 4:4 + ntok], func=AF.Exp, scale=-0.5),
                  r=[st_b], w=[st_b])
            nkc = width // 128
            for j in range(ntok):
                xb = xn_b[j]
                if j % 2 == 0:
                    cx.op("act", lambda en, j=j: en.activation(out=xn[:, j, :], in_=xt[:, j, :], func=AF.Copy,
                                                               scale=st[:, 8 + j:9 + j]),
                          r=[xt_b[j], st_b], w=[xb])
                else:
                    cx.op("dve", lambda en, j=j: en.tensor_scalar(xn[:, j, :], xt[:, j, :], st[:, 8 + j:9 + j], None,
                                                                  ALU.mult),
                          r=[xt_b[j], st_b], w=[xb])
                tp, tp_b = tr_next(P)
                for kc in range(nkc):
                    cx.op("pe", lambda en, j=j, kc=kc, tp=tp: en.transpose(
                        tp[:, kc * 128:(kc + 1) * 128], xn[:, j, kc * 128:(kc + 1) * 128], ident16[:]),
                        r=[xb, b_c2], w=[tp_b], inc=(kc == nkc - 1))
                e = ev_engine(P)
                cx.op(e, copy_op(e, xnT[:, 0:nkc, j * 128:(j + 1) * 128],
                                 tp[:, 0:nkc * 128].rearrange("p (k n) -> p k n", k=nkc)), r=[tp_b], w=[xnT_b])

        def ffn(P, Win, Wout, xnT, xnT_b, hT, hT_b, sg, sg_b, xt, xt_b, mid=None, step=None):
            for blk in range(6):
                if step is not None:
                    step()
                nchk = 4 if blk < 5 else 2
                wg, wg_b = wload(P, Win, 0, 8, blk * 512, nchk * 128)
                wu, wu_b = wload(P, Win, 0, 8, DFF + blk * 512, nchk * 128)
                for c in range(nchk):
                    fc = blk * 4 + c
                    pg, pg_b = mm_next(P)
                    for kc in range(8):
                        cx.op("pe", lambda en, kc=kc, c=c, pg=pg, wg=wg: en.matmul(
                            pg[:], wg[:, kc, c * 128:(c + 1) * 128], xnT[:, kc, :], start=(kc == 0), stop=(kc == 7)),
                            r=[wg_b, xnT_b], w=[pg_b], inc=(kc == 7))
                    si = fc % 2
                    cx.op("act", lambda en, pg=pg, si=si: en.activation(out=sg[si][:], in_=pg[:], func=AF.Silu),
                          r=[pg_b], w=[sg_b[si]])
                    pu, pu_b = mm_next(P)
                    for kc in range(8):
                        cx.op("pe", lambda en, kc=kc, c=c, pu=pu, wu=wu: en.matmul(
                            pu[:], wu[:, kc, c * 128:(c + 1) * 128], xnT[:, kc, :], start=(kc == 0), stop=(kc == 7)),
                            r=[wu_b, xnT_b], w=[pu_b], inc=(kc == 7))
                    cx.op("dve", lambda en, pu=pu, si=si, fc=fc: en.tensor_tensor(hT[:, fc, :], pu[:], sg[si][:], ALU.mult),
                          r=[pu_b, sg_b[si]], w=[hT_b[0 if fc < 8 else (1 if fc < 16 else 2)]])
            if mid is not None:
                mid()
            for nb in range(2):
                if step is not None:
                    step()
                accs = [mm_next(P) for _ in range(4)]
                for ksb, (k0, nk) in enumerate(((0, 8), (8, 8), (16, 6))):
                    wv, wv_b = wload(P, Wout, k0, nk, nb * 512, 512)
                    for j in range(4):
                        for k in range(nk):
                            fc = k0 + k
                            cx.op("pe", lambda en, j=j, k=k, fc=fc, wv=wv: en.matmul(
                                accs[j][0][:], hT[:, fc, j * 128:(j + 1) * 128], wv[:, k, :],
                                start=(fc == 0), stop=(fc == 21)),
                                r=[wv_b, hT_b[ksb]], w=[accs[j][1]], inc=(k == nk - 1))
                for j in range(4):
                    cx.op("dve", lambda en, j=j, nb=nb: en.scalar_tensor_tensor(
                        xt[:, j, nb * 512:(nb + 1) * 512], accs[j][0][:], 0.5, xt[:, j, nb * 512:(nb + 1) * 512],
                        ALU.mult, ALU.add), r=[accs[j][1], xt_b[j]], w=[xt_b[j]])

        if stop_after >= 1:
            with contextlib.ExitStack() as ph:
                P = rowlocal_pools(ph)
                xts = [sb(ph, "xt%d" % i, [128, 4, D], F32) for i in range(2)]
                xts_b = [[Buf("xt%d_%d" % (i, j_)) for j_ in range(4)] for i in range(2)]
                xn = sb(ph, "xn", [128, 4, D], BF16)
                xn_b = [Buf("xn%d" % j_) for j_ in range(4)]
                xnT = sb(ph, "xnT", [128, 8, T], BF16)
                xnT_b = Buf("xnT")
                xnT1 = [sb(ph, "xnT1_%d" % i, [128, 8, T], BF16) for i in range(2)]
                xnT1_b = [Buf("xnT1_0"), Buf("xnT1_1")]
                hT = sb(ph, "hT", [128, 22, T], BF16)
                hT_b = [Buf("hT%d" % k_) for k_ in range(3)]
                sg = [sb(ph, "sg%d" % i, [128, T], BF16) for i in range(2)]
                sg_b = [Buf("sg0"), Buf("sg1")]
                st = sb(ph, "st", [128, 16], F32)
                st_b = Buf("st")
                junk = sb(ph, "junk", [128, D], BF16)
                junk_b = Buf("junk")
                NST = 8
                stg = [sb(ph, "stg%d" % i, [128, T], BF16) for i in range(NST)]
                stg_b = [Buf("stg%d" % i) for i in range(NST)]
                sgt = [sb(ph, "sgt%d" % i, [8, T], F32) for i in range(2)]
                sgt_b = [Buf("sgt0"), Buf("sgt1")]
                ang = sb(ph, "ang", [128, T], F32)
                a2 = sb(ph, "a2", [128, T], F32)
                nf = sb(ph, "nf", [128, T], F32)
                ni = sb(ph, "ni", [128, T], I32)
                cosT = sb(ph, "cosT", [128, T], F32)
                sinT = sb(ph, "sinT", [128, T], F32)
                tab_b = Buf("tab")
                tmp_b = Buf("ropetmp")
                zq = [sb(ph, "zq_%d" % i, [128, T], BF16) for i in range(2)]
                zq_b = [Buf("zq0"), Buf("zq1")]
                r1 = [sb(ph, "r1_%d" % i, [128, T], F32) for i in range(2)]
                r2 = [sb(ph, "r2_%d" % i, [128, T], F32) for i in range(2)]
                r1_b = [Buf("r1_0"), Buf("r1_1")]
                r2_b = [Buf("r2_0"), Buf("r2_1")]
                stc = {"i": 0, "g": 0, "r": 0}
                bwl = [sb(ph, "bwl%d" % i, [128, 512], F32) for i in range(6)]
                bwo = [sb(ph, "bwo%d" % i, [128, 512], BF16) for i in range(6)]
                bg = {"g": bg_convert(bwl, [Buf("bwl%d" % i) for i in range(6)], bwo, [Buf("bwo%d" % i) for i in range(6)])}

                def bg_step(drain=False):
                    bg["n"] = bg.get("n", 0) + 1
                    if not drain and bg["n"] % 2 != 0:
                        return
                    while bg["g"] is not None:
                        try:
                            next(bg["g"])
                        except StopIteration:
                            bg["g"] = None
                        if not drain:
                            return

                def stage_next():
                    i = stc["i"] % NST
                    stc["i"] += 1
                    return stg[i], stg_b[i]

                def load_x(ti):
                    s, t0 = tiles[ti]
                    cx.dma("sp", xts[ti % 2][:], x_in[s][t0:t0 + T, :].rearrange("(j p) d -> p j d", p=128),
                           w=xts_b[ti % 2])

                def rope_tables(t0):
                    cx.op("dve", lambda en: en.tensor_scalar(ang[:], posrow[:], float(t0), ropec[:, 0:1], ALU.add, ALU.mult),
                          r=[b_const], w=[tmp_b])
                    for (ph_off, tab) in ((0.0, sinT), (0.5 * math.pi, cosT)):
                        cx.op("dve", lambda en: en.tensor_scalar(a2[:], ang[:], ph_off, None, ALU.add), r=[tmp_b], w=[tmp_b])
                        cx.op("dve", lambda en: en.tensor_scalar(nf[:], a2[:], 1.0 / TWO_PI, None, ALU.mult),
                              r=[tmp_b], w=[tmp_b])
                        cx.op("dve", lambda en: en.tensor_copy(ni[:], nf[:]), r=[tmp_b], w=[tmp_b])
                        cx.op("dve", lambda en: en.tensor_copy(nf[:], ni[:]), r=[tmp_b], w=[tmp_b])
                        cx.op("dve", lambda en: en.scalar_tensor_tensor(a2[:], nf[:], -CW1, a2[:], ALU.mult, ALU.add),
                              r=[tmp_b], w=[tmp_b])
                        cx.op("dve", lambda en: en.scalar_tensor_tensor(a2[:], nf[:], -CW2, a2[:], ALU.mult, ALU.add),
                              r=[tmp_b], w=[tmp_b])
                        cx.op("dve", lambda en: en.tensor_scalar(a2[:], a2[:], math.pi, -math.pi, ALU.min, ALU.max),
                              r=[tmp_b], w=[tmp_b])
                        cx.op("act", lambda en, tab=tab: en.activation(out=tab[:], in_=a2[:], func=AF.Sin),
                              r=[tmp_b], w=[tab_b])

                load_x(0)
                rms_to_fm(P, xts[0], xts_b[0], xn, xn_b, xnT1[0], xnT1_b[0], st, st_b, junk, junk_b)
                for ti, (s, t0) in enumerate(tiles):
                    xt, xt_b = xts[ti % 2], xts_b[ti % 2]
                    if ti + 1 < len(tiles):
                        load_x(ti + 1)
                    SC = sc[s]
                    ffn(P, W1, W1o, xnT1[ti % 2], xnT1_b[ti % 2], hT, hT_b, sg, sg_b, xt, xt_b,
                        mid=lambda t0=t0: rope_tables(t0), step=bg_step)
                    cx.dma("pool", SC["x1"][t0:t0 + T, :].rearrange("(j p) d -> p j d", p=128), xt[:], r=xt_b)
                    rms_to_fm(P, xt, xt_b, xn, xn_b, xnT, xnT_b, st, st_b, junk, junk_b)
                    rope_pending = []
                    for blk in range(NFC // 4):
                        bg_step()
                        wv, wv_b = wload(P, WF, 0, 8, blk * 512, 512)
                        for c in range(4):
                            ch = blk * 4 + c
                            pz, pz_b = mm_next(P)
                            for kc in range(8):
                                cx.op("pe", lambda en, kc=kc, c=c, pz=pz, wv=wv: en.matmul(
                                    pz[:], wv[:, kc, c * 128:(c + 1) * 128], xnT[:, kc, :],
                                    start=(kc == 0), stop=(kc == 7)),
                                    r=[wv_b, xnT_b], w=[pz_b], inc=(kc == 7))
                            for fn in rope_pending:
                                fn()
                            rope_pending = []
                            if ch < C_QD:
                                dst = SC["qmT"] if ch < C_KM else SC["kmT"]
                                sg_, sgb_ = stage_next()
                                e = ev_engine(P)
                                cx.op(e, copy_op(e, sg_[:], pz[:]), r=[pz_b], w=[sgb_])
                                cx.dma("pool", dst[ch % 4][:, t0:t0 + T], sg_[:], r=[sgb_])
                            elif ch < C_GA:
                                rel = ch - C_QD
                                hh = rel % 8
                                dst = SC["qdT"] if rel < 8 else SC["kdT"]
                                ri = stc["r"] % 2
                                stc["r"] += 1
                                cx.op("act", lambda en, pz=pz, ri=ri: en.activation(out=zq[ri][:], in_=pz[:], func=AF.Copy),
                                      r=[pz_b], w=[zq_b[ri]])

                                def rope_tail(ri=ri, dst=dst, hh=hh):
                                    pr, pr_b = mm_next(P)
                                    cx.op("pe", lambda en: en.matmul(pr[:], rot16[:], zq[ri][:], start=True, stop=True),
                                          r=[zq_b[ri], b_c2], w=[pr_b])
                                    cx.op("dve", lambda en: en.tensor_tensor(r1[ri][:], zq[ri][:], cosT[:], ALU.mult),
                                          r=[zq_b[ri], tab_b], w=[r1_b[ri]])
                                    cx.op("dve", lambda en: en.tensor_tensor(r2[ri][:], pr[:], sinT[:], ALU.mult),
                                          r=[pr_b, tab_b], w=[r2_b[ri]])
                                    sg_, sgb_ = stage_next()
                                    cx.op("dve", lambda en: en.tensor_tensor(sg_[:], r1[ri][:], r2[ri][:], ALU.add),
                                          r=[r1_b[ri], r2_b[ri]], w=[sgb_])
                                    cx.dma("pool", dst[hh][:, t0:t0 + T], sg_[:], r=[sgb_])
                                rope_pending.append(rope_tail)
                            else:
                                dst = SC["gaT"] if ch < C_GB else (SC["gbT"] if ch < C_OM else SC["omT"])
                                sg_, sgb_ = stage_next()
                                cx.op("act", lambda en, pz=pz, sg_=sg_: en.activation(out=sg_[:], in_=pz[:], func=AF.Sigmoid),
                                      r=[pz_b], w=[sgb_])
                                cx.dma("pool", dst[ch % 8][:, t0:t0 + T], sg_[:], r=[sgb_])
                    for fn in rope_pending:
                        fn()
                    rope_pending = []
                    wv, wv_b = wload(P, WF, 0, 8, NFC * 128, 16)
                    for gi, dst in ((0, SC["gig"]), (1, SC["gfg"])):
                        pz, pz_b = mm_next(P)
                        for kc in range(8):
                            cx.op("pe", lambda en, kc=kc, gi=gi, pz=pz, wv=wv: en.matmul(
                                pz[0:8, :], wv[:, kc, gi * 8:(gi + 1) * 8], xnT[:, kc, :], start=(kc == 0), stop=(kc == 7)),
                                r=[wv_b, xnT_b], w=[pz_b], inc=(kc == 7))
                        k = stc["g"] % 2
                        stc["g"] += 1
                        cx.op("act", lambda en, pz=pz, k=k, gi=gi: en.activation(
                            out=sgt[k][:], in_=pz[0:8, :], func=AF.Identity, bias=gbias[:, gi:gi + 1]),
                            r=[pz_b, b_const], w=[sgt_b[k]])
                        cx.dma("pool", dst[:, t0:t0 + T], sgt[k][:], r=[sgt_b[k]])
                    if ti + 1 < len(tiles):
                        nx = (ti + 1) % 2
                        rms_to_fm(P, xts[nx], xts_b[nx], xn, xn_b, xnT1[nx], xnT1_b[nx], st, st_b, junk, junk_b)
                    for blk in range(5):
                        bg_step()
                        wv, wv_b = wload(P, WT, 0, 8, blk * 512, 512)
                        for j in range(4):
                            pz, pz_b = mm_next(P)
                            for kc in range(8):
                                cx.op("pe", lambda en, kc=kc, j=j, pz=pz, wv=wv: en.matmul(
                                    pz[:], xnT[:, kc, j * 128:(j + 1) * 128], wv[:, kc, :], start=(kc == 0), stop=(kc == 7)),
                                    r=[wv_b, xnT_b], w=[pz_b], inc=(kc == 7))
                            sg_, sgb_ = stage_next()
                            e = ev_engine(P)
                            cx.op(e, copy_op(e, sg_[:], pz[:]), r=[pz_b], w=[sgb_])
                            rows = slice(t0 + j * 128, t0 + (j + 1) * 128)
                            if blk == 0:
                                dstap = SC["km"][rows, :]
                            elif blk < 3:
                                dstap = SC["vm"][rows, (blk - 1) * 512:blk * 512]
                            else:
                                dstap = SC["vd"][rows, (blk - 3) * 512:(blk - 2) * 512]
                            cx.dma("pool", dstap, sg_[:], r=[sgb_])
                bg_step(drain=True)
                cx.barrier()


        for s in range(NSEQ if stop_after >= 2 else 0):
            S = S_LIST[s]
            nch = S // 128
            SC = sc[s]
            NBK = 4 if nch % 4 == 0 else 1
            with contextlib.ExitStack() as seqst:
                aT = sb(seqst, "aT", [128, nch * 8], F32)
                rT = sb(seqst, "rT", [128, nch * 8], F32)
                decb = sb(seqst, "decb", [128, 8 * nch], F32)
                b_gt = Buf("gt")
                with contextlib.ExitStack() as ph:
                    gA = sb(ph, "gA", [8, S], F32)
                    gB = sb(ph, "gB", [8, S], F32)
                    gC = sb(ph, "gC", [8, S], F32)
                    sm = [sb(ph, "gsm%d" % i, [8, nch], F32) for i in range(8)]
                    tpa = pst(ph, "tpa", [128, 512], F32)
                    tpr = pst(ph, "tpr", [128, 512], F32)
                    b_tpa, b_tpr = Buf("tpa"), Buf("tpr")
                    bg = Buf("g")
                    AL, BE = abc[0:8, 0:1], abc[0:8, 1:2]

                    def G(e, fn):
                        cx.op(e, fn, r=[bg, b_const], w=[bg])

                    cx.dma("sp", gC[:], SC["gig"][:, :], w=[bg])
                    cx.dma("sp", gA[:], SC["gfg"][:, :], w=[bg], merge=True)
                    G("act", lambda en: en.activation(out=gA[:], in_=gA[:], func=AF.Exp, scale=-1.0))
                    G("act", lambda en: en.activation(out=gA[:], in_=gA[:], func=AF.Ln, bias=1.0))
                    G("dve", lambda en: en.tensor_tensor_scan(gB[:], gA[:], gA[:], 0.0, ALU.add, ALU.bypass))
                    G("dve", lambda en: en.tensor_scalar(gA[:], gA[:], gB[:, S - 1:S], BE, ALU.add, ALU.mult))
                    G("dve", lambda en: en.scalar_tensor_tensor(gB[:], gB[:], AL, gA[:], ALU.mult, ALU.add))
                    G("dve", lambda en: en.tensor_tensor(gA[:], gC[:], gB[:], ALU.add))
                    G("dve", lambda en: en.tensor_reduce(sm[0][:], gA[:].rearrange("p (c l) -> p c l", l=128), AX.X, ALU.max))
                    G("dve", lambda en: en.tensor_tensor_scan(sm[1][:], sm[0][:], sm[0][:], 0.0, ALU.max, ALU.bypass))
                    G("dve", lambda en: en.memset(sm[2][:], 0.0))
                    if nch > 1:
                        G("dve", lambda en: en.tensor_copy(sm[2][:, 1:nch], sm[1][:, 0:nch - 1]))
                    cur = sm[0]
                    pp = [sm[3], sm[4]]
                    sh = 1
                    k = 0
                    while sh < nch:
                        nxt = pp[k % 2]
                        G("dve", lambda en, nxt=nxt, cur=cur, sh=sh: en.tensor_tensor(
                            nxt[:, 0:nch - sh], cur[:, 0:nch - sh], cur[:, sh:nch], ALU.max))
                        G("dve", lambda en, nxt=nxt, cur=cur, sh=sh: en.tensor_copy(nxt[:, nch - sh:nch], cur[:, nch - sh:nch]))
                        cur = nxt
                        k += 1
                        sh *= 2
                    G("dve", lambda en: en.memset(sm[5][:], 0.0))
                    if nch > 1:
                        G("dve", lambda en: en.tensor_scalar(sm[5][:, 0:nch - 1], cur[:, 1:nch], 0.0, None, ALU.max))
                    G("dve", lambda en: en.tensor_tensor(sm[5][:], sm[5][:], sm[2][:], ALU.subtract))
                    G("dve", lambda en: en.scalar_tensor_tensor(sm[6][:], sm[5][:], BE, sm[2][:], ALU.mult, ALU.add))
                    if NBK > 1:
                        M3 = sm[6][:].rearrange("p (b k) -> p b k", k=NBK)
                        F3 = sm[7][:].rearrange("p (b k) -> p b k", k=NBK)
                        B3 = sm[0][:].rearrange("p (b k) -> p b k", k=NBK)
                        G("dve", lambda en: en.tensor_copy(F3, M3[:, :, 0:1].broadcast_to([8, nch // NBK, NBK])))
                        G("dve", lambda en: en.tensor_copy(B3, M3[:, :, NBK - 1:NBK].broadcast_to([8, nch // NBK, NBK])))
                        G("dve", lambda en: en.tensor_tensor(sm[0][:], sm[0][:], sm[7][:], ALU.subtract))
                        G("dve", lambda en: en.scalar_tensor_tensor(sm[6][:], sm[0][:], BE, sm[7][:], ALU.mult, ALU.add))
                    Mt = sm[6]
                    G("dve", lambda en: en.tensor_copy(sm[1][:], Mt[:]))
                    G("dve", lambda en: en.tensor_copy(sm[3][:], Mt[:]))
                    if nch > 1:
                        G("dve", lambda en: en.tensor_copy(sm[1][:, 0:nch - 1], Mt[:, 1:nch]))
                        G("dve", lambda en: en.tensor_copy(sm[3][:, 1:nch], Mt[:, 0:nch - 1]))
                    G("dve", lambda en: en.tensor_tensor(sm[3][:], sm[3][:], sm[1][:], ALU.subtract))
                    G("dve", lambda en: en.scalar_tensor_tensor(sm[3][:], sm[3][:], BE, sm[1][:], ALU.mult, ALU.add))
                    G("dve", lambda en: en.tensor_tensor(sm[4][:], Mt[:], sm[3][:], ALU.subtract))
                    G("act", lambda en: en.activation(out=sm[4][:], in_=sm[4][:], func=AF.Exp))
                    b_decd = Buf("decd")
                    cx.dma("pool", SC["dec"].rearrange("(r c) -> r c", r=8), sm[4][:], r=[bg], w=[b_decd])
                    cx.dma("sp", decb[:], SC["dec"].partition_broadcast(128), r=[b_decd], w=[b_gt])
                    Mbc = Mt[:].unsqueeze(2).broadcast_to([8, nch, 128])
                    gA3 = gA[:].rearrange("p (c l) -> p c l", l=128)
                    gB3 = gB[:].rearrange("p (c l) -> p c l", l=128)
                    G("dve", lambda en: en.tensor_tensor(gA3, gA3, Mbc, ALU.subtract))
                    G("act", lambda en: en.activation(out=gA[:], in_=gA[:], func=AF.Exp))
                    G("dve", lambda en: en.tensor_tensor(gB3, gB3, Mbc, ALU.subtract))
                    G("act", lambda en: en.activation(out=gB[:], in_=gB[:], func=AF.Exp))
                    for (src, tp_, tpb_, dstt) in ((gA, tpa, b_tpa, aT), (gB, tpr, b_tpr, rT)):
                        for c in range(nch):
                            cx.op("pe", lambda en, c=c, src=src, tp_=tp_: en.transpose(
                                tp_[:, c * 8:(c + 1) * 8], src[:, c * 128:(c + 1) * 128], ident[0:8, 0:8]),
                                r=[bg, b_const], w=[tpb_], inc=(c == nch - 1))
                        cx.op("dve", lambda en, tp_=tp_, dstt=dstt: en.tensor_copy(dstt[:], tp_[:, 0:nch * 8]),
                              r=[tpb_], w=[b_gt] if dstt is aT else [b_gt])
                    cx.barrier()

                if stop_after >= 3:
                    with contextlib.ExitStack() as ph:
                        qT = sb(ph, "m_qT", [128, S], BF16)
                        kT = sb(ph, "m_kT", [128, S], BF16)
                        kTM = sb(ph, "m_kTM", [128, nch, 128], BF16)
                        vp = sb(ph, "m_vp", [128, nch, 257], BF16)
                        hfirst = sb(ph, "m_hf", [128, nch, 256], F32)
                        S32 = [sb(ph, "m_S32_%d" % d, [128, 257], F32) for d in range(2)]
                        S16 = [[sb(ph, "m_S16_%d_%d" % (d, k), [128, 257], BF16) for k in range(2)] for d in range(2)]
                        tmpS = [sb(ph, "m_tmp_%d" % d, [128, 257], F32) for d in range(2)]
                        WTt = [[sb(ph, "m_WT_%d_%d" % (d, k), [128, 128], BF16) for k in range(2)] for d in range(2)]
                        ks = [[sb(ph, "m_ks_%d_%d" % (d, k), [128, 128], BF16) for k in range(2)] for d in range(2)]
                        dn = [sb(ph, "m_dn_%d" % d, [128, 8], F32) for d in range(2)]
                        hs = [sb(ph, "m_hs_%d" % i, [128, 256], F32) for i in range(4)]
                        hn = [sb(ph, "m_hn_%d" % i, [128, 256], BF16) for i in range(4)]
                        hstg = [sb(ph, "m_hstg_%d" % i, [128, 256], BF16) for i in range(4)]
                        hst = [sb(ph, "m_hst_%d" % i, [128, 8], F32) for i in range(4)]
                        mjunk = sb(ph, "m_junk", [128, 256], BF16)
                        p_sT = [pst(ph, "m_psT%d" % d, [128, 512], F32) for d in range(2)]
                        p_out = [pst(ph, "m_pout%d" % d, [128, 512], F32) for d in range(2)]
                        p_upd = [pst(ph, "m_pupd%d" % d, [128, 512], F32) for d in range(2)]
                        p_tr = [pst(ph, "m_ptr%d" % d, [128, 1024], BF16) for d in range(2)]
                        b_q, b_k, b_ktm, b_vp, b_hf = Buf("q"), Buf("k"), Buf("ktm"), Buf("vp"), Buf("hf")
                        b_S32 = [Buf("S32"), Buf("S32")]
                        b_S16 = [[Buf("S16"), Buf("S16")], [Buf("S16"), Buf("S16")]]
                        b_tmp = [Buf("tmp"), Buf("tmp")]
                        b_WT = [[Buf("WT"), Buf("WT")], [Buf("WT"), Buf("WT")]]
                        b_ks = [[Buf("ks"), Buf("ks")], [Buf("ks"), Buf("ks")]]
                        b_dn = [Buf("dn"), Buf("dn")]
                        b_hs = [Buf("hs") for _ in range(4)]
                        b_hn = [Buf("hn") for _ in range(4)]
                        b_hstg = [Buf("hstg") for _ in range(4)]
                        b_hst = [Buf("hst") for _ in range(4)]
                        b_mj = Buf("mj")
                        b_psT = [Buf("psT"), Buf("psT")]
                        b_pout = [Buf("pout"), Buf("pout")]
                        b_pupd = [Buf("pupd"), Buf("pupd")]
                        b_ptr = [Buf("ptr0"), Buf("ptr1")]
                        cx.op("dve", lambda en: en.memset(vp[:, :, 256:257], 1.0), w=[b_vp])
                        sec = {"i": 0}
                        for h in range(4):
                            cx.dma("sp", qT[:], SC["qmT"][h], w=[b_q])
                            cx.dma("sp", kT[:], SC["kmT"][h], w=[b_k])
                            cx.dma("sp", kTM[:], SC["km"][:, h * 128:(h + 1) * 128].rearrange("(c p) d -> p c d", p=128),
                                   w=[b_ktm])
                            cx.dma("sp", vp[:, :, 0:256],
                                   SC["vm"][:, h * 256:(h + 1) * 256].rearrange("(c p) d -> p c d", p=128), w=[b_vp], merge=True)
                            for d in range(2):
                                cx.op("dve", lambda en, d=d: en.memset(S32[d][:], 0.0), w=[b_S32[d]])
                                cx.op("dve", lambda en, d=d: en.memset(S16[d][0][:], 0.0), w=[b_S16[d][0]])

                            def chunk(d, i):
                                return i if d == 0 else nch - 1 - i

                            def gcol(d, c):
                                r_ = d * 4 + h
                                return (aT[:, c * 8 + r_:c * 8 + r_ + 1], rT[:, c * 8 + r_:c * 8 + r_ + 1],
                                        decb[:, r_ * nch + c:r_ * nch + c + 1])

                            def emit_ks(i):
                                for d in range(2):
                                    c = chunk(d, i)
                                    a_col = gcol(d, c)[0]
                                    cx.op("act", lambda en, d=d, c=c, a_col=a_col: en.activation(
                                        out=ks[d][i % 2][:], in_=kTM[:, c, :], func=AF.Copy, scale=a_col),
                                        r=[b_ktm, b_gt], w=[b_ks[d][i % 2]])

                            def emit_s1(i):
                                for d in range(2):
                                    c = chunk(d, i)
                                    cs = slice(c * 128, (c + 1) * 128)
                                    cx.op("pe", lambda en, d=d, cs=cs: en.matmul(p_sT[d][:, 0:128], kT[:, cs], qT[:, cs],
                                                                               start=True, stop=True),
                                          r=[b_k, b_q], w=[b_psT[d]])
                                for d in range(2):
                                    c = chunk(d, i)
                                    a_col = gcol(d, c)[0]
                                    cx.op("dve", lambda en, d=d, a_col=a_col: en.scalar_tensor_tensor(
                                        WTt[d][i % 2][:], p_sT[d][:, 0:128], a_col, masks[:, d * 128:(d + 1) * 128],
                                        ALU.mult, ALU.mult),
                                        r=[b_psT[d], b_gt, b_const], w=[b_WT[d][i % 2]])
                                if i < nch - 1:
                                    for d in range(2):
                                        c = chunk(d, i)
                                        cx.op("pe", lambda en, d=d, c=c: en.matmul(
                                            p_upd[d][:, 0:257], ks[d][i % 2][:], vp[:, c, :],
                                            start=(i % NBK == 0), stop=(i % NBK == NBK - 1 or i == nch - 2),
                                            skip_group_check=True),
                                            r=[b_ks[d][i % 2], b_vp], w=[b_pupd[d]])

                            if nch > 1:
                                emit_ks(0)
                            emit_s1(0)
                            pending = []
                            for i in range(nch):
                                if i + 1 < nch - 1:
                                    emit_ks(i + 1)
                                for d in range(2):
                                    c = chunk(d, i)
                                    cs = slice(c * 128, (c + 1) * 128)
                                    cx.op("pe", lambda en, d=d, cs=cs: en.matmul(p_out[d][:, 0:257], qT[:, cs], S16[d][i % 2][:],
                                                                               start=True, stop=False),
                                          r=[b_q, b_S16[d][i % 2]], w=[b_pout[d]], inc=False)
                                    cx.op("pe", lambda en, d=d, c=c: en.matmul(p_out[d][:, 0:257], WTt[d][i % 2][:], vp[:, c, :],
                                                                             start=False, stop=True),
                                          r=[b_WT[d][i % 2], b_vp], w=[b_pout[d]])
                                for fn in pending:
                                    fn()
                                pending = []
                                if i < nch - 1:
                                    for d in range(2):
                                        c = chunk(d, i)
                                        dec_col = gcol(d, c)[2]
                                        nxt = S16[d][(i + 1) % 2]
                                        nxt_b = b_S16[d][(i + 1) % 2]
                                        if (i + 1) % NBK == 0:
                                            cx.op("dve", lambda en, d=d: en.tensor_tensor(tmpS[d][:], S32[d][:], p_upd[d][:, 0:257], ALU.add),
                                                  r=[b_S32[d], b_pupd[d]], w=[b_tmp[d]])
                                            cx.op("dve", lambda en, d=d, dec_col=dec_col: en.tensor_scalar(
                                                S32[d][:], tmpS[d][:], dec_col, None, ALU.mult),
                                                r=[b_tmp[d], b_gt], w=[b_S32[d]])
                                            cx.op("dve", lambda en, d=d, nxt=nxt: en.tensor_copy(nxt[:], S32[d][:]),
                                                  r=[b_S32[d]], w=[nxt_b])
                                        else:
                                            cx.op("dve", lambda en, d=d, nxt=nxt: en.tensor_tensor(nxt[:], S32[d][:], p_upd[d][:, 0:257],
                                                                                                ALU.add),
                                                  r=[b_S32[d], b_pupd[d]], w=[nxt_b])
                                if i + 1 < nch:
                                    emit_s1(i + 1)
                                for d in range(2):
                                    cx.op("act", lambda en, d=d: en.activation(out=dn[d][:, 0:1], in_=p_out[d][:, 256:257], func=AF.Abs),
                                          r=[b_pout[d]], w=[b_dn[d]])
                                for d in range(2):
                                    r_col = gcol(d, chunk(d, i))[1]
                                    cx.op("dve", lambda en, d=d, r_col=r_col: en.tensor_tensor(dn[d][:, 1:2], dn[d][:, 0:1], r_col, ALU.max),
                                          r=[b_dn[d], b_gt], w=[b_dn[d]])
                                for d in range(2):
                                    cx.op("dve", lambda en, d=d: en.reciprocal(dn[d][:, 2:3], dn[d][:, 1:2]), r=[b_dn[d]], w=[b_dn[d]])
                                secs = []
                                for d in range(2):
                                    c = chunk(d, i)
                                    other = nch - 1 - i
                                    second_d = (other < i) or (d == 1 and other == i)
                                    if not second_d:
                                        cx.op("act", lambda en, d=d, c=c: en.activation(
                                            out=hfirst[:, c, :], in_=p_out[d][:, 0:256], func=AF.Copy, scale=dn[d][:, 2:3]),
                                            r=[b_pout[d], b_dn[d]], w=[b_hf])
                                    else:
                                        k4 = sec["i"] % 4
                                        sec["i"] += 1
                                        secs.append((d, c, k4))
                                for (d, c, k4) in secs:
                                    cx.op("dve", lambda en, d=d, c=c, k4=k4: en.scalar_tensor_tensor(
                                        hs[k4][:], p_out[d][:, 0:256], dn[d][:, 2:3], hfirst[:, c, :], ALU.mult, ALU.add),
                                        r=[b_pout[d], b_dn[d], b_hf], w=[b_hs[k4]])
                                for (d, c, k4) in secs:
                                    cx.op("act", lambda en, k4=k4: en.activation(out=mjunk[:], in_=hs[k4][:], func=AF.Square,
                                                                                 accum_out=hst[k4][:, 0:1]),
                                          r=[b_hs[k4]], w=[b_mj, b_hst[k4]])
                                for (d, c, k4) in secs:
                                    cx.op("act", lambda en, k4=k4: en.activation(out=hst[k4][:, 1:2], in_=hst[k4][:, 0:1], func=AF.Sqrt,
                                                                                 scale=1.0 / 256, bias=EPS),
                                          r=[b_hst[k4]], w=[b_hst[k4]])
                                for (d, c, k4) in secs:
                                    cx.op("dve", lambda en, k4=k4: en.reciprocal(hst[k4][:, 2:3], hst[k4][:, 1:2]),
                                          r=[b_hst[k4]], w=[b_hst[k4]])
                                for (d, c, k4) in secs:
                                    cx.op("act", lambda en, k4=k4: en.activation(out=hn[k4][:], in_=hs[k4][:], func=AF.Copy,
                                                                                 scale=hst[k4][:, 2:3]),
                                          r=[b_hs[k4], b_hst[k4]], w=[b_hn[k4]])
                                for (d, c, k4) in secs:
                                    cs = slice(c * 128, (c + 1) * 128)

                                    def tail(k4=k4, cs=cs, d=d):
                                        for half in range(2):
                                            cx.op("pe", lambda en, half=half: en.transpose(
                                                p_tr[d][:, half * 128:(half + 1) * 128], hn[k4][:, half * 128:(half + 1) * 128],
                                                ident16[:]),
                                                r=[b_hn[k4], b_c2], w=[b_ptr[d]], inc=(half == 1))
                                        cx.op("dve", lambda en: en.tensor_copy(hstg[k4][:], p_tr[d][:, 0:256]),
                                              r=[b_ptr[d]], w=[b_hstg[k4]])
                                        cx.dma("pool", SC["hmT"][h * 2:h * 2 + 2, :, cs].rearrange("t p n -> p t n"),
                                               hstg[k4][:].rearrange("p (t n) -> p t n", t=2), r=[b_hstg[k4]])
                                    pending.append(tail)
                            for fn in pending:
                                fn()
                            pending = []
                        cx.barrier()

                if stop_after >= 4:
                    with contextlib.ExitStack() as ph:
                        nkt = nch
                        nqb = S // 512
                        qT2 = [sb(ph, "d_qT%d" % i, [128, S], BF16) for i in range(2)]
                        kT2 = [sb(ph, "d_kT%d" % i, [128, S], BF16) for i in range(2)]
                        vp2 = [sb(ph, "d_vp%d" % i, [128, nkt, 129], BF16) for i in range(2)]
                        Et = [sb(ph, "d_E%d" % i, [128, 2, 512], BF16) for i in range(3)]
                        accS = [sb(ph, "d_accS%d" % i, [128, 3, 387], F32) for i in range(2)]
                        rc = [sb(ph, "d_rc%d" % i, [128, 20], F32) for i in range(2)]
                        t1 = [sb(ph, "d_t1_%d" % i, [128, 128], F32) for i in range(2)]
                        ot = [sb(ph, "d_o_%d" % i, [128, 4, 128], F32) for i in range(2)]
                        on = [sb(ph, "d_on_%d" % i, [128, 4, 128], BF16) for i in range(2)]
                        ost = [sb(ph, "d_ost_%d" % i, [128, 512], BF16) for i in range(2)]
                        djunk = sb(ph, "d_junk", [128, 128], F32)
                        scp = [pst(ph, "d_scp%d" % i, [128, 2, 512], F32) for i in range(2)]
                        accp = pst(ph, "d_accp", [128, 3, 512], F32)
                        trp2 = pst(ph, "d_trp", [128, 1024], BF16)
                        b_qk = [Buf("qk0"), Buf("qk1")]
                        b_v = [Buf("v0"), Buf("v1")]
                        b_sc = [Buf("sc0"), Buf("sc1")]
                        b_E = [Buf("E0"), Buf("E1"), Buf("E2")]
                        b_acc = Buf("acc")
                        b_accS = [Buf("accS0"), Buf("accS1")]
                        b_rc = [Buf("rc0"), Buf("rc1")]
                        b_t1 = [Buf("t1"), Buf("t1")]
                        b_o = [Buf("o"), Buf("o")]
                        b_on = [Buf("on"), Buf("on")]
                        b_ost = [Buf("ost"), Buf("ost")]
                        b_dj = Buf("dj")
                        b_trp = Buf("trp")
                        for i in range(2):
                            cx.op("dve", lambda en, i=i: en.memset(vp2[i][:, :, 128:129], 1.0), w=[b_v[i]])

                        def accv(a):
                            return accp[:, a // 3, (a % 3) * 129:(a % 3) * 129 + 129]

                        def dload(h):
                            i = h % 2
                            cx.dma("sp", qT2[i][:], SC["qdT"][h], w=[b_qk[i]])
                            cx.dma("sp", kT2[i][:], SC["kdT"][h], w=[b_qk[i]], merge=True)
                            cx.dma("sp", vp2[i][:, :, 0:128],
                                   SC["vd"][:, h * 128:(h + 1) * 128].rearrange("(c p) d -> p c d", p=128),
                                   w=[b_v[i]], merge=True)

                        fin = {"i": 0, "gen": None}

                        def finalize(fi, h, qsl):
                            for b in range(3):
                                wdt = 387 if b < 2 else 258
                                cx.op("dve", lambda en, b=b, wdt=wdt: en.tensor_copy(accS[fi][:, b, 0:wdt], accp[:, b, 0:wdt]),
                                      r=[b_acc], w=[b_accS[fi]])
                            yield
                            rcq = rc[fi]

                            def A(a):
                                return accS[fi][:, a // 3, (a % 3) * 129:(a % 3) * 129 + 129]
                            for qs in range(4):
                                A1, A2 = A(qs * 2), A(qs * 2 + 1)
                                cx.op("dve", lambda en: en.reciprocal(rcq[:, 0:1], A1[:, 128:129]), r=[b_accS[fi], b_rc[fi]], w=[b_rc[fi]])
                                cx.op("dve", lambda en: en.reciprocal(rcq[:, 1:2], A2[:, 128:129]), r=[b_accS[fi], b_rc[fi]], w=[b_rc[fi]])
                                cx.op("dve", lambda en: en.tensor_tensor(rcq[:, 2:3], rcq[:, 1:2], NEG_LAM, ALU.mult),
                                      r=[b_rc[fi], b_l], w=[b_rc[fi]])
                                cx.op("dve", lambda en: en.tensor_scalar(t1[fi][:], A1[:, 0:128], rcq[:, 0:1], None, ALU.mult),
                                      r=[b_accS[fi], b_rc[fi]], w=[b_t1[fi]])
                                cx.op("dve", lambda en: en.scalar_tensor_tensor(ot[fi][:, qs, :], A2[:, 0:128], rcq[:, 2:3], t1[fi][:],
                                                                                ALU.mult, ALU.add),
                                      r=[b_accS[fi], b_rc[fi], b_t1[fi]], w=[b_o[fi]])
                                cx.op("dve", lambda en: en.scalar_tensor_tensor(djunk[:], ot[fi][:, qs, :], 1.0, ot[fi][:, qs, :],
                                                                                ALU.mult, ALU.mult, accum_out=rcq[:, 8 + qs:9 + qs]),
                                      r=[b_o[fi], b_rc[fi]], w=[b_dj, b_rc[fi]])
                                if qs == 1:
                                    yield
                            for _ in range(5):
                                yield
                            cx.op("act", lambda en: en.activation(out=rcq[:, 12:16], in_=rcq[:, 8:12], func=AF.Ln,
                                                                  scale=1.0 / 128, bias=EPS),
                                  r=[b_rc[fi]], w=[b_rc[fi]])
                            cx.op("act", lambda en: en.activation(out=rcq[:, 16:20], in_=rcq[:, 12:16], func=AF.Exp, scale=-0.5),
                                  r=[b_rc[fi]], w=[b_rc[fi]])
                            yield
                            yield
                            for qs in range(4):
                                cx.op("dve", lambda en, qs=qs: en.tensor_scalar(on[fi][:, qs, :], ot[fi][:, qs, :], rcq[:, 16 + qs:17 + qs],
                                                                               None, ALU.mult),
                                      r=[b_o[fi], b_rc[fi]], w=[b_on[fi]])
                            yield
                            for qs in range(4):
                                cx.op("pe", lambda en, qs=qs: en.transpose(trp2[:, qs * 128:(qs + 1) * 128], on[fi][:, qs, :], ident16[:]),
                                      r=[b_on[fi], b_c2], w=[b_trp], inc=(qs == 3))
                            yield
                            cx.op("dve", lambda en: en.tensor_copy(ost[fi][:], trp2[:, 0:512]), r=[b_trp], w=[b_ost[fi]])
                            cx.dma("pool", SC["odT"][h][:, qsl], ost[fi][:], r=[b_ost[fi]])

                        def fin_step(drain=False):
                            g = fin["gen"]
                            while g is not None:
                                try:
                                    next(g)
                                except StopIteration:
                                    fin["gen"] = None
                                    return
                                if not drain:
                                    return

                        steps = [(h, qb, kt) for h in range(8) for qb in range(nqb) for kt in range(nkt)]
                        NS = len(steps)

                        def qk(n):
                            (h, qb, kt) = steps[n]
                            hi = h % 2
                            sl_ = n % 2
                            es_ = n % 3
                            qsl = slice(qb * 512, (qb + 1) * 512)
                            for c in range(2):
                                cx.op("pe", lambda en, c=c: en.matmul(
                                    scp[sl_][:, c, :],
                                    kT2[hi][c * 64:(c + 1) * 64, kt * 128:(kt + 1) * 128],
                                    qT2[hi][c * 64:(c + 1) * 64, qsl], start=True, stop=True),
                                    r=[b_qk[hi]], w=[b_sc[sl_]], inc=(c == 1))
                            cx.op("act", lambda en: en.activation(out=Et[es_][:], in_=scp[sl_][:], func=AF.Exp, scale=0.125),
                                  r=[b_sc[sl_]], w=[b_E[es_]])

                        def pv(n):
                            (h, qb, kt) = steps[n]
                            hi = h % 2
                            es_ = n % 3
                            for qs in range(4):
                                for c in range(2):
                                    a = qs * 2 + c
                                    cx.op("pe", lambda en, a=a, c=c, qs=qs: en.matmul(
                                        accv(a), Et[es_][:, c, qs * 128:(qs + 1) * 128], vp2[hi][:, kt, :],
                                        start=(kt == 0 and a % 3 == 0), stop=(kt == nkt - 1), skip_group_check=True),
                                        r=[b_E[es_], b_v[hi]], w=[b_acc], inc=(a == 7))

                        dload(0)
                        qk(0)
                        if NS > 1:
                            qk(1)
                        for n in range(NS):
                            (h, qb, kt) = steps[n]
                            if qb == 0 and kt == 0 and h + 1 < 8:
                                dload(h + 1)
                            if n + 2 < NS:
                                qk(n + 2)
                            pv(n)
                            if kt == nkt - 1:
                                fin_step(drain=True)
                                fi = fin["i"] % 2
                                fin["i"] += 1
                                fin["gen"] = finalize(fi, h, slice(qb * 512, (qb + 1) * 512))
                                fin_step()
                            elif kt >= 1:
                                fin_step()
                        fin_step(drain=True)
                        cx.barrier()

        if stop_after >= 5:
            with contextlib.ExitStack() as ph:
                P = rowlocal_pools(ph)
                xts = [sb(ph, "xt%d" % i, [128, 4, D], F32) for i in range(2)]
                xts_b = [[Buf("xt%d_%d" % (i, j_)) for j_ in range(4)] for i in range(2)]
                xn = sb(ph, "xn", [128, 4, D], BF16)
                xn_b = [Buf("xn%d" % j_) for j_ in range(4)]
                ox = sb(ph, "ox", [128, 4, D], BF16)
                ox_b = [Buf("ox%d" % j_) for j_ in range(4)]
                hT = sb(ph, "hT", [128, 22, T], BF16)
                hT_b = [Buf("hT%d" % k_) for k_ in range(3)]
                sg = [sb(ph, "sg%d" % i, [128, T], BF16) for i in range(2)]
                sg_b = [Buf("sg0"), Buf("sg1")]
                st = sb(ph, "st", [128, 16], F32)
                st_b = Buf("st")
                junk = sb(ph, "junk", [128, D], BF16)
                junk_b = Buf("junk")
                fmr = [sb(ph, "fmr%d" % i, [128, 8, T], BF16) for i in range(3)]
                fmr_b = [Buf("fmr%d" % i) for i in range(3)]
                gen = [sb(ph, "gen%d" % i, [128, 8, T], BF16) for i in range(2)]
                gen_b = [Buf("gen%d" % i) for i in range(2)]
                tf = [sb(ph, "tf%d" % i, [128, T], F32) for i in range(4)]
                tf_b = [Buf("tf%d" % i) for i in range(4)]
                ETt = [sb(ph, "ET%d" % i, [128, 2, T], BF16) for i in range(2)]
                ET_b = [Buf("ET0"), Buf("ET1")]
                rcx = sb(ph, "rcx", [128, 16], F32)
                rcx_b = Buf("rcx")
                KxT = [sb(ph, "KxT%d" % s_, [128, 8, NMEM], BF16) for s_ in range(NSEQ)]
                Vx = [sb(ph, "Vx%d" % s_, [128, 2, 4, 257], BF16) for s_ in range(NSEQ)]
                kv_b = [Buf("kv%d" % s_) for s_ in range(NSEQ)]
                cnt3 = {"fm": 0, "tf": 0}

                def tm_to_fm(src, src_b, dst, dst_b, ntok=4):
                    for j in range(ntok):
                        tp, tp_b = tr_next(P)
                        for kc in range(8):
                            cx.op("pe", lambda en, j=j, kc=kc, tp=tp: en.transpose(
                                tp[:, kc * 128:(kc + 1) * 128], src[:, j, kc * 128:(kc + 1) * 128], ident16[:]),
                                r=[src_b[j], b_c2], w=[tp_b], inc=(kc == 7))
                        e = ev_engine(P)
                        cx.op(e, copy_op(e, dst[:, 0:8, j * 128:(j + 1) * 128],
                                         tp[:, 0:1024].rearrange("p (k n) -> p k n", k=8)), r=[tp_b], w=[dst_b])

                def tm_linear_add(Wd, src, src_b, xt, xt_b):
                    for nb in range(2):
                        wv, wv_b = wload(P, Wd, 0, 8, nb * 512, 512)
                        for j in range(4):
                            pz, pz_b = mm_next(P)
                            for kc in range(8):
                                cx.op("pe", lambda en, kc=kc, j=j, pz=pz, wv=wv: en.matmul(
                                    pz[:], src[:, kc, j * 128:(j + 1) * 128], wv[:, kc, :], start=(kc == 0), stop=(kc == 7)),
                                    r=[wv_b, src_b], w=[pz_b], inc=(kc == 7))
                            cx.op("dve", lambda en, j=j, nb=nb, pz=pz: en.tensor_tensor(
                                xt[:, j, nb * 512:(nb + 1) * 512], pz[:], xt[:, j, nb * 512:(nb + 1) * 512], ALU.add),
                                r=[pz_b, xt_b[j]], w=[xt_b[j]])

                def tm_linear_add_norm(Wd, src, src_b, xt, xt_b, dstT, dstT_b):
                    wvs = [wload(P, Wd, 0, 8, nb * 512, 512) for nb in range(2)]
                    pend = []
                    for j in range(4):
                        for nb in range(2):
                            wv, wv_b = wvs[nb]
                            pz, pz_b = mm_next(P)
                            for kc in range(8):
                                cx.op("pe", lambda en, kc=kc, j=j, pz=pz, wv=wv: en.matmul(
                                    pz[:], src[:, kc, j * 128:(j + 1) * 128], wv[:, kc, :], start=(kc == 0), stop=(kc == 7)),
                                    r=[wv_b, src_b], w=[pz_b], inc=(kc == 7))
                            cx.op("dve", lambda en, j=j, nb=nb, pz=pz: en.tensor_tensor(
                                xt[:, j, nb * 512:(nb + 1) * 512], pz[:], xt[:, j, nb * 512:(nb + 1) * 512], ALU.add),
                                r=[pz_b, xt_b[j]], w=[xt_b[j]])
                        cx.op("act", lambda en, j=j: en.activation(out=junk[:], in_=xt[:, j, :], func=AF.Square,
                                                                   accum_out=st[:, j:j + 1]),
                              r=[xt_b[j]], w=[junk_b, st_b])
                        cx.op("act", lambda en, j=j: en.activation(out=st[:, 4 + j:5 + j], in_=st[:, j:j + 1], func=AF.Ln,
                                                                   scale=1.0 / D, bias=EPS), r=[st_b], w=[st_b])
                        cx.op("act", lambda en, j=j: en.activation(out=st[:, 8 + j:9 + j], in_=st[:, 4 + j:5 + j], func=AF.Exp,
                                                                   scale=-0.5), r=[st_b], w=[st_b])
                        xb = xn_b[j]
                        if j % 2 == 0:
                            cx.op("act", lambda en, j=j: en.activation(out=xn[:, j, :], in_=xt[:, j, :], func=AF.Copy,
                                                                       scale=st[:, 8 + j:9 + j]),
                                  r=[xt_b[j], st_b], w=[xb])
                        else:
                            cx.op("dve", lambda en, j=j: en.tensor_scalar(xn[:, j, :], xt[:, j, :], st[:, 8 + j:9 + j], None,
                                                                          ALU.mult),
                                  r=[xt_b[j], st_b], w=[xb])
                        for fn in pend:
                            fn()
                        pend = []

                        def tail(j=j, xb=xb):
                            tp, tp_b = tr_next(P)
                            for kc in range(8):
                                cx.op("pe", lambda en, kc=kc: en.transpose(
                                    tp[:, kc * 128:(kc + 1) * 128], xn[:, j, kc * 128:(kc + 1) * 128], ident16[:]),
                                    r=[xb, b_c2], w=[tp_b], inc=(kc == 7))
                            e = ev_engine(P)
                            cx.op(e, copy_op(e, dstT[:, 0:8, j * 128:(j + 1) * 128],
                                             tp[:, 0:1024].rearrange("p (k n) -> p k n", k=8)), r=[tp_b], w=[dstT_b])
                        pend.append(tail)
                    for fn in pend:
                        fn()

                for s_ in range(NSEQ):
                    memt, memt_b = xts[s_ % 2], xts_b[s_ % 2]
                    cx.dma("sp", memt[:, 0:2, :], mem_in[s_].rearrange("(j p) d -> p j d", p=128), w=memt_b)
                    memT, memT_b = gen[0], gen_b[0]
                    rms_to_fm(P, memt, memt_b, xn, xn_b, memT, memT_b, st, st_b, junk, junk_b, ntok=2)
                    cx.op("dve", lambda en, s_=s_: en.memset(Vx[s_][:, :, :, 256:257], 1.0), w=[kv_b[s_]])
                    for blk in range(2):
                        wv, wv_b = wload(P, WXKV, 0, 8, blk * 512, 512)
                        for c in range(4):
                            ch = blk * 4 + c
                            pz, pz_b = mm_next(P)
                            for kc in range(8):
                                cx.op("pe", lambda en, kc=kc, c=c, pz=pz, wv=wv: en.matmul(
                                    pz[:, 0:NMEM], wv[:, kc, c * 128:(c + 1) * 128], memT[:, kc, 0:NMEM],
                                    start=(kc == 0), stop=(kc == 7)),
                                    r=[wv_b, memT_b], w=[pz_b], inc=(kc == 7))
                            e = ev_engine(P)
                            cx.op(e, copy_op(e, KxT[s_][:, ch, :], pz[:, 0:NMEM]), r=[pz_b], w=[kv_b[s_]])
                    for nb in range(2):
                        wv, wv_b = wload(P, WXKV, 0, 8, D + nb * 512, 512)
                        for mt in range(2):
                            pz, pz_b = mm_next(P)
                            for kc in range(8):
                                cx.op("pe", lambda en, kc=kc, mt=mt, pz=pz, wv=wv: en.matmul(
                                    pz[:], memT[:, kc, mt * 128:(mt + 1) * 128], wv[:, kc, :], start=(kc == 0), stop=(kc == 7)),
                                    r=[wv_b, memT_b], w=[pz_b], inc=(kc == 7))
                            e = ev_engine(P)
                            cx.op(e, copy_op(e, Vx[s_][:, mt, nb * 2:nb * 2 + 2, 0:256],
                                             pz[:].rearrange("p (h d) -> p h d", h=2)), r=[pz_b], w=[kv_b[s_]])

                def load_x1(ti):
                    s_, t0_ = tiles[ti]
                    cx.dma("sp", xts[ti % 2][:], sc[s_]["x1"][t0_:t0_ + T, :].rearrange("(j p) d -> p j d", p=128),
                           w=xts_b[ti % 2])

                def fm_load(src, t0_, q="sp"):
                    i = cnt3["fm"] % 3
                    cnt3["fm"] += 1
                    cx.dma(q, fmr[i][:], src[:, :, t0_:t0_ + T].rearrange("c p n -> p c n"), w=[fmr_b[i]])
                    return fmr[i], fmr_b[i]

                def tf_next():
                    i = cnt3["tf"] % 4
                    cnt3["tf"] += 1
                    return tf[i], tf_b[i]

                def prefetch_fm(ti_, q="sp"):
                    s_, t0_ = tiles[ti_]
                    SC_ = sc[s_]
                    hm, hm_b = fm_load(SC_["hmT"], t0_, q)
                    om, om_b = fm_load(SC_["omT"], t0_, q)
                    hg, hg_b = gen[0], gen_b[0]
                    cx.op("dve", lambda en: en.tensor_tensor(hg[:], hm[:], om[:], ALU.mult), r=[hm_b, om_b], w=[hg_b])
                    od, od_b = fm_load(SC_["odT"], t0_, q)
                    ga, ga_b = fm_load(SC_["gaT"], t0_, q)
                    gb, gb_b = fm_load(SC_["gbT"], t0_, q)
                    return (hg, hg_b, od, od_b, ga, ga_b, gb, gb_b)

                load_x1(0)
                pre = prefetch_fm(0)
                for ti, (s, t0) in enumerate(tiles):
                    xt, xt_b = xts[ti % 2], xts_b[ti % 2]
                    SC = sc[s]
                    (hg, hg_b, od, od_b, ga, ga_b, gb, gb_b) = pre
                    mg, mg_b = gen[1], gen_b[1]
                    for blk in range(2):
                        wa, wa_b = wload(P, WA, 0, 8, blk * 512, 512)
                        wb, wb_b = wload(P, WB, 0, 8, blk * 512, 512)
                        for c in range(4):
                            dch = blk * 4 + c
                            pa, pa_b = mm_next(P)
                            for kc in range(8):
                                cx.op("pe", lambda en, kc=kc, c=c, pa=pa, wa=wa: en.matmul(
                                    pa[:], wa[:, kc, c * 128:(c + 1) * 128], hg[:, kc, :], start=(kc == 0), stop=(kc == 7)),
                                    r=[wa_b, hg_b], w=[pa_b], inc=(kc == 7))
                            ta, ta_b = tf_next()
                            cx.op("dve", lambda en, pa=pa, ta=ta, dch=dch: en.tensor_tensor(ta[:], pa[:], ga[:, dch, :], ALU.mult),
                                  r=[pa_b, ga_b], w=[ta_b])
                            pb, pb_b = mm_next(P)
                            for kc in range(8):
                                cx.op("pe", lambda en, kc=kc, c=c, pb=pb, wb=wb: en.matmul(
                                    pb[:], wb[:, kc, c * 128:(c + 1) * 128], od[:, kc, :], start=(kc == 0), stop=(kc == 7)),
                                    r=[wb_b, od_b], w=[pb_b], inc=(kc == 7))
                            tb, tb_b = tf_next()
                            cx.op("dve", lambda en, pb=pb, tb=tb, dch=dch: en.tensor_tensor(tb[:], pb[:], gb[:, dch, :], ALU.mult),
                                  r=[pb_b, gb_b], w=[tb_b])
                            cx.op("dve", lambda en, ta=ta, tb=tb, dch=dch: en.tensor_tensor(mg[:, dch, :], ta[:], tb[:], ALU.add),
                                  r=[ta_b, tb_b], w=[mg_b])
                    xnT, xnT_b = gen[0], gen_b[0]
                    tm_linear_add_norm(WMO, mg, mg_b, xt, xt_b, xnT, xnT_b)
                    qxT, qxT_b = gen[1], gen_b[1]
                    for blk in range(2):
                        wv, wv_b = wload(P, WXQ, 0, 8, blk * 512, 512)
                        for c in range(4):
                            ch = blk * 4 + c
                            pz, pz_b = mm_next(P)
                            for kc in range(8):
                                cx.op("pe", lambda en, kc=kc, c=c, pz=pz, wv=wv: en.matmul(
                                    pz[:], wv[:, kc, c * 128:(c + 1) * 128], xnT[:, kc, :], start=(kc == 0), stop=(kc == 7)),
                                    r=[wv_b, xnT_b], w=[pz_b], inc=(kc == 7))
                            e = ev_engine(P)
                            cx.op(e, copy_op(e, qxT[:, ch, :], pz[:]), r=[pz_b], w=[qxT_b])
                    def xa_scores(hx):
                        eb = hx % 2
                        for mt in range(2):
                            pz, pz_b = mm_next(P)
                            for dc in range(2):
                                cx.op("pe", lambda en, dc=dc, mt=mt, pz=pz: en.matmul(
                                    pz[:], KxT[s][:, hx * 2 + dc, mt * 128:(mt + 1) * 128], qxT[:, hx * 2 + dc, :],
                                    start=(dc == 0), stop=(dc == 1)),
                                    r=[kv_b[s], qxT_b], w=[pz_b], inc=(dc == 1))
                            cx.op("act", lambda en, mt=mt, pz=pz: en.activation(out=ETt[eb][:, mt, :], in_=pz[:], func=AF.Exp,
                                                                               scale=1.0 / 16.0),
                                  r=[pz_b], w=[ET_b[eb]])

                    def xa_out(hx):
                        eb = hx % 2
                        for j in range(4):
                            po, po_b = mm_next(P)
                            for mt in range(2):
                                cx.op("pe", lambda en, mt=mt, j=j, po=po: en.matmul(
                                    po[:, 0:257], ETt[eb][:, mt, j * 128:(j + 1) * 128], Vx[s][:, mt, hx, :],
                                    start=(mt == 0), stop=(mt == 1)),
                                    r=[ET_b[eb], kv_b[s]], w=[po_b], inc=(mt == 1))
                            cx.op("dve", lambda en, po=po, j=j: en.reciprocal(rcx[:, hx * 4 + j:hx * 4 + j + 1], po[:, 256:257]),
                                  r=[po_b], w=[rcx_b])
                            cx.op("act", lambda en, po=po, j=j: en.activation(
                                out=ox[:, j, hx * 256:(hx + 1) * 256], in_=po[:, 0:256], func=AF.Copy,
                                scale=rcx[:, hx * 4 + j:hx * 4 + j + 1]),
                                r=[po_b, rcx_b], w=[ox_b[j]])

                    xa_scores(0)
                    for hx in range(4):
                        if hx + 1 < 4:
                            xa_scores(hx + 1)
                        xa_out(hx)
                    oxT, oxT_b = gen[0], gen_b[0]
                    tm_to_fm(ox, ox_b, oxT, oxT_b)
                    xnT2, xnT2_b = gen[1], gen_b[1]
                    tm_linear_add_norm(WXO, oxT, oxT_b, xt, xt_b, xnT2, xnT2_b)
                    if ti + 1 < len(tiles):
                        load_x1(ti + 1)
                        pre = prefetch_fm(ti + 1, q="act")
                    ffn(P, W2, W2o, xnT2, xnT2_b, hT, hT_b, sg, sg_b, xt, xt_b)
                    for j in range(4):
                        cx.op("act", lambda en, j=j: en.activation(out=junk[:], in_=xt[:, j, :], func=AF.Square,
                                                                   accum_out=st[:, j:j + 1]),
                              r=[xt_b[j]], w=[junk_b, st_b])
                    cx.op("act", lambda en: en.activation(out=st[:, 4:8], in_=st[:, 0:4], func=AF.Ln, scale=1.0 / D, bias=EPS),
                          r=[st_b], w=[st_b])
                    cx.op("act", lambda en: en.activation(out=st[:, 8:12], in_=st[:, 4:8], func=AF.Exp, scale=-0.5),
                          r=[st_b], w=[st_b])
                    for j in range(4):
                        cx.op("dve", lambda en, j=j: en.scalar_tensor_tensor(
                            xt[:, j, :], xt[:, j, :], st[:, 8 + j:9 + j], fgain[:], ALU.mult, ALU.mult),
                            r=[xt_b[j], st_b, b_const], w=[xt_b[j]])
                    cx.dma("pool", y_out[s][t0:t0 + T, :].rearrange("(j p) d -> p j d", p=128), xt[:], r=xt_b)
                cx.barrier()
        cx.barrier()
    return nc


def _consts():
    c = {}
    c["c_ident"] = np.eye(128, dtype=np.float32)
    j = np.arange(128)[:, None]
    i = np.arange(128)[None, :]
    c["c_masks"] = np.concatenate([(j <= i), (j >= i)], axis=1).astype(np.float32)
    inv_freq = (np.float32(500000.0) ** (-(np.arange(0, 16, 2, dtype=np.float32) / np.float32(16)))).astype(np.float32)
    rope = np.zeros((128, 2), np.float32)
    for p in range(128):
        f = p % 64
        if f < 16:
            rope[p, 0] = inv_freq[f % 8]
    c["c_rope"] = rope
    c["c_pos"] = np.tile(np.arange(T, dtype=np.float32)[None, :], (128, 1))
    ab = np.zeros((8, 2), np.float32)
    ab[0:4, 0] = 1.0
    ab[4:8, 0] = -1.0
    ab[4:8, 1] = 1.0
    c["c_ab"] = ab
    rot = np.zeros((128, 128), np.float32)
    for b in (0, 64):
        for f in range(8):
            rot[b + f + 8, b + f] = -1.0
            rot[b + f, b + f + 8] = 1.0
    c["c_rot"] = rot
    return c


def make_in_maps(inputs, S_LIST, n_cores, seq_of_core):
    f = lambda a: np.ascontiguousarray(np.asarray(a, dtype=np.float32))
    gains = [inputs[k] for k in ("ffn1_norm", "mix_norm", "mlstm_norm", "diff_norm", "xattn_norm", "mem_norm", "ffn2_norm")]
    gcols = np.concatenate([f(g).reshape(8, 128).T for g in gains], axis=1)
    shared = {
        "ffn1_w_in": f(inputs["ffn1_w_in"])[0], "ffn1_w_out": f(inputs["ffn1_w_out"])[0],
        "w_mix_in": f(inputs["w_mix_in"])[0], "w_branch_a": f(inputs["w_branch_a"])[0],
        "w_branch_b": f(inputs["w_branch_b"])[0], "w_mix_out": f(inputs["w_mix_out"])[0],
        "w_xq": f(inputs["w_xq"])[0], "w_xkv": f(inputs["w_xkv"])[0], "w_xo": f(inputs["w_xo"])[0],
        "ffn2_w_in": f(inputs["ffn2_w_in"])[0], "ffn2_w_out": f(inputs["ffn2_w_out"])[0],
        "gcols": np.ascontiguousarray(gcols),
        "final_norm": f(inputs["final_norm"]).reshape(D),
        "lam_vecs": np.concatenate([f(inputs[k]).reshape(64) for k in ("lambda_q1", "lambda_k1", "lambda_q2", "lambda_k2")]),
        "gate_bias": np.ascontiguousarray(np.stack([f(inputs["b_igate"]).reshape(8), f(inputs["b_fgate"]).reshape(8)], axis=1)),
    }
    shared.update(_consts())
    maps = []
    for c in range(n_cores):
        m = dict(shared)
        xs, ms = seq_of_core(c)
        for s in range(len(S_LIST)):
            m["x%d" % s] = np.ascontiguousarray(xs[s])
            m["mem%d" % s] = np.ascontiguousarray(ms[s])
        maps.append(m)
    return maps


def kernel(**inputs):
    S_LIST = (2048, 8192)
    nc = build(S_LIST)
    xp, xs_ = np.asarray(inputs["x_prompt"]), np.asarray(inputs["x_sample"])
    mp, ms_ = np.asarray(inputs["mem_prompt"]), np.asarray(inputs["mem_sample"])
    maps = make_in_maps(inputs, S_LIST, 8, lambda c: ((xp[c], xs_[c]), (mp[c], ms_[c])))
    res = run_bass_kernel_spmd(nc, maps, core_ids=list(range(8)))
    y_p = np.stack([res.results[c]["y0"] for c in range(8)], axis=0).astype(np.float32)
    y_s = np.stack([res.results[c]["y1"] for c in range(8)], axis=0).astype(np.float32)
    return (y_p, y_s)
```
